# Optimizing a Trainium2 kernel written in Bass

```python
import jax, jax.numpy as jnp
from jax import lax
import numpy as np

D_MODEL = 1024
BATCH = 32
SEQ = 2048
DEPTH = 4

CTX_LEN = 256
GRID_W = 64
N_EVEN = (DEPTH + 1) // 2
N_ODD = DEPTH // 2
RET_HEADS = 4
RET_DK = 128
RET_DV = 128
RET_CHUNK = 128
ATT_HEADS = 4
ATT_KV_HEADS = 2
ATT_HD = 128
Q_BLOCK = 128
ROPE_BASE = 10000.0
CM_WIDTH = 1024
CM_GROUPS = 8
CM_GROUP_DIM = CM_WIDTH // CM_GROUPS
CM_CHUNK = 128
FF_HIDDEN = 4 * D_MODEL
EPS = 1e-6
AB_SIZES = (RET_HEADS * RET_DK, RET_HEADS * RET_DK, RET_HEADS * RET_DV, RET_HEADS * RET_DV,
            ATT_HEADS * ATT_HD, ATT_KV_HEADS * ATT_HD, ATT_KV_HEADS * ATT_HD)
AB_IN_W = sum(AB_SIZES)
AB_OUT_W = RET_HEADS * RET_DV + ATT_HEADS * ATT_HD

kernel_name = 'hybrid_retention_gqa_chunkmlp_dit'


def rms_norm(x, g):
    xf = x.astype(jnp.float32)
    y = xf * lax.rsqrt(jnp.mean(xf * xf, axis=-1, keepdims=True) + EPS)
    return (y * g.astype(jnp.float32)).astype(x.dtype)


def modulate(h, shift, scale):
    return h * (1.0 + scale) + shift


def grid_rope(L, hd):
    rows = L // GRID_W
    row = jnp.repeat(jnp.arange(rows, dtype=jnp.float32), GRID_W)
    col = jnp.tile(jnp.arange(GRID_W, dtype=jnp.float32), rows)
    n_freq = hd // 4
    inv = ROPE_BASE ** (-jnp.arange(n_freq, dtype=jnp.float32) / n_freq)
    ang = jnp.concatenate([row[:, None] * inv[None, :], col[:, None] * inv[None, :]], axis=-1)
    return jnp.cos(ang), jnp.sin(ang)


def apply_rope(x, cos, sin):
    half = x.shape[-1] // 2
    x1, x2 = x[..., :half], x[..., half:]
    cs, sn = cos[None, :, None, :], sin[None, :, None, :]
    out = jnp.concatenate([x1 * cs - x2 * sn, x1 * sn + x2 * cs], axis=-1)
    return out.astype(x.dtype)


def split_heads(t, n, d):
    return t.reshape(t.shape[0], t.shape[1], n, d)


def split_ab(p):
    out, start = [], 0
    for s in AB_SIZES:
        out.append(p[..., start:start + s])
        start += s
    return out


def retention_chunkwise(q, k, v, log_gamma, state0):
    B, L, H, dk = q.shape
    dv = v.shape[-1]
    C = RET_CHUNK
    N = L // C
    qc = q.reshape(B, N, C, H, dk)
    kc = k.reshape(B, N, C, H, dk)
    vc = v.reshape(B, N, C, H, dv)
    pos = jnp.arange(C, dtype=jnp.float32)
    rel = pos[:, None] - pos[None, :]
    decay = jnp.where((rel >= 0)[None], jnp.exp(jnp.maximum(rel, 0.0)[None] * log_gamma[:, None, None]), 0.0)
    scores = jnp.einsum('bnihd,bnjhd->bnhij', qc, kc) * decay
    intra = jnp.einsum('bnhij,bnjhe->bnihe', scores, vc)
    k_decay = jnp.exp((C - 1.0 - pos)[:, None] * log_gamma[None, :])
    q_decay = jnp.exp((pos + 1.0)[:, None] * log_gamma[None, :])
    chunk_decay = jnp.exp(C * log_gamma)[None, :, None, None]
    chunk_kv = jnp.einsum('bnjhd,jh,bnjhe->nbhde', kc, k_decay, vc)

    def step(state, kv_n):
        return chunk_decay * state + kv_n, state

    state_final, state_prev = lax.scan(step, state0, chunk_kv)
    cross = jnp.einsum('bnihd,nbhde->bnihe', qc, state_prev) * q_decay[None, None, :, :, None]
    return (intra + cross).reshape(B, L, H, dv), state_final


def retention_out(o, g):
    B, L, H, dv = o.shape
    of = o.astype(jnp.float32)
    mu = jnp.mean(of, axis=-1, keepdims=True)
    var = jnp.mean(jnp.square(of - mu), axis=-1, keepdims=True)
    y = ((of - mu) * lax.rsqrt(var + EPS)).reshape(B, L, H * dv)
    return y.astype(g.dtype) * jax.nn.silu(g)


def block_attention(q, k, v):
    B, Lq, H, hd = q.shape
    KV = k.shape[2]
    G = H // KV
    nb = Lq // Q_BLOCK
    scale = hd ** -0.5
    qb = q.reshape(B, nb, Q_BLOCK, KV, G, hd).transpose(1, 0, 2, 3, 4, 5)

    def one_block(qi):
        s = jnp.einsum('bqkgd,bskd->bkgqs', qi, k).astype(jnp.float32) * scale
        p = jax.nn.softmax(s, axis=-1)
        return jnp.einsum('bkgqs,bskd->bqkgd', p.astype(v.dtype), v)

    o = lax.map(one_block, qb)
    return o.transpose(1, 0, 2, 3, 4, 5).reshape(B, Lq, H * hd)


def mix_ab(h_lat, h_ctx, w_in, w_out, ret_decay, q_g, k_g):
    B, L, _ = h_lat.shape
    cos_r, sin_r = grid_rope(L, RET_DK)
    cos_a, sin_a = grid_rope(L, ATT_HD)
    lat = split_ab(h_lat @ w_in)
    ctx = split_ab(h_ctx @ w_in)
    flip = lambda t: jnp.flip(t, axis=1)
    rq_l = apply_rope(split_heads(lat[0], RET_HEADS, RET_DK), cos_r, sin_r)
    rk_l = apply_rope(split_heads(lat[1], RET_HEADS, RET_DK), cos_r, sin_r) * (RET_DK ** -0.5)
    rv_l = split_heads(lat[2], RET_HEADS, RET_DV)
    rq_c = split_heads(ctx[0], RET_HEADS, RET_DK)
    rk_c = split_heads(ctx[1], RET_HEADS, RET_DK) * (RET_DK ** -0.5)
    rv_c = split_heads(ctx[2], RET_HEADS, RET_DV)
    log_gamma = jax.nn.log_sigmoid(ret_decay.astype(jnp.float32))
    zero = jnp.zeros((B, RET_HEADS, RET_DK, RET_DV), jnp.float32)
    oc_f, st_f = retention_chunkwise(rq_c, rk_c, rv_c, log_gamma[0], zero)
    ol_f, _ = retention_chunkwise(rq_l, rk_l, rv_l, log_gamma[0], st_f)
    oc_b, st_b = retention_chunkwise(flip(rq_c), flip(rk_c), flip(rv_c), log_gamma[1], zero)
    ol_b, _ = retention_chunkwise(flip(rq_l), flip(rk_l), flip(rv_l), log_gamma[1], st_b)
    ret_l = retention_out(ol_f + flip(ol_b), lat[3])
    ret_c = retention_out(oc_f + flip(oc_b), ctx[3])
    aq_l = apply_rope(rms_norm(split_heads(lat[4], ATT_HEADS, ATT_HD), q_g), cos_a, sin_a)
    ak_l = apply_rope(rms_norm(split_heads(lat[5], ATT_KV_HEADS, ATT_HD), k_g), cos_a, sin_a)
    av_l = split_heads(lat[6], ATT_KV_HEADS, ATT_HD)
    aq_c = rms_norm(split_heads(ctx[4], ATT_HEADS, ATT_HD), q_g)
    ak_c = rms_norm(split_heads(ctx[5], ATT_KV_HEADS, ATT_HD), k_g)
    av_c = split_heads(ctx[6], ATT_KV_HEADS, ATT_HD)
    att_l = block_attention(aq_l, jnp.concatenate([ak_c, ak_l], axis=1), jnp.concatenate([av_c, av_l], axis=1))
    att_c = block_attention(aq_c, ak_c, av_c)
    out_l = jnp.concatenate([ret_l, att_l], axis=-1) @ w_out
    out_c = jnp.concatenate([ret_c, att_c], axis=-1) @ w_out
    return out_l, out_c


def mix_chunk_mlp(h, w_in, v_g, w_s, b_s, w_out):
    B, L, _ = h.shape
    z = jax.nn.gelu(h @ w_in)
    u, v = z[..., :CM_WIDTH], z[..., CM_WIDTH:]
    v = rms_norm(v, v_g)
    vc = v.reshape(B, L // CM_CHUNK, CM_CHUNK, CM_GROUPS, CM_GROUP_DIM)
    sv = jnp.einsum('gpq,bnqgd->bnpgd', w_s, vc) + b_s.T[None, None, :, :, None]
    return (u * sv.reshape(B, L, CM_WIDTH)) @ w_out


def sq_relu_mlp(h, w1, w2):
    return jnp.square(jax.nn.relu(h @ w1)) @ w2


def setup_inputs(seed: int = 0) -> dict:
    key = jax.random.key(seed)
    ks = jax.random.split(key, 24)
    f32 = jnp.float32

    def nrm(k, shape, scale):
        return jax.random.normal(k, shape, f32) * scale

    def gain(k, shape):
        return 1.0 + 0.01 * jax.random.normal(k, shape, f32)

    base = 1.0 - 2.0 ** (-5.0 - jnp.arange(RET_HEADS, dtype=f32))
    logit = jnp.log(base) - jnp.log1p(-base)
    ret_decay = logit[None, None, :] + 0.05 * jax.random.normal(ks[10], (N_EVEN, 2, RET_HEADS), f32)
    return {
        'x': nrm(ks[0], (BATCH, SEQ, D_MODEL), 1.0),
        'c': nrm(ks[1], (BATCH, D_MODEL), 1.0),
        'ctx': nrm(ks[2], (BATCH, CTX_LEN, D_MODEL), 1.0),
        'c_ctx': nrm(ks[3], (D_MODEL,), 1.0),
        'mod_w': nrm(ks[4], (DEPTH, D_MODEL, 6 * D_MODEL), D_MODEL ** -0.5),
        'mod_b': nrm(ks[5], (DEPTH, 6 * D_MODEL), 0.01),
        'norm1_g': gain(ks[6], (DEPTH, D_MODEL)),
        'norm2_g': gain(ks[7], (DEPTH, D_MODEL)),
        'ab_w_in': nrm(ks[8], (N_EVEN, D_MODEL, AB_IN_W), D_MODEL ** -0.5),
        'ab_w_out': nrm(ks[9], (N_EVEN, AB_OUT_W, D_MODEL), AB_OUT_W ** -0.5),
        'ret_decay': ret_decay,
        'att_q_norm_g': gain(ks[11], (N_EVEN, ATT_HD)),
        'att_k_norm_g': gain(ks[12], (N_EVEN, ATT_HD)),
        'cm_w_in': nrm(ks[13], (N_ODD, D_MODEL, 2 * CM_WIDTH), D_MODEL ** -0.5),
        'cm_v_norm_g': gain(ks[14], (N_ODD, CM_WIDTH)),
        'cm_w_s': nrm(ks[15], (N_ODD, CM_GROUPS, CM_CHUNK, CM_CHUNK), CM_CHUNK ** -0.5),
        'cm_b_s': gain(ks[16], (N_ODD, CM_GROUPS, CM_CHUNK)),
        'cm_w_out': nrm(ks[17], (N_ODD, CM_WIDTH, D_MODEL), CM_WIDTH ** -0.5),
        'ff_w1': nrm(ks[18], (DEPTH, D_MODEL, FF_HIDDEN), D_MODEL ** -0.5),
        'ff_w2': nrm(ks[19], (DEPTH, FF_HIDDEN, D_MODEL), FF_HIDDEN ** -0.5),
    }


def reference(x, c, ctx, c_ctx, mod_w, mod_b, norm1_g, norm2_g, ab_w_in, ab_w_out, ret_decay,
              att_q_norm_g, att_k_norm_g, cm_w_in, cm_v_norm_g, cm_w_s, cm_b_s, cm_w_out, ff_w1, ff_w2):
    silu_c = jax.nn.silu(c)
    silu_cc = jax.nn.silu(c_ctx)
    h_stream = ctx
    for l in range(DEPTH):
        last = l == DEPTH - 1
        is_even = l % 2 == 0
        i = l // 2
        mod_lat = (silu_c @ mod_w[l] + mod_b[l])[:, None, :]
        mod_ctx = silu_cc @ mod_w[l] + mod_b[l]
        sh1, sc1, g1, sh2, sc2, g2 = jnp.split(mod_lat, 6, axis=-1)
        csh1, csc1, cg1, csh2, csc2, cg2 = jnp.split(mod_ctx, 6, axis=-1)
        h_lat = modulate(rms_norm(x, norm1_g[l]), sh1, sc1)
        if is_even or not last:
            h_ctx = modulate(rms_norm(h_stream, norm1_g[l]), csh1, csc1)
        if is_even:
            o_lat, o_ctx = mix_ab(h_lat, h_ctx, ab_w_in[i], ab_w_out[i], ret_decay[i],
                                  att_q_norm_g[i], att_k_norm_g[i])
        else:
            o_lat = mix_chunk_mlp(h_lat, cm_w_in[i], cm_v_norm_g[i], cm_w_s[i], cm_b_s[i], cm_w_out[i])
            if not last:
                o_ctx = mix_chunk_mlp(h_ctx, cm_w_in[i], cm_v_norm_g[i], cm_w_s[i], cm_b_s[i], cm_w_out[i])
        x = x + g1 * o_lat
        x = x + g2 * sq_relu_mlp(modulate(rms_norm(x, norm2_g[l]), sh2, sc2), ff_w1[l], ff_w2[l])
        if not last:
            h_stream = h_stream + cg1 * o_ctx
            h_stream = h_stream + cg2 * sq_relu_mlp(modulate(rms_norm(h_stream, norm2_g[l]), csh2, csc2),
                                                     ff_w1[l], ff_w2[l])
    return x
```

```python
import contextlib
import math
import numpy as np
import concourse.bass as bass
import concourse.mybir as mybir
from concourse.bass_utils import run_bass_kernel_spmd

F32 = mybir.dt.float32
BF16 = mybir.dt.bfloat16
AF = mybir.ActivationFunctionType
ALU = mybir.AluOpType
EPS = 1e-6
ROPE_BASE = 10000.0
GRID_W = 64


class Cfg:
    def __init__(self, D=1024, L=2048, CTX=256, DEPTH=4, RH=4, AH=4, AKV=2, NB=4, NCORES=8, FP=512):
        self.D, self.L, self.CTX, self.DEPTH = D, L, CTX, DEPTH
        self.RH, self.AH, self.AKV = RH, AH, AKV
        self.NB, self.NCORES = NB, NCORES
        self.KD = D // 128
        self.T = L + CTX
        self.NCH = self.T // 128
        self.FF = 4 * D
        self.FP = FP
        self.WD = D
        self.G = self.WD // 128
        self.NE = (DEPTH + 1) // 2
        self.NO = DEPTH // 2
        self.ABW = 4 * RH * 128 + AH * 128 + 2 * AKV * 128
        self.NMIX = RH + AH
        self.GQ = AH // AKV
        self.tiles = [(0, CTX)] + [(CTX + i * 512, CTX + (i + 1) * 512) for i in range(L // 512)]
        self.SLOT = 4096


class Sched:
    def __init__(self, nc, stack):
        self.nc = nc
        self.stack = stack
        self.engs = {"pe": nc.tensor, "act": nc.scalar, "dve": nc.vector, "pool": nc.gpsimd, "sp": nc.sync}
        self.psem, self.pcnt = {}, {}
        for e in self.engs:
            self.psem[e] = stack.enter_context(nc.semaphore("prog_" + e))
            self.pcnt[e] = 0
        self.seen = {e: {} for e in self.engs}
        self.last_w = {}
        self.readers = {}
        self.self_wait = {"pe": False, "act": True, "dve": True, "pool": True, "sp": True}
        self.n_wait = 0
        self.n_ops = 0

    def new_sem(self, name):
        return self.stack.enter_context(self.nc.semaphore(name))

    def _deps(self, reads, writes):
        deps = []
        for k in reads:
            t = self.last_w.get(k)
            if t is not None:
                deps.append(t)
        for k in writes:
            t = self.last_w.get(k)
            if t is not None:
                deps.append(t)
            deps.extend(self.readers.get(k, ()))
        return deps

    def _wait(self, e, deps):
        eng = self.engs[e]
        best = {}
        for (sem, val) in deps:
            sid = id(sem)
            if sid not in best or best[sid][1] < val:
                best[sid] = (sem, val)
        own = id(self.psem[e])
        for sid, (sem, val) in best.items():
            if sid == own and not self.self_wait[e]:
                continue
            if self.seen[e].get(sid, 0) >= val:
                continue
            eng.wait_ge(sem, val)
            self.n_wait += 1
            self.seen[e][sid] = val

    def _record(self, tok, reads, writes):
        for k in reads:
            lst = self.readers.setdefault(k, [])
            lst[:] = [t for t in lst if t[0] is not tok[0]]
            lst.append(tok)
        for k in writes:
            self.last_w[k] = tok
            self.readers[k] = []

    def op(self, e, fn, reads=(), writes=()):
        self._wait(e, self._deps(reads, writes))
        ins = fn(self.engs[e])
        self.pcnt[e] += 1
        self.n_ops += 1
        ins.then_inc(self.psem[e], 1)
        tok = (self.psem[e], self.pcnt[e])
        self._record(tok, reads, writes)
        return tok

    def dma(self, q, semst, fn, reads=(), writes=()):
        deps = self._deps(reads, writes)
        if semst[1] > 0:
            deps.append((semst[0], semst[1]))
        self._wait(q, deps)
        inss = fn(self.engs[q])
        if not isinstance(inss, (list, tuple)):
            inss = [inss]
        for ins in inss:
            ins.then_inc(semst[0], 16)
            semst[1] += 16
        tok = (semst[0], semst[1])
        self._record(tok, reads, writes)
        return tok

    def barrier(self, engines=("pe", "act", "dve")):
        for e in engines:
            deps = [(self.psem[o], self.pcnt[o]) for o in engines if o != e and self.pcnt[o] > 0]
            self._wait(e, deps)


def build_program(cfg, layers=None, debug_out=False):
    c = cfg
    D, KD, T, L, CTX, NB, DEPTH = c.D, c.KD, c.T, c.L, c.CTX, c.NB, c.DEPTH
    RH, AH, AKV, GQ, G, WD, FF, FP = c.RH, c.AH, c.AKV, c.GQ, c.G, c.WD, c.FF, c.FP
    NE, NO, NCH = c.NE, c.NO, c.NCH
    tiles = c.tiles
    NT = len(tiles)
    if layers is None:
        layers = list(range(DEPTH))
    NBC = NB + 1
    nc = bass.Bass("TRN2", target_bir_lowering=False)
    dt_in = lambda name, shape: nc.dram_tensor(name, list(shape), F32, kind="ExternalInput").ap()
    xin = dt_in("xin", [NB, D, T])
    cT = dt_in("cT", [128, KD, NBC])
    mod_w = dt_in("mod_w", [DEPTH, D, 6 * D])
    mod_b = dt_in("mod_b", [128, DEPTH, 6 * KD])
    gains = dt_in("gains", [128, DEPTH, 2, KD])
    ab_w_in = dt_in("ab_w_in", [NE, D, c.ABW])
    ab_w_out = dt_in("ab_w_out", [NE, c.NMIX * 128, D])
    ret_dec = dt_in("ret_dec", [128, NE * 2 * RH])
    qk_g = dt_in("qk_g", [128, NE, 2])
    cm_w_in = dt_in("cm_w_in", [max(NO, 1), D, 2 * WD])
    cm_vg = dt_in("cm_vg", [128, max(NO, 1), WD])
    cm_wsT = dt_in("cm_wsT", [max(NO, 1), 128, G, 128])
    cm_bs = dt_in("cm_bs", [1, max(NO, 1) * G * 128])
    cm_w_out = dt_in("cm_w_out", [max(NO, 1), WD, D])
    ff_w1 = dt_in("ff_w1", [DEPTH, D, FF])
    ff_w2 = dt_in("ff_w2", [DEPTH, FF, D])
    rope_cs = dt_in("rope_cs", [128, 2, L])
    dconst = dt_in("dconst", [128, 6 * 128 + 2])
    mats = dt_in("mats", [128, 4, 128])
    y = nc.dram_tensor("y", [NB, D, L], F32, kind="ExternalOutput").ap()

    with contextlib.ExitStack() as st:
        S = Sched(nc, st)
        sb = lambda name, shape, dt: st.enter_context(nc.sbuf_tensor(name, list(shape), dt))
        X = sb("X", [128, KD, T], F32)
        hbuf = sb("hbuf", [128, KD, T], BF16)
        NSLOT = 4
        slots = [sb(f"wslot{i}", [128, c.SLOT], BF16) for i in range(NSLOT)]
        slot_sem = [[S.new_sem(f"slot{i}"), 0] for i in range(NSLOT)]
        cos_t = sb("cos_t", [128, L], BF16)
        sin_t = sb("sin_t", [128, L], BF16)
        mats_b = sb("mats_b", [128, 4, 128], BF16)
        ones_m, ident_m, c_m, avg_m = (mats_b[:, i, :] for i in range(4))
        dcs = sb("dcs", [128, 6 * 128 + 2], BF16)
        relf, maskf, relb, maskb, posq1, posq2 = (dcs[:, i * 128:(i + 1) * 128] for i in range(6))
        posr = dcs[:, 768:769]
        posj = dcs[:, 769:770]
        modT = sb("modT", [128, DEPTH, 6 * KD, NBC], F32)
        gsc = sb("gsc", [128, DEPTH, 2, NBC, KD], F32)
        gains_s = sb("gains_s", [128, DEPTH, 2, KD], F32)
        modb_s = sb("modb_s", [128, DEPTH, 6 * KD], F32)
        lg = sb("lg", [128, NE * 2 * RH], F32)
        qkg_s = sb("qkg_s", [128, NE, 2], F32)
        silc = sb("silc", [128, KD, NBC], BF16)
        eps_t = sb("eps_t", [128, 1], F32)
        nshift_t = sb("nshift_t", [128, 1], F32)
        NPB = 6
        pbanks = [st.enter_context(nc.psum_tensor(f"psb{i}", [128, 512], F32)) for i in range(NPB)]
        ptrs = [st.enter_context(nc.psum_tensor(f"ptr{i}", [128, 1024], BF16)) for i in range(2)]
        pstate = {"i": 0}

        ps_live = set()

        def PS():
            while True:
                i = pstate["i"]
                pstate["i"] = (i + 1) % NPB
                if ("ps", i) not in ps_live:
                    return pbanks[i], ("ps", i)

        _uid = [0]

        def uid():
            _uid[0] += 1
            return f"_u{_uid[0]}"

        csem = {"pool": [S.new_sem("csem_pool"), 0], "sp": [S.new_sem("csem_sp"), 0]}
        xsem = [S.new_sem("xsem"), 0]
        ysem = [S.new_sem("ysem"), 0]
        xs_sems = [[S.new_sem(f"xs{k}"), 0] for k in range(4)]

        XK = lambda tt: [("X", tt, kc) for kc in range(KD)]
        HK = lambda tt: [("h", tt, kc) for kc in range(KD)]

        def tile_of_chunk(ch):
            t0 = ch * 128
            for tt, (a, b_) in enumerate(tiles):
                if a <= t0 < b_:
                    return tt
            raise ValueError

        def load_slot(si, pieces, extra_key=None):
            def fn(e):
                return [e.dma_start(out=d, in_=s) for (d, s) in pieces]
            return S.dma("pool", slot_sem[si], fn, writes=[("slot", si)])

        def cdma(q, out, in_, key):
            S.dma(q, csem[q], lambda e: e.dma_start(out=out, in_=in_), writes=[key])

        cdma("pool", cos_t[:, :], rope_cs[:, 0, :], "cos")
        cdma("pool", sin_t[:, :], rope_cs[:, 1, :], "sin")
        cdma("pool", mats_b[:, :, :], mats[:, :, :], "mats")
        cdma("pool", dcs[:, :], dconst[:, :], "dcs")
        cdma("sp", gains_s[:, :, :, :], gains[:, :, :, :], "gains")
        cdma("sp", modb_s[:, :, :], mod_b[:, :, :], "modb")
        cdma("sp", lg[:, :], ret_dec[:, :], "lg")
        cdma("sp", qkg_s[:, :, :], qk_g[:, :, :], "qkg")
        with contextlib.ExitStack() as pst:
            psb = lambda name, shape, dt: pst.enter_context(nc.sbuf_tensor(name, list(shape), dt))
            cT_s = psb("cT_s", [128, KD, NBC], F32)
            lgt = psb("lgt", [128, NE * 2 * RH], F32)
            cdma("sp", cT_s[:, :, :], cT[:, :, :], "cT")
            S.op("dve", lambda e: e.memset(eps_t[:, :], EPS), writes=["eps"])
            S.op("dve", lambda e: e.memset(nshift_t[:, :], -math.sqrt(128.0)), writes=["nshift"])
            S.op("act", lambda e: e.activation(out=silc[:, :, :], in_=cT_s[:, :, :], func=AF.Silu), reads=["cT"], writes=["silc"])
            S.op("act", lambda e: e.activation(out=lgt[:, :], in_=lg[:, :], func=AF.Exp, scale=-1.0), reads=["lg"], writes=["lgt"])
            S.op("act", lambda e: e.activation(out=lgt[:, :], in_=lgt[:, :], func=AF.Ln, bias=1.0, scale=1.0), reads=["lgt"], writes=["lgt"])
            S.op("act", lambda e: e.mul(lg[:, :], lgt[:, :], -1.0), reads=["lgt"], writes=["lg"])
            NPC = min(c.SLOT // KD, 512)
            si = 0
            for l in layers:
                mwv = mod_w[l].rearrange("(kc p) n -> p kc n", p=128)
                for pc in range(6 * D // NPC):
                    sl = slots[si]
                    slv = sl[:, 0:KD * NPC].rearrange("p (kc n) -> p kc n", kc=KD)
                    load_slot(si, [(slv[:, :, :], mwv[:, :, pc * NPC:(pc + 1) * NPC])])
                    ps, pk = PS()
                    nchk = NPC // 128

                    def mm(e):
                        ins = None
                        for j in range(nchk):
                            for kc in range(KD):
                                ins = e.matmul(ps[:, j * NBC:(j + 1) * NBC], lhsT=slv[:, kc, j * 128:(j + 1) * 128],
                                               rhs=silc[:, kc, :], start=(kc == 0), stop=(kc == KD - 1))
                        return ins
                    S.op("pe", mm, reads=[("slot", si), "silc"], writes=[pk])
                    for j in range(nchk):
                        ch = pc * nchk + j
                        S.op("act", lambda e: e.activation(out=modT[:, l, ch, :], in_=ps[:, j * NBC:(j + 1) * NBC],
                                                           func=AF.Identity, bias=modb_s[:, l, ch:ch + 1], scale=1.0),
                             reads=[pk, "modb"], writes=[("modT", l, ch)])
                    si = (si + 1) % NSLOT
                for ni in range(2):
                    for bi in range(NBC):
                        base = (3 * ni + 1) * KD
                        S.op("dve", lambda e: e.scalar_tensor_tensor(out=gsc[:, l, ni, bi, :], in0=modT[:, l, base:base + KD, bi],
                                                                     scalar=1.0, in1=gains_s[:, l, ni, :], op0=ALU.add, op1=ALU.mult),
                             reads=[("modT", l, base + k) for k in range(KD)] + ["gains"], writes=[("gsc", l, ni, bi)])
            S.barrier()

        def rope(src, src_keys, dst, dst_keys, lt0, W, ra, rb, ri, inplace=False):
            a, bb = ra[0], rb[0]
            ak, bk = ("ra", 0), ("rb", 0)
            S.op("dve", lambda e: e.tensor_tensor(out=bb[0:64, :W], in0=src[64:128, :], in1=sin_t[64:128, lt0:lt0 + W], op=ALU.mult),
                 reads=src_keys + ["sin"], writes=[bk])
            S.op("dve", lambda e: e.tensor_tensor(out=bb[64:128, :W], in0=src[0:64, :], in1=sin_t[0:64, lt0:lt0 + W], op=ALU.mult),
                 reads=src_keys + ["sin"], writes=[bk])
            S.op("dve", lambda e: e.tensor_tensor(out=a[:, :W], in0=src, in1=cos_t[:, lt0:lt0 + W], op=ALU.mult),
                 reads=src_keys + ["cos"], writes=[ak])
            S.op("dve", lambda e: e.tensor_tensor(out=dst, in0=a[:, :W], in1=bb[:, :W], op=ALU.add),
                 reads=[ak, bk], writes=dst_keys)

        def proj_fm(wv, wkey, tt, ps, pk):
            t0, t1 = tiles[tt]
            W = t1 - t0

            def mm(e):
                ins = None
                for kc in range(KD):
                    ins = e.matmul(ps[:, :W], lhsT=wv[:, kc, :], rhs=hbuf[:, kc, t0:t1], start=(kc == 0), stop=(kc == KD - 1))
                return ins
            S.op("pe", mm, reads=[wkey] + HK(tt), writes=[pk])

        def norm_bufs(pb):
            return dict(sq=[pb(f"sq{i}", [128, KD, 512], BF16) for i in range(2)],
                        tmp=[pb(f"ntmp{i}", [128, 512], F32) for i in range(3)],
                        srt=[pb(f"srt{i}", [128, 512], F32) for i in range(2)],
                        rstd=[pb(f"rstd{i}", [128, 512], F32) for i in range(2)])

        def norm_body(l, ni, b, tlist, bufs):
            sq, tmp, srt, rstd = bufs["sq"], bufs["tmp"], bufs["srt"], bufs["rstd"]
            cnt = 0
            for tt in tlist:
                t0, t1 = tiles[tt]
                W = t1 - t0
                bi = NB if tt == 0 else b
                i2 = tt % 2
                sqb = sq[i2]
                S.op("act", lambda e: e.activation(out=sqb[:, :, :W], in_=X[:, :, t0:t1], func=AF.Square),
                     reads=XK(tt), writes=[("sq", i2)])
                ps, pk = PS()

                def mm(e):
                    ins = None
                    for kc in range(KD):
                        ins = e.matmul(ps[:, :W], lhsT=ones_m, rhs=sqb[:, kc, :W], start=(kc == 0), stop=(kc == KD - 1))
                    return ins
                S.op("pe", mm, reads=[("sq", i2), "mats"], writes=[pk])
                S.op("act", lambda e: e.activation(out=srt[i2][:, :W], in_=ps[:, :W], func=AF.Ln, scale=1.0 / D, bias=eps_t[:, 0:1]),
                     reads=[pk, "eps"], writes=[("srt", i2)])
                S.op("act", lambda e: e.activation(out=rstd[i2][:, :W], in_=srt[i2][:, :W], func=AF.Exp, scale=-0.5), reads=[("srt", i2)], writes=[("rstd", i2)])
                for kc in range(KD):
                    tb = tmp[cnt % 3]
                    tk = ("ntmp", cnt % 3)
                    cnt += 1
                    S.op("dve", lambda e: e.scalar_tensor_tensor(out=tb[:, :W], in0=X[:, kc, t0:t1], scalar=gsc[:, l, ni, bi, kc:kc + 1],
                                                                 in1=rstd[i2][:, :W], op0=ALU.mult, op1=ALU.mult),
                         reads=[("X", tt, kc), ("rstd", i2), ("gsc", l, ni, bi)], writes=[tk])
                    ch = 3 * ni * KD + kc
                    S.op("act", lambda e: e.activation(out=hbuf[:, kc, t0:t1], in_=tb[:, :W], func=AF.Identity,
                                                       bias=modT[:, l, ch, bi:bi + 1], scale=1.0),
                         reads=[tk, ("modT", l, ch)], writes=[("h", tt, kc)])

        def norm_phase(l, ni, b, tlist):
            with contextlib.ExitStack() as ph:
                pb = lambda name, shape, dt: ph.enter_context(nc.sbuf_tensor(name + uid(), list(shape), dt))
                bufs = norm_bufs(pb)
                S.barrier()
                norm_body(l, ni, b, tlist, bufs)

        def resid(ps, pk, l, gi, bi, m, tt):
            t0, t1 = tiles[tt]
            W = t1 - t0
            ch = gi * KD + m
            S.op("dve", lambda e: e.scalar_tensor_tensor(out=X[:, m, t0:t1], in0=ps[:, :W], scalar=modT[:, l, ch, bi:bi + 1],
                                                         in1=X[:, m, t0:t1], op0=ALU.mult, op1=ALU.add),
                 reads=[pk, ("modT", l, ch), ("X", tt, m)], writes=[("X", tt, m)])

        def ffn_phase(l, b, tlist):
            nparts = FF // FP
            nj = FP // 128
            with contextlib.ExitStack() as ph:
                pb = lambda name, shape, dt: ph.enter_context(nc.sbuf_tensor(name + uid(), list(shape), dt))
                bufs = norm_bufs(pb)
                hid = [pb(f"hid{i}", [128, nj, 512], BF16) for i in range(2)]
                rl = [pb(f"rl{i}", [128, 512], BF16) for i in range(3)]
                S.barrier()
                norm_body(l, 1, b, tlist, bufs)
                w1v = ff_w1[l].rearrange("(kc p) n -> p kc n", p=128)
                w2v = ff_w2[l].rearrange("(c p) n -> p c n", p=128)
                items = [(part, tt) for part in range(nparts) for tt in tlist]
                wviews = {}
                rc = [0]

                def wv_of(part):
                    if part not in wviews:
                        sa, sb_ = (part % 2) * 2, (part % 2) * 2 + 1
                        w1s = slots[sa][:, 0:KD * FP].rearrange("p (kc n) -> p kc n", kc=KD)
                        w2s = slots[sb_][:, 0:nj * D].rearrange("p (c n) -> p c n", c=nj)
                        load_slot(sa, [(w1s[:, :, :], w1v[:, :, part * FP:(part + 1) * FP])])
                        load_slot(sb_, [(w2s[:, :, :], w2v[:, part * nj:(part + 1) * nj, :])])
                        wviews[part] = (sa, sb_, w1s, w2s)
                    return wviews[part]

                def F1(k):
                    part, tt = items[k]
                    sa, sb_, w1s, w2s = wv_of(part)
                    W = tiles[tt][1] - tiles[tt][0]
                    hb, hk = hid[k % 2], ("hid", k % 2)
                    for j in range(nj):
                        ps, pk = PS()
                        proj_fm(w1s[:, :, j * 128:(j + 1) * 128], ("slot", sa), tt, ps, pk)
                        rb_ = rl[rc[0] % 3]
                        rk = ("rl", rc[0] % 3)
                        rc[0] += 1
                        S.op("act", lambda e: e.activation(out=rb_[:, :W], in_=ps[:, :W], func=AF.Relu), reads=[pk], writes=[rk])
                        S.op("act", lambda e: e.activation(out=hb[:, j, :W], in_=rb_[:, :W], func=AF.Square), reads=[rk], writes=[hk + (j,)])

                def F2(k):
                    part, tt = items[k]
                    sa, sb_, w1s, w2s = wv_of(part)
                    W = tiles[tt][1] - tiles[tt][0]
                    bi = NB if tt == 0 else b
                    hb, hk = hid[k % 2], ("hid", k % 2)
                    for m in range(KD):
                        ps, pk = PS()

                        def mm(e):
                            ins = None
                            for j in range(nj):
                                ins = e.matmul(ps[:, :W], lhsT=w2s[:, j, m * 128:(m + 1) * 128], rhs=hb[:, j, :W],
                                               start=(j == 0), stop=(j == nj - 1))
                            return ins
                        S.op("pe", mm, reads=[("slot", sb_)] + [hk + (j,) for j in range(nj)], writes=[pk])
                        resid(ps, pk, l, 5, bi, m, tt)
                F1(0)
                for k in range(len(items)):
                    if k + 1 < len(items):
                        F1(k + 1)
                    F2(k)

        def cm_phase(l, b, tlist):
            i = l // 2
            NPC = min(c.SLOT // KD, 2 * WD)
            n_in_slots = (2 * WD) // NPC
            assert n_in_slots <= NSLOT
            cps = c.SLOT // D
            n_out_slots = (G + cps - 1) // cps
            with contextlib.ExitStack() as ph:
                pb = lambda name, shape, dt: ph.enter_context(nc.sbuf_tensor(name + uid(), list(shape), dt))
                xslots = [pb(f"xslot{k}", [128, c.SLOT], BF16) for k in range(n_out_slots)]
                xsem_ = xs_sems
                u_t = [pb(f"u_t{k}", [128, G, 512], BF16) for k in range(1)]
                vg = [pb(f"vg{k}", [128, WD], F32) for k in range(2)]
                vn = [pb(f"vn{k}", [128, WD], BF16) for k in range(2)]
                ss = [pb(f"cmss{k}", [128, 4], F32) for k in range(2)]
                vgain = pb("vgain", [128, WD], F32)
                wsT = pb("wsT", [128, G, 128], BF16)
                bsr = pb("bsr", [1, G * 128], BF16)
                S.barrier(("pe", "act", "dve", "pool"))
                wiv = cm_w_in[i].rearrange("(kc p) n -> p kc n", p=128)
                wins = []
                for s_ in range(n_in_slots):
                    v_ = slots[s_][:, 0:KD * NPC].rearrange("p (kc n) -> p kc n", kc=KD)
                    load_slot(s_, [(v_[:, :, :], wiv[:, :, s_ * NPC:(s_ + 1) * NPC])])
                    wins.append(v_)

                def win_cols(c0, n):
                    s_ = c0 // NPC
                    o = c0 % NPC
                    assert o + n <= NPC
                    return wins[s_][:, :, o:o + n], ("slot", s_)
                wov = cm_w_out[i].rearrange("(c p) n -> p c n", p=128)
                wouts = []
                for k in range(n_out_slots):
                    nch_ = min(cps, G - k * cps)
                    v_ = xslots[k][:, 0:nch_ * D].rearrange("p (c n) -> p c n", c=nch_)
                    S.dma("pool", xsem_[k], lambda e: e.dma_start(out=v_[:, :, :], in_=wov[:, k * cps:k * cps + nch_, :]), writes=[("xslot", k)])
                    wouts.append(v_)
                S.dma("pool", csem["pool"], lambda e: e.dma_start(out=wsT[:, :, :], in_=cm_wsT[i]), writes=["wsT"])
                S.dma("pool", csem["pool"], lambda e: e.dma_start(out=bsr[:, :], in_=cm_bs[:, i * G * 128:(i + 1) * G * 128]), writes=["bsr"])
                S.dma("pool", csem["pool"], lambda e: e.dma_start(out=vgain[:, :], in_=cm_vg[:, i, :]), writes=["vgain"])
                vcs = [0]
                for ti, tt in enumerate(tlist):
                    t0, t1 = tiles[tt]
                    W = t1 - t0
                    bi = NB if tt == 0 else b
                    ub, uk = u_t[0], ("u_t", 0)
                    uvb = ub
                    for g in range(G):
                        ps, pk = PS()
                        wv, wk = win_cols(g * 128, 128)
                        proj_fm(wv, wk, tt, ps, pk)
                        S.op("act", lambda e: e.activation(out=ub[:, g, :W], in_=ps[:, :W], func=AF.Gelu_apprx_tanh), reads=[pk], writes=[uk + (g,)])
                    npc_v = min(512, WD)
                    npv = WD // npc_v

                    def Vproj(cc):
                        c0 = t0 + cc * 128
                        vi = vcs[0] % 2
                        vcs[0] += 1
                        vgb, vgk = vg[vi], ("vg", vi)
                        for pcv in range(npv):
                            ps, pk = PS()
                            wv, wk = win_cols(WD + pcv * npc_v, npc_v)

                            def mm(e):
                                ins = None
                                for kc in range(KD):
                                    ins = e.matmul(ps[:, :npc_v], lhsT=hbuf[:, kc, c0:c0 + 128], rhs=wv[:, kc, :], start=(kc == 0), stop=(kc == KD - 1))
                                return ins
                            S.op("pe", mm, reads=[wk] + HK(tt), writes=[pk])
                            S.op("act", lambda e: e.activation(out=vgb[:, pcv * npc_v:(pcv + 1) * npc_v], in_=ps[:, :npc_v], func=AF.Gelu_apprx_tanh),
                                 reads=[pk], writes=[vgk + (pcv,)])
                        return vi

                    def Vrest(cc, vi):
                        vgb, vgk = vg[vi], ("vg", vi)
                        vnb, vnk = vn[vi], ("vn", vi)
                        ssb, ssk = ss[vi], ("cmss", vi)
                        allv = [vgk + (p_,) for p_ in range(npv)]
                        S.op("act", lambda e: e.activation(out=vnb[:, :], in_=vgb[:, :], func=AF.Square, accum_out=ssb[:, 0:1]),
                             reads=allv, writes=[ssk, vnk])
                        S.op("act", lambda e: e.activation(out=ssb[:, 1:2], in_=ssb[:, 0:1], func=AF.Sqrt, scale=1.0 / WD, bias=eps_t[:, 0:1]),
                             reads=[ssk, "eps"], writes=[ssk + (1,)])
                        S.op("dve", lambda e: e.reciprocal(out=ssb[:, 2:3], in_=ssb[:, 1:2]), reads=[ssk + (1,)], writes=[ssk + (2,)])
                        S.op("dve", lambda e: e.scalar_tensor_tensor(out=vnb[:, :], in0=vgb[:, :], scalar=ssb[:, 2:3], in1=vgain[:, :],
                                                                     op0=ALU.mult, op1=ALU.mult),
                             reads=allv + [ssk + (2,), "vgain"], writes=[vnk])
                        for g0 in range(0, G, 4):
                            ng = min(4, G - g0)
                            ps, pk = PS()

                            def mm(e):
                                ins = None
                                for gg in range(ng):
                                    g = g0 + gg
                                    e.matmul(ps[:, gg * 128:(gg + 1) * 128], lhsT=vnb[:, g * 128:(g + 1) * 128], rhs=wsT[:, g, :], start=True, stop=False)
                                    ins = e.matmul(ps[:, gg * 128:(gg + 1) * 128], lhsT=ones_m[0:1, :], rhs=bsr[0:1, g * 128:(g + 1) * 128], start=False, stop=True)
                                return ins
                            S.op("pe", mm, reads=[vnk, "wsT", "bsr", "mats"], writes=[pk])
                            S.op("dve", lambda e: e.tensor_tensor(out=uvb[:, g0:g0 + ng, cc * 128:(cc + 1) * 128],
                                                                  in0=ps[:, 0:ng * 128].rearrange("p (g n) -> p g n", g=ng),
                                                                  in1=ub[:, g0:g0 + ng, cc * 128:(cc + 1) * 128], op=ALU.mult),
                                 reads=[pk] + [uk + (g0 + gg,) for gg in range(ng)], writes=[uk + (g0 + gg,) for gg in range(ng)])
                    nchk_t = W // 128
                    vi_next = Vproj(0)
                    for cc in range(nchk_t):
                        vi_cur = vi_next
                        if cc + 1 < nchk_t:
                            vi_next = Vproj(cc + 1)
                        Vrest(cc, vi_cur)
                    uv_keys = [uk + (g,) for g in range(G)]
                    for m in range(KD):
                        ps, pk = PS()

                        def mm(e):
                            ins = None
                            for g in range(G):
                                wv_ = wouts[g // cps]
                                ins = e.matmul(ps[:, :W], lhsT=wv_[:, g % cps, m * 128:(m + 1) * 128], rhs=uvb[:, g, :W], start=(g == 0), stop=(g == G - 1))
                            return ins
                        S.op("pe", mm, reads=[("xslot", k) for k in range(n_out_slots)] + uv_keys, writes=[pk])
                        resid(ps, pk, l, 2, bi, m, tt)
                S.barrier(("pe", "act", "dve", "pool"))

        def ab_phase(l, b):
            i = l // 2
            winv = ab_w_in[i].rearrange("(kc p) n -> p kc n", p=128)
            woutv = ab_w_out[i].rearrange("(c p) n -> p c n", p=128)
            AX = mybir.AxisListType.X
            with contextlib.ExitStack() as ph:
                pb = lambda name, shape, dt: ph.enter_context(nc.sbuf_tensor(name + uid(), list(shape), dt))
                carve1 = lambda k: slots[1][:, 2048 + k * 512: 2048 + (k + 1) * 512]
                carve3 = lambda k: slots[3][:, 2048 + k * 512: 2048 + (k + 1) * 512]
                mixout = pb("mixout", [128, 2, T], BF16)
                kT = pb("kT", [128, T], BF16)
                v_tok = pb("v_tok", [128, NCH, 128], BF16)
                Sf_bf = pb("Sf_bf", [128, NCH, 128], BF16)
                Sb_bf = pb("Sb_bf", [128, NCH, 128], BF16)
                qh = [pb("qh0", [128, 512], BF16), carve1(0)]
                qf = [pb("qf0", [128, 512], BF16), carve1(1)]
                qb = [pb("qb0", [128, 512], BF16), carve1(2)]
                gt = [pb("gt0", [128, 512], BF16), carve1(3)]
                sqn = [pb("sqn0", [128, 512], BF16), carve3(0)]
                pT = [pb(f"pT{k}", [128, 512], BF16) for k in range(2)] + [carve3(1), carve3(2), carve3(3)]
                NPT = len(pT)
                ra = [pb("ra0", [128, 512], F32)]
                rb = [pb("rb0", [128, 512], F32)]
                sdv = [pb("sdv0", [128, 512], F32)]
                sdT = [pb(f"sdT{k}", [128, 128], BF16) for k in range(4)]
                NKT = 4
                ktok = [pb(f"ktok{k}", [128, 512], BF16) for k in range(NKT)]
                Sst = [pb(f"Sst{k}", [128, 2, 128], F32) for k in range(2)]
                Mh = pb("Mh", [128, 128], BF16)
                mtmp = pb("mtmp", [128, 2, 128], F32)
                qdf = pb("qdf", [128, 512], BF16)
                qdb = pb("qdb", [128, 512], BF16)
                dsc = pb("dsc", [128, 8], F32)
                wmean = pb("wmean", [128, KD], F32)
                S.barrier()
                CK = ("carve",)
                cnt = {"pT": 0, "sdT": 0, "kt": 0}
                inv_sqrt_dk = 1.0 / math.sqrt(128.0)
                LA = 2

                def out_proj(pair, slot_i):
                    wo = slots[slot_i][:, 0:2 * D].rearrange("p (c n) -> p c n", c=2)
                    load_slot(slot_i, [(wo[:, :, :], woutv[:, 2 * pair:2 * pair + 2, :])])
                    for tt in range(NT):
                        t0, t1 = tiles[tt]
                        W = t1 - t0
                        bi = NB if tt == 0 else b
                        for m in range(KD):
                            ps, pk = PS()

                            def mm(e):
                                e.matmul(ps[:, :W], lhsT=wo[:, 0, m * 128:(m + 1) * 128], rhs=mixout[:, 0, t0:t1], start=True, stop=False)
                                return e.matmul(ps[:, :W], lhsT=wo[:, 1, m * 128:(m + 1) * 128], rhs=mixout[:, 1, t0:t1], start=False, stop=True)
                            S.op("pe", mm, reads=[("slot", slot_i), ("mix", 0, tt), ("mix", 1, tt)], writes=[pk])
                            resid(ps, pk, l, 2, bi, m, tt)

                def v_proj(wv, wk):
                    for c0 in range(0, NCH, 4):
                        n = min(4, NCH - c0)
                        ps, pk = PS()

                        def mm(e):
                            ins = None
                            for cc in range(n):
                                ch = c0 + cc
                                for kc in range(KD):
                                    ins = e.matmul(ps[:, cc * 128:(cc + 1) * 128], lhsT=hbuf[:, kc, ch * 128:(ch + 1) * 128], rhs=wv[:, kc, :],
                                                   start=(kc == 0), stop=(kc == KD - 1))
                            return ins
                        tts = sorted(set(tile_of_chunk(c0 + cc) for cc in range(n)))
                        S.op("pe", mm, reads=[wk] + [k for tt in tts for k in HK(tt)], writes=[pk])
                        S.op("act", lambda e: e.activation(out=v_tok[:, c0:c0 + n, :], in_=ps[:, 0:n * 128].rearrange("p (c n) -> p c n", c=n), func=AF.Copy),
                             reads=[pk], writes=[("v_tok", c0)])

                def vkey(ch):
                    return ("v_tok", (ch // 4) * 4)

                for hh in range(RH):
                    si = (hh % 2) * 2
                    wsl = slots[si][:, 0:KD * 512].rearrange("p (kc f n) -> p kc f n", kc=KD, f=4)
                    load_slot(si, [(wsl[:, :, f, :], winv[:, :, f * RH * 128 + hh * 128: f * RH * 128 + (hh + 1) * 128]) for f in range(4)])
                    wk = ("slot", si)
                    S.op("dve", lambda e: e.reduce_sum(out=wmean[:, :], in_=wsl[:, :, 2, :], axis=AX), reads=[wk], writes=["wmean"])
                    S.op("dve", lambda e: e.tensor_scalar(out=wmean[:, :], in0=wmean[:, :], scalar1=-1.0 / 128, scalar2=None, op0=ALU.mult),
                         reads=["wmean"], writes=["wmean"])
                    for kc in range(KD):
                        S.op("dve", lambda e: e.tensor_scalar(out=wsl[:, kc, 2, :], in0=wsl[:, kc, 2, :], scalar1=wmean[:, kc:kc + 1], scalar2=None, op0=ALU.add),
                             reads=[wk, "wmean"], writes=[("wvc", kc)])
                    wvk = [("wvc", kc) for kc in range(KD)]
                    cf = (i * 2 + 0) * RH + hh
                    cb = (i * 2 + 1) * RH + hh
                    lgf, lgb = lg[:, cf:cf + 1], lg[:, cb:cb + 1]
                    S.op("act", lambda e: e.activation(out=mtmp[:, 0, :], in_=relf, func=AF.Exp, scale=lgf), reads=["dcs", "lg"], writes=["mtmp0"])
                    S.op("act", lambda e: e.activation(out=mtmp[:, 1, :], in_=relb, func=AF.Exp, scale=lgb), reads=["dcs", "lg"], writes=["mtmp1"])
                    S.op("dve", lambda e: e.tensor_tensor(out=mtmp[:, 0, :], in0=mtmp[:, 0, :], in1=maskf, op=ALU.mult), reads=["mtmp0", "dcs"], writes=["mtmp0"])
                    S.op("dve", lambda e: e.tensor_tensor(out=mtmp[:, 1, :], in0=mtmp[:, 1, :], in1=maskb, op=ALU.mult), reads=["mtmp1", "dcs"], writes=["mtmp1"])
                    S.op("dve", lambda e: e.tensor_tensor(out=mtmp[:, 0, :], in0=mtmp[:, 0, :], in1=mtmp[:, 1, :], op=ALU.add),
                         reads=["mtmp0", "mtmp1"], writes=["mtmp0"])
                    S.op("act", lambda e: e.mul(Mh[:, :], mtmp[:, 0, :], inv_sqrt_dk), reads=["mtmp0"], writes=["Mh"])
                    for k4 in range(4):
                        S.op("act", lambda e: e.activation(out=qdf[:, k4 * 128:(k4 + 1) * 128], in_=posq1, func=AF.Exp, scale=lgf), reads=["dcs", "lg"], writes=[("qdf", k4)])
                        S.op("act", lambda e: e.activation(out=qdb[:, k4 * 128:(k4 + 1) * 128], in_=posq2, func=AF.Exp, scale=lgb), reads=["dcs", "lg"], writes=[("qdb", k4)])
                    S.op("act", lambda e: e.activation(out=dsc[:, 0:1], in_=posr, func=AF.Exp, scale=lgf), reads=["dcs", "lg"], writes=[("dsc", 0)])
                    S.op("act", lambda e: e.activation(out=dsc[:, 1:2], in_=posj, func=AF.Exp, scale=lgb), reads=["dcs", "lg"], writes=[("dsc", 1)])
                    S.op("act", lambda e: e.activation(out=dsc[:, 2:3], in_=lgf, func=AF.Exp, scale=128.0), reads=["lg"], writes=[("dsc", 2)])
                    S.op("act", lambda e: e.activation(out=dsc[:, 3:4], in_=lgb, func=AF.Exp, scale=128.0), reads=["lg"], writes=[("dsc", 3)])
                    S.op("dve", lambda e: e.tensor_scalar(out=dsc[:, 4:6], in0=dsc[:, 0:2], scalar1=inv_sqrt_dk, scalar2=None, op0=ALU.mult),
                         reads=[("dsc", 0), ("dsc", 1)], writes=[("dsc", 4)])
                    qdkeys = [("qdf", k4) for k4 in range(4)] + [("qdb", k4) for k4 in range(4)]
                    for tt in range(NT):
                        t0, t1 = tiles[tt]
                        W = t1 - t0
                        ps, pk = PS()
                        proj_fm(wsl[:, :, 1, :], wk, tt, ps, pk)
                        if tt == 0:
                            S.op("act", lambda e: e.activation(out=kT[:, t0:t1], in_=ps[:, :W], func=AF.Copy), reads=[pk], writes=[("kT", tt)])
                        else:
                            rope(ps[:, :W], [pk], kT[:, t0:t1], [("kT", tt)], t0 - CTX, W, ra, rb, 0)
                    for c0 in range(0, NCH, 4):
                        n = min(4, NCH - c0)
                        ps, pk = PS()

                        def mmv(e):
                            ins = None
                            for cc in range(n):
                                ch = c0 + cc
                                for kc in range(KD):
                                    ins = e.matmul(ps[:, cc * 128:(cc + 1) * 128], lhsT=hbuf[:, kc, ch * 128:(ch + 1) * 128], rhs=wsl[:, kc, 2, :],
                                                   start=(kc == 0), stop=(kc == KD - 1))
                            return ins
                        tts = sorted(set(tile_of_chunk(c0 + cc) for cc in range(n)))
                        S.op("pe", mmv, reads=[wk] + wvk + [k for tt in tts for k in HK(tt)], writes=[pk])
                        S.op("act", lambda e: e.activation(out=v_tok[:, c0:c0 + n, :], in_=ps[:, 0:n * 128].rearrange("p (c n) -> p c n", c=n), func=AF.Copy),
                             reads=[pk], writes=[("v_tok", c0)])
                    nctx = CTX // 128
                    orders = [list(range(NCH)), list(range(nctx - 1, -1, -1)) + list(range(NCH - 1, nctx - 1, -1))]
                    Sdsts = [Sf_bf, Sb_bf]
                    for d_ in range(2):
                        S.op("dve", lambda e: e.memset(Sst[0][:, d_, :], 0.0), writes=[("Sst", 0, d_)])
                    nsteps = NCH - 1
                    nbatch = (nsteps + 3) // 4

                    def emit_batch(d_, bidx):
                        chs = orders[d_][bidx * 4: min(bidx * 4 + 4, nsteps)]
                        n = len(chs)
                        bank = cnt["kt"] % 2
                        kbuf = cnt["kt"] % NKT
                        cnt["kt"] += 1
                        pb_, tk = ptrs[bank], ("ptr", bank)

                        def tr(e):
                            ins = None
                            for k_, ch in enumerate(chs):
                                ins = e.transpose(pb_[:, k_ * 128:(k_ + 1) * 128], kT[:, ch * 128:(ch + 1) * 128], ident_m)
                            return ins
                        S.op("pe", tr, reads=sorted(set(("kT", tile_of_chunk(ch)) for ch in chs)) + ["mats"], writes=[tk])
                        S.op("dve", lambda e: e.tensor_scalar(out=ktok[kbuf][:, 0:n * 128], in0=pb_[:, 0:n * 128], scalar1=dsc[:, 4 + d_:5 + d_],
                                                              scalar2=None, op0=ALU.mult),
                             reads=[tk, ("dsc", 4)], writes=[("ktok", kbuf)])
                        return kbuf
                    kb_cur = [None, None]
                    kb_nxt = [emit_batch(0, 0), emit_batch(1, 0)]
                    for oi in range(nsteps + 1):
                        for d_ in range(2):
                            order = orders[d_]
                            if oi % 4 == 0 and oi < nsteps:
                                kb_cur[d_] = kb_nxt[d_]
                                if oi // 4 + 1 < nbatch:
                                    kb_nxt[d_] = emit_batch(d_, oi // 4 + 1)
                            ch = order[oi]
                            cur, nxt = Sst[oi % 2], Sst[(oi + 1) % 2]
                            ck, nk = ("Sst", oi % 2, d_), ("Sst", (oi + 1) % 2, d_)
                            S.op("act", lambda e: e.activation(out=Sdsts[d_][:, ch, :], in_=cur[:, d_, :], func=AF.Copy), reads=[ck], writes=[("Sbf", d_, ch)])
                            if oi == nsteps:
                                continue
                            kbuf = kb_cur[d_]
                            k_ = oi % 4
                            ps, pk = PS()
                            S.op("pe", lambda e: e.matmul(ps[:, 0:128], lhsT=ktok[kbuf][:, k_ * 128:(k_ + 1) * 128], rhs=v_tok[:, ch, :], start=True, stop=True),
                                 reads=[("ktok", kbuf), vkey(ch)], writes=[pk])
                            S.op("dve", lambda e: e.scalar_tensor_tensor(out=nxt[:, d_, :], in0=cur[:, d_, :], scalar=dsc[:, 2 + d_:3 + d_], in1=ps[:, 0:128],
                                                                         op0=ALU.mult, op1=ALU.add),
                                 reads=[ck, pk, ("dsc", 2 + d_)], writes=[nk])
                    mslot = hh % 2

                    def Qprep(tt):
                        bi_ = tt % 2
                        t0, t1 = tiles[tt]
                        W = t1 - t0
                        ps, pk = PS()
                        proj_fm(wsl[:, :, 0, :], wk, tt, ps, pk)
                        if tt == 0:
                            S.op("act", lambda e: e.activation(out=qh[bi_][:, :W], in_=ps[:, :W], func=AF.Copy), reads=[pk, CK], writes=[("qh", bi_)])
                        else:
                            rope(ps[:, :W], [pk, CK], qh[bi_][:, :W], [("qh", bi_)], t0 - CTX, W, ra, rb, 0)
                        S.op("dve", lambda e: e.tensor_tensor(out=qf[bi_][:, :W], in0=qh[bi_][:, :W], in1=qdf[:, :W], op=ALU.mult),
                             reads=[("qh", bi_), CK] + qdkeys, writes=[("qf", bi_)])
                        S.op("dve", lambda e: e.tensor_tensor(out=qb[bi_][:, :W], in0=qh[bi_][:, :W], in1=qdb[:, :W], op=ALU.mult),
                             reads=[("qh", bi_), CK] + qdkeys, writes=[("qb", bi_)])

                    def Mpart(tt):
                        bi_ = tt % 2
                        t0, t1 = tiles[tt]
                        W = t1 - t0
                        ps, pk = PS()
                        proj_fm(wsl[:, :, 3, :], wk, tt, ps, pk)
                        S.op("act", lambda e: e.activation(out=sdv[0][:, :W], in_=ps[:, :W], func=AF.Exp, scale=-1.0), reads=[pk], writes=[("sdv", 0)])
                        S.op("act", lambda e: e.activation(out=sdv[0][:, :W], in_=sdv[0][:, :W], func=AF.Ln, bias=1.0, scale=1.0), reads=[("sdv", 0)], writes=[("sdv", 0)])
                        S.op("act", lambda e: e.activation(out=sdv[0][:, :W], in_=sdv[0][:, :W], func=AF.Exp, scale=-1.0), reads=[("sdv", 0)], writes=[("sdv", 0)])
                        S.op("dve", lambda e: e.tensor_tensor(out=gt[bi_][:, :W], in0=ps[:, :W], in1=sdv[0][:, :W], op=ALU.mult),
                             reads=[pk, ("sdv", 0), CK], writes=[("gt", bi_)])
                        ops_, opk = PS()
                        ps_live.add(opk)
                        nch_t = W // 128
                        pend = []

                        def score(cc):
                            ch = t0 // 128 + cc
                            cs = slice(cc * 128, (cc + 1) * 128)
                            ps2, pk2 = PS()
                            S.op("pe", lambda e: e.matmul(ps2[:, 0:128], lhsT=kT[:, ch * 128:(ch + 1) * 128], rhs=qh[bi_][:, cs], start=True, stop=True),
                                 reads=[("kT", tt), ("qh", bi_), CK], writes=[pk2])
                            sdi = cnt["sdT"] % 4
                            cnt["sdT"] += 1
                            S.op("dve", lambda e: e.tensor_tensor(out=sdT[sdi][:, :], in0=ps2[:, 0:128], in1=Mh[:, :], op=ALU.mult),
                                 reads=[pk2, "Mh"], writes=[("sdT", sdi)])
                            return (cc, ch, cs, sdi)

                        def accum(item):
                            cc, ch, cs, sdi = item

                            def mm(e):
                                e.matmul(ops_[:, cs], lhsT=v_tok[:, ch, :], rhs=sdT[sdi][:, :], start=True, stop=False)
                                e.matmul(ops_[:, cs], lhsT=Sf_bf[:, ch, :], rhs=qf[bi_][:, cs], start=False, stop=False)
                                return e.matmul(ops_[:, cs], lhsT=Sb_bf[:, ch, :], rhs=qb[bi_][:, cs], start=False, stop=True)
                            S.op("pe", mm, reads=[vkey(ch), ("sdT", sdi), ("Sbf", 0, ch), ("Sbf", 1, ch), ("qf", bi_), ("qb", bi_), CK], writes=[opk])
                        for cc in range(nch_t):
                            pend.append(score(cc))
                            if len(pend) > LA:
                                accum(pend.pop(0))
                        while pend:
                            accum(pend.pop(0))
                        S.op("act", lambda e: e.activation(out=sqn[bi_][:, :W], in_=ops_[:, :W], func=AF.Square), reads=[opk, CK], writes=[("sqn", bi_)])
                        return ops_, opk

                    def Npart(tt, ops_, opk):
                        bi_ = tt % 2
                        t0, t1 = tiles[tt]
                        W = t1 - t0
                        pv_, pvk = PS()
                        S.op("pe", lambda e: e.matmul(pv_[:, :W], lhsT=avg_m, rhs=sqn[bi_][:, :W], start=True, stop=True), reads=[("sqn", bi_), CK, "mats"], writes=[pvk])
                        S.op("act", lambda e: e.activation(out=sdv[0][:, :W], in_=pv_[:, :W], func=AF.Ln, bias=eps_t[:, 0:1], scale=1.0),
                             reads=[pvk, "eps"], writes=[("sdv", 0)])
                        S.op("act", lambda e: e.activation(out=sdv[0][:, :W], in_=sdv[0][:, :W], func=AF.Exp, scale=-0.5), reads=[("sdv", 0)], writes=[("sdv", 0)])
                        S.op("dve", lambda e: e.tensor_tensor(out=ra[0][:, :W], in0=ops_[:, :W], in1=sdv[0][:, :W], op=ALU.mult),
                             reads=[opk, ("sdv", 0)], writes=[("ra", 0)])
                        S.op("dve", lambda e: e.tensor_tensor(out=mixout[:, mslot, t0:t1], in0=ra[0][:, :W], in1=gt[bi_][:, :W], op=ALU.mult),
                             reads=[("ra", 0), ("gt", bi_), CK], writes=[("mix", mslot, tt)])

                    Qprep(0)
                    prev = None
                    for tt in range(NT):
                        if tt + 1 < NT:
                            Qprep(tt + 1)
                        ops_, opk = Mpart(tt)
                        if prev is not None:
                            Npart(prev[0], prev[1], prev[2])
                            ps_live.discard(prev[2])
                        prev = (tt, ops_, opk)
                    Npart(prev[0], prev[1], prev[2])
                    ps_live.discard(prev[2])
                    if hh % 2 == 1:
                        out_proj(hh // 2, (hh // 2 % 2) * 2 + 1)

                qoff = 4 * RH * 128
                koff = qoff + AH * 128
                voff = koff + AKV * 128

                def qk_norm_a(ps, pk, W, bi_):
                    S.op("act", lambda e: e.activation(out=sqn[bi_][:, :W], in_=ps[:, :W], func=AF.Square), reads=[pk, CK], writes=[("sqn", bi_)])

                def qk_norm_b(ps, pk, W, bi_, gcol, dst, dst_keys, tt):
                    p2, p2k = PS()
                    S.op("pe", lambda e: e.matmul(p2[:, :W], lhsT=avg_m, rhs=sqn[bi_][:, :W], start=True, stop=True), reads=[("sqn", bi_), CK, "mats"], writes=[p2k])
                    S.op("act", lambda e: e.activation(out=sdv[0][:, :W], in_=p2[:, :W], func=AF.Ln, bias=eps_t[:, 0:1], scale=1.0),
                         reads=[p2k, "eps"], writes=[("sdv", 0)])
                    S.op("act", lambda e: e.activation(out=sdv[0][:, :W], in_=sdv[0][:, :W], func=AF.Exp, scale=-0.5), reads=[("sdv", 0)], writes=[("sdv", 0)])
                    S.op("dve", lambda e: e.scalar_tensor_tensor(out=ra[0][:, :W], in0=ps[:, :W], scalar=gcol, in1=sdv[0][:, :W], op0=ALU.mult, op1=ALU.mult),
                         reads=[pk, ("sdv", 0), "qkg"], writes=[("ra", 0)])
                    t0, t1 = tiles[tt]
                    if tt == 0:
                        S.op("act", lambda e: e.activation(out=dst, in_=ra[0][:, :W], func=AF.Copy), reads=[("ra", 0), CK], writes=dst_keys)
                    else:
                        rope(ra[0][:, :W], [("ra", 0), CK], dst, dst_keys, t0 - CTX, W, ra, rb, 0, inplace=True)

                for gi in range(AKV):
                    si = (gi % 2) * 2
                    nf = GQ + 2
                    wsl = slots[si][:, 0:KD * nf * 128].rearrange("p (kc f n) -> p kc f n", kc=KD, f=nf)
                    pieces = [(wsl[:, :, j, :], winv[:, :, qoff + (gi * GQ + j) * 128: qoff + (gi * GQ + j + 1) * 128]) for j in range(GQ)]
                    pieces.append((wsl[:, :, GQ, :], winv[:, :, koff + gi * 128: koff + (gi + 1) * 128]))
                    pieces.append((wsl[:, :, GQ + 1, :], winv[:, :, voff + gi * 128: voff + (gi + 1) * 128]))
                    load_slot(si, pieces)
                    wk = ("slot", si)
                    for tt in range(NT):
                        t0, t1 = tiles[tt]
                        W = t1 - t0
                        ps, pk = PS()
                        proj_fm(wsl[:, :, GQ, :], wk, tt, ps, pk)
                        qk_norm_a(ps, pk, W, 0)
                        qk_norm_b(ps, pk, W, 0, qkg_s[:, i, 1:2], kT[:, t0:t1], [("kT", tt)], tt)
                    v_proj(wsl[:, :, GQ + 1, :], wk)
                    its = [(j, tt) for j in range(GQ) for tt in range(NT)]

                    def prepA(n):
                        j, tt = its[n]
                        W = tiles[tt][1] - tiles[tt][0]
                        ps, pk = PS()
                        proj_fm(wsl[:, :, j, :], wk, tt, ps, pk)
                        qk_norm_a(ps, pk, W, n % 2)
                        ps_live.add(pk)
                        return ps, pk

                    def prepB(n, ps, pk):
                        j, tt = its[n]
                        W = tiles[tt][1] - tiles[tt][0]
                        qk_norm_b(ps, pk, W, n % 2, qkg_s[:, i, 0:1], qh[n % 2][:, :W], [("qh", n % 2)], tt)
                        ps_live.discard(pk)
                    pq = prepA(0)
                    prepB(0, *pq)
                    for n, (j, tt) in enumerate(its):
                        bi_ = n % 2
                        mslot = j % 2
                        t0, t1 = tiles[tt]
                        W = t1 - t0
                        kchunks = list(range(CTX // 128)) if tt == 0 else list(range(NCH))
                        ops_, opk = PS()
                        ps_live.add(opk)
                        dps_, dpk = PS()
                        ps_live.add(dpk)
                        nxt_pq = prepA(n + 1) if n + 1 < len(its) else None
                        pend = []

                        def qk(n_, ch):
                            ps, pk = PS()
                            S.op("pe", lambda e: e.matmul(ps[:, :W], lhsT=kT[:, ch * 128:(ch + 1) * 128], rhs=qh[bi_][:, :W], start=True, stop=True),
                                 reads=[("kT", tile_of_chunk(ch)), ("qh", bi_), CK], writes=[pk])
                            pi = cnt["pT"] % NPT
                            cnt["pT"] += 1
                            S.op("act", lambda e: e.activation(out=pT[pi][:, :W], in_=ps[:, :W], func=AF.Exp, scale=inv_sqrt_dk, bias=nshift_t[:, 0:1]),
                                 reads=[pk, "nshift", CK], writes=[("pT", pi)])
                            return (n_, ch, pi)

                        def pv(item):
                            n_, ch, pi = item
                            first, last_ = (n_ == 0), (n_ == len(kchunks) - 1)

                            def mm(e):
                                e.matmul(ops_[:, :W], lhsT=v_tok[:, ch, :], rhs=pT[pi][:, :W], start=first, stop=last_)
                                return e.matmul(dps_[:, :W], lhsT=ones_m, rhs=pT[pi][:, :W], start=first, stop=last_)
                            S.op("pe", mm, reads=[vkey(ch), ("pT", pi), CK, "mats"], writes=[opk, dpk])
                        for n_, ch in enumerate(kchunks):
                            pend.append(qk(n_, ch))
                            if len(pend) > LA + 1:
                                pv(pend.pop(0))
                            if n_ == min(3, len(kchunks) - 1) and nxt_pq is not None:
                                prepB(n + 1, *nxt_pq)
                                nxt_pq = None
                        while pend:
                            pv(pend.pop(0))
                        S.op("dve", lambda e: e.reciprocal(out=sdv[0][:, :W], in_=dps_[:, :W]), reads=[dpk], writes=[("sdv", 0)])
                        S.op("dve", lambda e: e.tensor_tensor(out=mixout[:, mslot, t0:t1], in0=ops_[:, :W], in1=sdv[0][:, :W], op=ALU.mult),
                             reads=[opk, ("sdv", 0)], writes=[("mix", mslot, tt)])
                        ps_live.discard(opk)
                        ps_live.discard(dpk)
                        if tt == NT - 1 and j % 2 == 1:
                            pair = (RH + gi * GQ + j) // 2
                            out_proj(pair, (pair % 2) * 2 + 1)
                S.barrier()
                S.op("dve", lambda e: e.memset(dsc[:, 7:8], 0.0), writes=[("slot", 1), ("slot", 3), ("dsc", 7)])

        ytoks = []

        def load_x_tile(b, tt):
            t0, t1 = tiles[tt]
            S.dma("sp", xsem, lambda e: [e.dma_start(out=X[:, kc, t0:t1], in_=xin[b, kc * 128:(kc + 1) * 128, t0:t1]) for kc in range(KD)], writes=XK(tt))

        for tt in range(NT):
            load_x_tile(0, tt)
        for b in range(NB):
            for l in layers:
                last = (l == DEPTH - 1)
                even = (l % 2 == 0)
                skip_ctx_mix = last and not even
                tl_mix = list(range(1, NT)) if skip_ctx_mix else list(range(NT))
                tl_ffn = list(range(1, NT)) if last else list(range(NT))
                norm_phase(l, 0, b, tl_mix)
                if even:
                    ab_phase(l, b)
                else:
                    cm_phase(l, b, tl_mix)
                ffn_phase(l, b, tl_ffn)
            for tt in range(1, NT):
                t0, t1 = tiles[tt]
                ytoks.append(S.dma("sp", ysem, lambda e: [e.dma_start(out=y[b, kc * 128:(kc + 1) * 128, t0 - CTX:t1 - CTX], in_=X[:, kc, t0:t1]) for kc in range(KD)],
                                   reads=XK(tt)))
            if b + 1 < NB:
                for tt in range(NT):
                    load_x_tile(b + 1, tt)
        ytok = ytoks[-1]
        S._wait("sp", [ytok])
        S.barrier(("pe", "act", "dve", "pool", "sp"))
        nc._sched_stats = (S.n_ops, S.n_wait)
    return nc


def rope_tables(L):
    rows = L // GRID_W
    row = np.repeat(np.arange(rows, dtype=np.float32), GRID_W)
    col = np.tile(np.arange(GRID_W, dtype=np.float32), rows)
    n_freq = 32
    inv = (ROPE_BASE ** (-np.arange(n_freq, dtype=np.float32) / n_freq)).astype(np.float32)
    ang = np.concatenate([row[:, None] * inv[None, :], col[:, None] * inv[None, :]], axis=-1)
    cos, sin = np.cos(ang).astype(np.float32), np.sin(ang).astype(np.float32)
    out = np.zeros((128, 2, L), np.float32)
    out[0:64, 0] = cos.T
    out[64:128, 0] = cos.T
    out[0:64, 1] = sin.T
    out[64:128, 1] = -sin.T
    return out


def const_tables():
    j = np.arange(128, dtype=np.float32)[:, None]
    i = np.arange(128, dtype=np.float32)[None, :]
    d = np.zeros((128, 6 * 128 + 2), np.float32)
    d[:, 0:128] = np.maximum(i - j, 0)
    d[:, 128:256] = (i >= j)
    d[:, 256:384] = np.maximum(j - i, 0)
    d[:, 384:512] = (j >= i)
    d[:, 512:640] = np.broadcast_to(i + 1, (128, 128))
    d[:, 640:768] = np.broadcast_to(128 - i, (128, 128))
    d[:, 768] = 127 - j[:, 0]
    d[:, 769] = j[:, 0]
    m = np.zeros((128, 4, 128), np.float32)
    m[:, 0] = 1.0
    m[:, 1] = np.eye(128)
    m[:, 2] = np.eye(128) - 1.0 / 128
    m[:, 3] = 1.0 / 128
    return d, m


def prep_inputs(cfg, inp):
    c = cfg
    f = lambda a: np.ascontiguousarray(np.asarray(a, dtype=np.float32))
    KD, D, DEPTH = c.KD, c.D, c.DEPTH
    common = {}
    common["mod_w"] = f(inp["mod_w"])
    common["mod_b"] = f(np.asarray(inp["mod_b"]).reshape(DEPTH, 6 * KD, 128).transpose(2, 0, 1))
    g1 = np.asarray(inp["norm1_g"]).reshape(DEPTH, KD, 128)
    g2 = np.asarray(inp["norm2_g"]).reshape(DEPTH, KD, 128)
    common["gains"] = f(np.stack([g1, g2], axis=1).transpose(3, 0, 1, 2))
    common["ab_w_in"] = f(inp["ab_w_in"])
    common["ab_w_out"] = f(inp["ab_w_out"])
    rd = np.asarray(inp["ret_decay"]).reshape(1, -1)
    common["ret_dec"] = f(np.broadcast_to(rd, (128, rd.shape[1])))
    common["qk_g"] = f(np.stack([np.asarray(inp["att_q_norm_g"]), np.asarray(inp["att_k_norm_g"])], axis=-1).transpose(1, 0, 2))
    common["cm_w_in"] = f(inp["cm_w_in"])
    vg = np.asarray(inp["cm_v_norm_g"])
    common["cm_vg"] = f(np.broadcast_to(vg[None], (128,) + vg.shape))
    common["cm_wsT"] = f(np.asarray(inp["cm_w_s"]).transpose(0, 3, 1, 2))
    common["cm_bs"] = f(np.asarray(inp["cm_b_s"]).reshape(1, -1))
    common["cm_w_out"] = f(inp["cm_w_out"])
    common["ff_w1"] = f(inp["ff_w1"])
    common["ff_w2"] = f(inp["ff_w2"])
    common["rope_cs"] = rope_tables(c.L)
    d, m = const_tables()
    common["dconst"] = d
    common["mats"] = m
    x = np.asarray(inp["x"], dtype=np.float32)
    ctx = np.asarray(inp["ctx"], dtype=np.float32)
    cc = np.asarray(inp["c"], dtype=np.float32)
    c_ctx = np.asarray(inp["c_ctx"], dtype=np.float32)
    maps = []
    for core in range(c.NCORES):
        bs = slice(core * c.NB, (core + 1) * c.NB)
        xin = np.concatenate([ctx[bs].transpose(0, 2, 1), x[bs].transpose(0, 2, 1)], axis=2)
        cols = np.concatenate([cc[bs], c_ctx[None]], axis=0)
        cT = cols.T.reshape(KD, 128, c.NB + 1).transpose(1, 0, 2)
        m_ = dict(common)
        m_["xin"] = f(xin)
        m_["cT"] = f(cT)
        maps.append(m_)
    return maps


_CACHE = {}


def kernel(**inputs):
    cfg = Cfg()
    if "nc" not in _CACHE:
        _CACHE["nc"] = build_program(cfg)
    nc = _CACHE["nc"]
    maps = prep_inputs(cfg, inputs)
    res = run_bass_kernel_spmd(nc, maps, core_ids=list(range(cfg.NCORES)))
    outs = [np.asarray(r["y"]).transpose(0, 2, 1) for r in res.results]
    return np.ascontiguousarray(np.concatenate(outs, axis=0).astype(np.float32))
```

```python
import contextlib
import math
import numpy as np
import concourse.bass as bass
import concourse.mybir as mybir
from concourse.bass_utils import run_bass_kernel_spmd

F32 = mybir.dt.float32
BF16 = mybir.dt.bfloat16
AF = mybir.ActivationFunctionType
ALU = mybir.AluOpType
EPS = 1e-6
ROPE_BASE = 10000.0
GRID_W = 64


class Cfg:
    def __init__(self, D=1024, L=2048, CTX=256, DEPTH=4, RH=4, AH=4, AKV=2, NB=4, NCORES=8, FP=512):
        self.D, self.L, self.CTX, self.DEPTH = D, L, CTX, DEPTH
        self.RH, self.AH, self.AKV = RH, AH, AKV
        self.NB, self.NCORES = NB, NCORES
        self.KD = D // 128
        self.T = L + CTX
        self.NCH = self.T // 128
        self.FF = 4 * D
        self.FP = FP
        self.WD = D
        self.G = self.WD // 128
        self.NE = (DEPTH + 1) // 2
        self.NO = DEPTH // 2
        self.ABW = 4 * RH * 128 + AH * 128 + 2 * AKV * 128
        self.NMIX = RH + AH
        self.GQ = AH // AKV
        self.tiles = [(0, CTX)] + [(CTX + i * 512, CTX + (i + 1) * 512) for i in range(L // 512)]
        self.SLOT = 4096


class Sched:
    def __init__(self, nc, stack):
        self.nc = nc
        self.stack = stack
        self.engs = {"pe": nc.tensor, "act": nc.scalar, "dve": nc.vector, "pool": nc.gpsimd, "sp": nc.sync}
        self.psem, self.pcnt = {}, {}
        for e in self.engs:
            self.psem[e] = stack.enter_context(nc.semaphore("prog_" + e))
            self.pcnt[e] = 0
        self.seen = {e: {} for e in self.engs}
        self.last_w = {}
        self.readers = {}
        self.self_wait = {"pe": False, "act": True, "dve": True, "pool": True, "sp": True}
        self.n_wait = 0
        self.n_ops = 0

    def new_sem(self, name):
        return self.stack.enter_context(self.nc.semaphore(name))

    def _deps(self, reads, writes):
        deps = []
        for k in reads:
            t = self.last_w.get(k)
            if t is not None:
                deps.append(t)
        for k in writes:
            t = self.last_w.get(k)
            if t is not None:
                deps.append(t)
            deps.extend(self.readers.get(k, ()))
        return deps

    def _wait(self, e, deps):
        eng = self.engs[e]
        best = {}
        for (sem, val) in deps:
            sid = id(sem)
            if sid not in best or best[sid][1] < val:
                best[sid] = (sem, val)
        own = id(self.psem[e])
        for sid, (sem, val) in best.items():
            if sid == own and not self.self_wait[e]:
                continue
            if self.seen[e].get(sid, 0) >= val:
                continue
            eng.wait_ge(sem, val)
            self.n_wait += 1
            self.seen[e][sid] = val

    def _record(self, tok, reads, writes):
        for k in reads:
            lst = self.readers.setdefault(k, [])
            lst[:] = [t for t in lst if t[0] is not tok[0]]
            lst.append(tok)
        for k in writes:
            self.last_w[k] = tok
            self.readers[k] = []

    def op(self, e, fn, reads=(), writes=()):
        self._wait(e, self._deps(reads, writes))
        ins = fn(self.engs[e])
        self.pcnt[e] += 1
        self.n_ops += 1
        ins.then_inc(self.psem[e], 1)
        tok = (self.psem[e], self.pcnt[e])
        self._record(tok, reads, writes)
        return tok

    def dma(self, q, semst, fn, reads=(), writes=()):
        deps = self._deps(reads, writes)
        if semst[1] > 0:
            deps.append((semst[0], semst[1]))
        self._wait(q, deps)
        inss = fn(self.engs[q])
        if not isinstance(inss, (list, tuple)):
            inss = [inss]
        for ins in inss:
            ins.then_inc(semst[0], 16)
            semst[1] += 16
        tok = (semst[0], semst[1])
        self._record(tok, reads, writes)
        return tok

    def barrier(self, engines=("pe", "act", "dve")):
        for e in engines:
            deps = [(self.psem[o], self.pcnt[o]) for o in engines if o != e and self.pcnt[o] > 0]
            self._wait(e, deps)


def build_program(cfg, layers=None, debug_out=False):
    c = cfg
    D, KD, T, L, CTX, NB, DEPTH = c.D, c.KD, c.T, c.L, c.CTX, c.NB, c.DEPTH
    RH, AH, AKV, GQ, G, WD, FF, FP = c.RH, c.AH, c.AKV, c.GQ, c.G, c.WD, c.FF, c.FP
    NE, NO, NCH = c.NE, c.NO, c.NCH
    tiles = c.tiles
    NT = len(tiles)
    if layers is None:
        layers = list(range(DEPTH))
    NBC = NB + 1
    nc = bass.Bass("TRN2", target_bir_lowering=False)
    dt_in = lambda name, shape: nc.dram_tensor(name, list(shape), F32, kind="ExternalInput").ap()
    xin = dt_in("xin", [NB, D, T])
    cT = dt_in("cT", [128, KD, NBC])
    mod_w = dt_in("mod_w", [DEPTH, D, 6 * D])
    mod_b = dt_in("mod_b", [128, DEPTH, 6 * KD])
    gains = dt_in("gains", [128, DEPTH, 2, KD])
    ab_w_in = dt_in("ab_w_in", [NE, D, c.ABW])
    ab_w_out = dt_in("ab_w_out", [NE, c.NMIX * 128, D])
    ret_dec = dt_in("ret_dec", [128, NE * 2 * RH])
    qk_g = dt_in("qk_g", [128, NE, 2])
    cm_w_in = dt_in("cm_w_in", [max(NO, 1), D, 2 * WD])
    cm_vg = dt_in("cm_vg", [128, max(NO, 1), WD])
    cm_wsT = dt_in("cm_wsT", [max(NO, 1), 128, G, 128])
    cm_bs = dt_in("cm_bs", [1, max(NO, 1) * G * 128])
    cm_w_out = dt_in("cm_w_out", [max(NO, 1), WD, D])
    ff_w1 = dt_in("ff_w1", [DEPTH, D, FF])
    ff_w2 = dt_in("ff_w2", [DEPTH, FF, D])
    rope_cs = dt_in("rope_cs", [128, 2, L])
    dconst = dt_in("dconst", [128, 6 * 128 + 2])
    mats = dt_in("mats", [128, 4, 128])
    y = nc.dram_tensor("y", [NB, D, L], F32, kind="ExternalOutput").ap()

    with contextlib.ExitStack() as st:
        S = Sched(nc, st)
        sb = lambda name, shape, dt: st.enter_context(nc.sbuf_tensor(name, list(shape), dt))
        X = sb("X", [128, KD, T], F32)
        hbuf = sb("hbuf", [128, KD, T], BF16)
        NSLOT = 4
        slots = [sb(f"wslot{i}", [128, c.SLOT], BF16) for i in range(NSLOT)]
        slot_sem = [[S.new_sem(f"slot{i}"), 0] for i in range(NSLOT)]
        cos_t = sb("cos_t", [128, L], BF16)
        sin_t = sb("sin_t", [128, L], BF16)
        mats_b = sb("mats_b", [128, 4, 128], BF16)
        ones_m, ident_m, c_m, avg_m = (mats_b[:, i, :] for i in range(4))
        dcs = sb("dcs", [128, 6 * 128 + 2], BF16)
        relf, maskf, relb, maskb, posq1, posq2 = (dcs[:, i * 128:(i + 1) * 128] for i in range(6))
        posr = dcs[:, 768:769]
        posj = dcs[:, 769:770]
        modT = sb("modT", [128, DEPTH, 6 * KD, NBC], F32)
        gsc = sb("gsc", [128, DEPTH, 2, NBC, KD], F32)
        gains_s = sb("gains_s", [128, DEPTH, 2, KD], F32)
        modb_s = sb("modb_s", [128, DEPTH, 6 * KD], F32)
        lg = sb("lg", [128, NE * 2 * RH], F32)
        qkg_s = sb("qkg_s", [128, NE, 2], F32)
        silc = sb("silc", [128, KD, NBC], BF16)
        eps_t = sb("eps_t", [128, 1], F32)
        nshift_t = sb("nshift_t", [128, 1], F32)
        NPB = 6
        pbanks = [st.enter_context(nc.psum_tensor(f"psb{i}", [128, 512], F32)) for i in range(NPB)]
        ptrs = [st.enter_context(nc.psum_tensor(f"ptr{i}", [128, 1024], BF16)) for i in range(2)]
        pstate = {"i": 0}

        ps_live = set()

        def PS():
            while True:
                i = pstate["i"]
                pstate["i"] = (i + 1) % NPB
                if ("ps", i) not in ps_live:
                    return pbanks[i], ("ps", i)

        _uid = [0]

        def uid():
            _uid[0] += 1
            return f"_u{_uid[0]}"

        csem = {"pool": [S.new_sem("csem_pool"), 0], "sp": [S.new_sem("csem_sp"), 0]}
        xsem = [S.new_sem("xsem"), 0]
        ysem = [S.new_sem("ysem"), 0]
        xs_sems = [[S.new_sem(f"xs{k}"), 0] for k in range(4)]

        XK = lambda tt: [("X", tt, kc) for kc in range(KD)]
        HK = lambda tt: [("h", tt, kc) for kc in range(KD)]

        def tile_of_chunk(ch):
            t0 = ch * 128
            for tt, (a, b_) in enumerate(tiles):
                if a <= t0 < b_:
                    return tt
            raise ValueError

        def load_slot(si, pieces, extra_key=None):
            def fn(e):
                return [e.dma_start(out=d, in_=s) for (d, s) in pieces]
            return S.dma("pool", slot_sem[si], fn, writes=[("slot", si)])

        def cdma(q, out, in_, key):
            S.dma(q, csem[q], lambda e: e.dma_start(out=out, in_=in_), writes=[key])

        cdma("pool", cos_t[:, :], rope_cs[:, 0, :], "cos")
        cdma("pool", sin_t[:, :], rope_cs[:, 1, :], "sin")
        cdma("pool", mats_b[:, :, :], mats[:, :, :], "mats")
        cdma("pool", dcs[:, :], dconst[:, :], "dcs")
        cdma("sp", gains_s[:, :, :, :], gains[:, :, :, :], "gains")
        cdma("sp", modb_s[:, :, :], mod_b[:, :, :], "modb")
        cdma("sp", lg[:, :], ret_dec[:, :], "lg")
        cdma("sp", qkg_s[:, :, :], qk_g[:, :, :], "qkg")
        with contextlib.ExitStack() as pst:
            psb = lambda name, shape, dt: pst.enter_context(nc.sbuf_tensor(name, list(shape), dt))
            cT_s = psb("cT_s", [128, KD, NBC], F32)
            lgt = psb("lgt", [128, NE * 2 * RH], F32)
            cdma("sp", cT_s[:, :, :], cT[:, :, :], "cT")
            S.op("dve", lambda e: e.memset(eps_t[:, :], EPS), writes=["eps"])
            S.op("dve", lambda e: e.memset(nshift_t[:, :], -math.sqrt(128.0)), writes=["nshift"])
            S.op("act", lambda e: e.activation(out=silc[:, :, :], in_=cT_s[:, :, :], func=AF.Silu), reads=["cT"], writes=["silc"])
            S.op("act", lambda e: e.activation(out=lgt[:, :], in_=lg[:, :], func=AF.Exp, scale=-1.0), reads=["lg"], writes=["lgt"])
            S.op("act", lambda e: e.activation(out=lgt[:, :], in_=lgt[:, :], func=AF.Ln, bias=1.0, scale=1.0), reads=["lgt"], writes=["lgt"])
            S.op("act", lambda e: e.mul(lg[:, :], lgt[:, :], -1.0), reads=["lgt"], writes=["lg"])
            NPC = min(c.SLOT // KD, 512)
            si = 0
            for l in layers:
                mwv = mod_w[l].rearrange("(kc p) n -> p kc n", p=128)
                for pc in range(6 * D // NPC):
                    sl = slots[si]
                    slv = sl[:, 0:KD * NPC].rearrange("p (kc n) -> p kc n", kc=KD)
                    load_slot(si, [(slv[:, :, :], mwv[:, :, pc * NPC:(pc + 1) * NPC])])
                    ps, pk = PS()
                    nchk = NPC // 128

                    def mm(e):
                        ins = None
                        for j in range(nchk):
                            for kc in range(KD):
                                ins = e.matmul(ps[:, j * NBC:(j + 1) * NBC], lhsT=slv[:, kc, j * 128:(j + 1) * 128],
                                               rhs=silc[:, kc, :], start=(kc == 0), stop=(kc == KD - 1))
                        return ins
                    S.op("pe", mm, reads=[("slot", si), "silc"], writes=[pk])
                    for j in range(nchk):
                        ch = pc * nchk + j
                        S.op("act", lambda e: e.activation(out=modT[:, l, ch, :], in_=ps[:, j * NBC:(j + 1) * NBC],
                                                           func=AF.Identity, bias=modb_s[:, l, ch:ch + 1], scale=1.0),
                             reads=[pk, "modb"], writes=[("modT", l, ch)])
                    si = (si + 1) % NSLOT
                for ni in range(2):
                    for bi in range(NBC):
                        base = (3 * ni + 1) * KD
                        S.op("dve", lambda e: e.scalar_tensor_tensor(out=gsc[:, l, ni, bi, :], in0=modT[:, l, base:base + KD, bi],
                                                                     scalar=1.0, in1=gains_s[:, l, ni, :], op0=ALU.add, op1=ALU.mult),
                             reads=[("modT", l, base + k) for k in range(KD)] + ["gains"], writes=[("gsc", l, ni, bi)])
            S.barrier()

        def rope(src, src_keys, dst, dst_keys, lt0, W, ra, rb, ri, inplace=False):
            a, bb = ra[0], rb[0]
            ak, bk = ("ra", 0), ("rb", 0)
            S.op("dve", lambda e: e.tensor_tensor(out=bb[0:64, :W], in0=src[64:128, :], in1=sin_t[64:128, lt0:lt0 + W], op=ALU.mult),
                 reads=src_keys + ["sin"], writes=[bk])
            S.op("dve", lambda e: e.tensor_tensor(out=bb[64:128, :W], in0=src[0:64, :], in1=sin_t[0:64, lt0:lt0 + W], op=ALU.mult),
                 reads=src_keys + ["sin"], writes=[bk])
            S.op("dve", lambda e: e.tensor_tensor(out=a[:, :W], in0=src, in1=cos_t[:, lt0:lt0 + W], op=ALU.mult),
                 reads=src_keys + ["cos"], writes=[ak])
            S.op("dve", lambda e: e.tensor_tensor(out=dst, in0=a[:, :W], in1=bb[:, :W], op=ALU.add),
                 reads=[ak, bk], writes=dst_keys)

        def proj_fm(wv, wkey, tt, ps, pk):
            t0, t1 = tiles[tt]
            W = t1 - t0

            def mm(e):
                ins = None
                for kc in range(KD):
                    ins = e.matmul(ps[:, :W], lhsT=wv[:, kc, :], rhs=hbuf[:, kc, t0:t1], start=(kc == 0), stop=(kc == KD - 1))
                return ins
            S.op("pe", mm, reads=[wkey] + HK(tt), writes=[pk])

        def norm_bufs(pb):
            return dict(sq=[pb(f"sq{i}", [128, KD, 512], BF16) for i in range(2)],
                        tmp=[pb(f"ntmp{i}", [128, 512], F32) for i in range(3)],
                        srt=[pb(f"srt{i}", [128, 512], F32) for i in range(2)],
                        rstd=[pb(f"rstd{i}", [128, 512], F32) for i in range(2)])

        def norm_body(l, ni, b, tlist, bufs):
            sq, tmp, srt, rstd = bufs["sq"], bufs["tmp"], bufs["srt"], bufs["rstd"]
            cnt = 0
            for tt in tlist:
                t0, t1 = tiles[tt]
                W = t1 - t0
                bi = NB if tt == 0 else b
                i2 = tt % 2
                sqb = sq[i2]
                S.op("act", lambda e: e.activation(out=sqb[:, :, :W], in_=X[:, :, t0:t1], func=AF.Square),
                     reads=XK(tt), writes=[("sq", i2)])
                ps, pk = PS()

                def mm(e):
                    ins = None
                    for kc in range(KD):
                        ins = e.matmul(ps[:, :W], lhsT=ones_m, rhs=sqb[:, kc, :W], start=(kc == 0), stop=(kc == KD - 1))
                    return ins
                S.op("pe", mm, reads=[("sq", i2), "mats"], writes=[pk])
                S.op("act", lambda e: e.activation(out=srt[i2][:, :W], in_=ps[:, :W], func=AF.Ln, scale=1.0 / D, bias=eps_t[:, 0:1]),
                     reads=[pk, "eps"], writes=[("srt", i2)])
                S.op("act", lambda e: e.activation(out=rstd[i2][:, :W], in_=srt[i2][:, :W], func=AF.Exp, scale=-0.5), reads=[("srt", i2)], writes=[("rstd", i2)])
                for kc in range(KD):
                    tb = tmp[cnt % 3]
                    tk = ("ntmp", cnt % 3)
                    cnt += 1
                    S.op("dve", lambda e: e.scalar_tensor_tensor(out=tb[:, :W], in0=X[:, kc, t0:t1], scalar=gsc[:, l, ni, bi, kc:kc + 1],
                                                                 in1=rstd[i2][:, :W], op0=ALU.mult, op1=ALU.mult),
                         reads=[("X", tt, kc), ("rstd", i2), ("gsc", l, ni, bi)], writes=[tk])
                    ch = 3 * ni * KD + kc
                    S.op("act", lambda e: e.activation(out=hbuf[:, kc, t0:t1], in_=tb[:, :W], func=AF.Identity,
                                                       bias=modT[:, l, ch, bi:bi + 1], scale=1.0),
                         reads=[tk, ("modT", l, ch)], writes=[("h", tt, kc)])

        def norm_phase(l, ni, b, tlist):
            with contextlib.ExitStack() as ph:
                pb = lambda name, shape, dt: ph.enter_context(nc.sbuf_tensor(name + uid(), list(shape), dt))
                bufs = norm_bufs(pb)
                S.barrier()
                norm_body(l, ni, b, tlist, bufs)

        def resid(ps, pk, l, gi, bi, m, tt):
            t0, t1 = tiles[tt]
            W = t1 - t0
            ch = gi * KD + m
            S.op("dve", lambda e: e.scalar_tensor_tensor(out=X[:, m, t0:t1], in0=ps[:, :W], scalar=modT[:, l, ch, bi:bi + 1],
                                                         in1=X[:, m, t0:t1], op0=ALU.mult, op1=ALU.add),
                 reads=[pk, ("modT", l, ch), ("X", tt, m)], writes=[("X", tt, m)])

        def ffn_phase(l, b, tlist, next_norm=None):
            nparts = FF // FP
            nj = FP // 128
            with contextlib.ExitStack() as ph:
                pb = lambda name, shape, dt: ph.enter_context(nc.sbuf_tensor(name + uid(), list(shape), dt))
                bufs = norm_bufs(pb)
                hid = [pb(f"hid{i}", [128, nj, 512], BF16) for i in range(2)]
                rl = [pb(f"rl{i}", [128, 512], BF16) for i in range(3)]
                S.barrier()
                norm_body(l, 1, b, tlist, bufs)
                w1v = ff_w1[l].rearrange("(kc p) n -> p kc n", p=128)
                w2v = ff_w2[l].rearrange("(c p) n -> p c n", p=128)
                items = [(part, tt) for part in range(nparts) for tt in tlist]
                wviews = {}
                rc = [0]

                def wv_of(part):
                    if part not in wviews:
                        sa, sb_ = (part % 2) * 2, (part % 2) * 2 + 1
                        w1s = slots[sa][:, 0:KD * FP].rearrange("p (kc n) -> p kc n", kc=KD)
                        w2s = slots[sb_][:, 0:nj * D].rearrange("p (c n) -> p c n", c=nj)
                        load_slot(sa, [(w1s[:, :, :], w1v[:, :, part * FP:(part + 1) * FP])])
                        load_slot(sb_, [(w2s[:, :, :], w2v[:, part * nj:(part + 1) * nj, :])])
                        wviews[part] = (sa, sb_, w1s, w2s)
                    return wviews[part]

                def F1(k):
                    part, tt = items[k]
                    sa, sb_, w1s, w2s = wv_of(part)
                    W = tiles[tt][1] - tiles[tt][0]
                    hb, hk = hid[k % 2], ("hid", k % 2)
                    for j in range(nj):
                        ps, pk = PS()
                        proj_fm(w1s[:, :, j * 128:(j + 1) * 128], ("slot", sa), tt, ps, pk)
                        rb_ = rl[rc[0] % 3]
                        rk = ("rl", rc[0] % 3)
                        rc[0] += 1
                        S.op("act", lambda e: e.activation(out=rb_[:, :W], in_=ps[:, :W], func=AF.Relu), reads=[pk], writes=[rk])
                        S.op("act", lambda e: e.activation(out=hb[:, j, :W], in_=rb_[:, :W], func=AF.Square), reads=[rk], writes=[hk + (j,)])

                def F2(k):
                    part, tt = items[k]
                    sa, sb_, w1s, w2s = wv_of(part)
                    W = tiles[tt][1] - tiles[tt][0]
                    bi = NB if tt == 0 else b
                    hb, hk = hid[k % 2], ("hid", k % 2)
                    for m in range(KD):
                        ps, pk = PS()

                        def mm(e):
                            ins = None
                            for j in range(nj):
                                ins = e.matmul(ps[:, :W], lhsT=w2s[:, j, m * 128:(m + 1) * 128], rhs=hb[:, j, :W],
                                               start=(j == 0), stop=(j == nj - 1))
                            return ins
                        S.op("pe", mm, reads=[("slot", sb_)] + [hk + (j,) for j in range(nj)], writes=[pk])
                        resid(ps, pk, l, 5, bi, m, tt)
                F1(0)
                for k in range(len(items)):
                    if k + 1 < len(items):
                        F1(k + 1)
                    F2(k)
                    if next_norm is not None and items[k][0] == nparts - 1 and items[k][1] in next_norm[1]:
                        norm_body(next_norm[0], 0, b, [items[k][1]], bufs)

        def cm_prefetch(l):
            i = l // 2
            NPC = min(c.SLOT // KD, 2 * WD)
            n_in_slots = (2 * WD) // NPC
            wiv = cm_w_in[i].rearrange("(kc p) n -> p kc n", p=128)
            wins = []
            for s_ in range(n_in_slots):
                v_ = slots[s_][:, 0:KD * NPC].rearrange("p (kc n) -> p kc n", kc=KD)
                load_slot(s_, [(v_[:, :, :], wiv[:, :, s_ * NPC:(s_ + 1) * NPC])])
                wins.append(v_)
            return wins

        def cm_phase(l, b, tlist, wins):
            i = l // 2
            NPC = min(c.SLOT // KD, 2 * WD)
            n_in_slots = (2 * WD) // NPC
            assert n_in_slots <= NSLOT
            cps = c.SLOT // D
            n_out_slots = (G + cps - 1) // cps
            with contextlib.ExitStack() as ph:
                pb = lambda name, shape, dt: ph.enter_context(nc.sbuf_tensor(name + uid(), list(shape), dt))
                xslots = [pb(f"xslot{k}", [128, c.SLOT], BF16) for k in range(n_out_slots)]
                xsem_ = xs_sems
                u_t = [pb(f"u_t{k}", [128, G, 512], BF16) for k in range(1)]
                vg = [pb(f"vg{k}", [128, WD], F32) for k in range(2)]
                vn = [pb(f"vn{k}", [128, WD], BF16) for k in range(2)]
                ss = [pb(f"cmss{k}", [128, 4], F32) for k in range(2)]
                vgain = pb("vgain", [128, WD], F32)
                wsT = pb("wsT", [128, G, 128], BF16)
                bsr = pb("bsr", [1, G * 128], BF16)
                S.barrier(("pe", "act", "dve", "pool"))

                def win_cols(c0, n):
                    s_ = c0 // NPC
                    o = c0 % NPC
                    assert o + n <= NPC
                    return wins[s_][:, :, o:o + n], ("slot", s_)
                wov = cm_w_out[i].rearrange("(c p) n -> p c n", p=128)
                wouts = []
                for k in range(n_out_slots):
                    nch_ = min(cps, G - k * cps)
                    v_ = xslots[k][:, 0:nch_ * D].rearrange("p (c n) -> p c n", c=nch_)
                    S.dma("pool", xsem_[k], lambda e: e.dma_start(out=v_[:, :, :], in_=wov[:, k * cps:k * cps + nch_, :]), writes=[("xslot", k)])
                    wouts.append(v_)
                S.dma("pool", csem["pool"], lambda e: e.dma_start(out=wsT[:, :, :], in_=cm_wsT[i]), writes=["wsT"])
                S.dma("pool", csem["pool"], lambda e: e.dma_start(out=bsr[:, :], in_=cm_bs[:, i * G * 128:(i + 1) * G * 128]), writes=["bsr"])
                S.dma("pool", csem["pool"], lambda e: e.dma_start(out=vgain[:, :], in_=cm_vg[:, i, :]), writes=["vgain"])
                vcs = [0]
                for ti, tt in enumerate(tlist):
                    t0, t1 = tiles[tt]
                    W = t1 - t0
                    bi = NB if tt == 0 else b
                    ub, uk = u_t[0], ("u_t", 0)
                    uvb = ub
                    for g in range(G):
                        ps, pk = PS()
                        wv, wk = win_cols(g * 128, 128)
                        proj_fm(wv, wk, tt, ps, pk)
                        S.op("act", lambda e: e.activation(out=ub[:, g, :W], in_=ps[:, :W], func=AF.Gelu_apprx_tanh), reads=[pk], writes=[uk + (g,)])
                    npc_v = min(512, WD)
                    npv = WD // npc_v

                    def Vproj(cc):
                        c0 = t0 + cc * 128
                        vi = vcs[0] % 2
                        vcs[0] += 1
                        vgb, vgk = vg[vi], ("vg", vi)
                        for pcv in range(npv):
                            ps, pk = PS()
                            wv, wk = win_cols(WD + pcv * npc_v, npc_v)

                            def mm(e):
                                ins = None
                                for kc in range(KD):
                                    ins = e.matmul(ps[:, :npc_v], lhsT=hbuf[:, kc, c0:c0 + 128], rhs=wv[:, kc, :], start=(kc == 0), stop=(kc == KD - 1))
                                return ins
                            S.op("pe", mm, reads=[wk] + HK(tt), writes=[pk])
                            S.op("act", lambda e: e.activation(out=vgb[:, pcv * npc_v:(pcv + 1) * npc_v], in_=ps[:, :npc_v], func=AF.Gelu_apprx_tanh),
                                 reads=[pk], writes=[vgk + (pcv,)])
                        return vi

                    def Vrest(cc, vi):
                        vgb, vgk = vg[vi], ("vg", vi)
                        vnb, vnk = vn[vi], ("vn", vi)
                        ssb, ssk = ss[vi], ("cmss", vi)
                        allv = [vgk + (p_,) for p_ in range(npv)]
                        S.op("act", lambda e: e.activation(out=vnb[:, :], in_=vgb[:, :], func=AF.Square, accum_out=ssb[:, 0:1]),
                             reads=allv, writes=[ssk, vnk])
                        S.op("act", lambda e: e.activation(out=ssb[:, 1:2], in_=ssb[:, 0:1], func=AF.Sqrt, scale=1.0 / WD, bias=eps_t[:, 0:1]),
                             reads=[ssk, "eps"], writes=[ssk + (1,)])
                        S.op("dve", lambda e: e.reciprocal(out=ssb[:, 2:3], in_=ssb[:, 1:2]), reads=[ssk + (1,)], writes=[ssk + (2,)])
                        S.op("dve", lambda e: e.scalar_tensor_tensor(out=vnb[:, :], in0=vgb[:, :], scalar=ssb[:, 2:3], in1=vgain[:, :],
                                                                     op0=ALU.mult, op1=ALU.mult),
                             reads=allv + [ssk + (2,), "vgain"], writes=[vnk])
                        for g0 in range(0, G, 4):
                            ng = min(4, G - g0)
                            ps, pk = PS()

                            def mm(e):
                                ins = None
                                for gg in range(ng):
                                    g = g0 + gg
                                    e.matmul(ps[:, gg * 128:(gg + 1) * 128], lhsT=vnb[:, g * 128:(g + 1) * 128], rhs=wsT[:, g, :], start=True, stop=False)
                                    ins = e.matmul(ps[:, gg * 128:(gg + 1) * 128], lhsT=ones_m[0:1, :], rhs=bsr[0:1, g * 128:(g + 1) * 128], start=False, stop=True)
                                return ins
                            S.op("pe", mm, reads=[vnk, "wsT", "bsr", "mats"], writes=[pk])
                            S.op("dve", lambda e: e.tensor_tensor(out=uvb[:, g0:g0 + ng, cc * 128:(cc + 1) * 128],
                                                                  in0=ps[:, 0:ng * 128].rearrange("p (g n) -> p g n", g=ng),
                                                                  in1=ub[:, g0:g0 + ng, cc * 128:(cc + 1) * 128], op=ALU.mult),
                                 reads=[pk] + [uk + (g0 + gg,) for gg in range(ng)], writes=[uk + (g0 + gg,) for gg in range(ng)])
                    nchk_t = W // 128
                    vi_next = Vproj(0)
                    for cc in range(nchk_t):
                        vi_cur = vi_next
                        if cc + 1 < nchk_t:
                            vi_next = Vproj(cc + 1)
                        Vrest(cc, vi_cur)
                    uv_keys = [uk + (g,) for g in range(G)]
                    for m in range(KD):
                        ps, pk = PS()

                        def mm(e):
                            ins = None
                            for g in range(G):
                                wv_ = wouts[g // cps]
                                ins = e.matmul(ps[:, :W], lhsT=wv_[:, g % cps, m * 128:(m + 1) * 128], rhs=uvb[:, g, :W], start=(g == 0), stop=(g == G - 1))
                            return ins
                        S.op("pe", mm, reads=[("xslot", k) for k in range(n_out_slots)] + uv_keys, writes=[pk])
                        resid(ps, pk, l, 2, bi, m, tt)

        def ab_phase(l, b):
            i = l // 2
            winv = ab_w_in[i].rearrange("(kc p) n -> p kc n", p=128)
            woutv = ab_w_out[i].rearrange("(c p) n -> p c n", p=128)
            AX = mybir.AxisListType.X
            with contextlib.ExitStack() as ph:
                pb = lambda name, shape, dt: ph.enter_context(nc.sbuf_tensor(name + uid(), list(shape), dt))
                carve1 = lambda k: slots[1][:, 2048 + k * 512: 2048 + (k + 1) * 512]
                carve3 = lambda k: slots[3][:, 2048 + k * 512: 2048 + (k + 1) * 512]
                mixout = pb("mixout", [128, 2, T], BF16)
                kT = pb("kT", [128, T], BF16)
                v_tok = pb("v_tok", [128, NCH, 128], BF16)
                Sf_bf = pb("Sf_bf", [128, NCH, 128], BF16)
                Sb_bf = pb("Sb_bf", [128, NCH, 128], BF16)
                qh = [pb("qh0", [128, 512], BF16), carve1(0)]
                qf = [pb("qf0", [128, 512], BF16), carve1(1)]
                qb = [pb("qb0", [128, 512], BF16), carve1(2)]
                gt = [pb("gt0", [128, 512], BF16), carve1(3)]
                sqn = [pb("sqn0", [128, 512], BF16), carve3(0)]
                pT = [pb(f"pT{k}", [128, 512], BF16) for k in range(2)] + [carve3(1), carve3(2), carve3(3)]
                NPT = len(pT)
                ra = [pb("ra0", [128, 512], F32)]
                rb = [pb("rb0", [128, 512], F32)]
                sdv = [pb("sdv0", [128, 512], F32)]
                sdT = [pb(f"sdT{k}", [128, 128], BF16) for k in range(4)]
                NKT = 4
                ktok = [pb(f"ktok{k}", [128, 512], BF16) for k in range(NKT)]
                Sst = [pb(f"Sst{k}", [128, 2, 128], F32) for k in range(2)]
                Mh = pb("Mh", [128, 128], BF16)
                mtmp = pb("mtmp", [128, 2, 128], F32)
                qdf = pb("qdf", [128, 512], BF16)
                qdb = pb("qdb", [128, 512], BF16)
                dsc = pb("dsc", [128, 8], F32)
                wmean = pb("wmean", [128, KD], F32)
                S.barrier()
                CK = ("carve",)
                cnt = {"pT": 0, "sdT": 0, "kt": 0}
                inv_sqrt_dk = 1.0 / math.sqrt(128.0)
                LA = 2

                def out_proj(pair, slot_i):
                    wo = slots[slot_i][:, 0:2 * D].rearrange("p (c n) -> p c n", c=2)
                    load_slot(slot_i, [(wo[:, :, :], woutv[:, 2 * pair:2 * pair + 2, :])])
                    for tt in range(NT):
                        t0, t1 = tiles[tt]
                        W = t1 - t0
                        bi = NB if tt == 0 else b
                        for m in range(KD):
                            ps, pk = PS()

                            def mm(e):
                                e.matmul(ps[:, :W], lhsT=wo[:, 0, m * 128:(m + 1) * 128], rhs=mixout[:, 0, t0:t1], start=True, stop=False)
                                return e.matmul(ps[:, :W], lhsT=wo[:, 1, m * 128:(m + 1) * 128], rhs=mixout[:, 1, t0:t1], start=False, stop=True)
                            S.op("pe", mm, reads=[("slot", slot_i), ("mix", 0, tt), ("mix", 1, tt)], writes=[pk])
                            resid(ps, pk, l, 2, bi, m, tt)

                def v_proj(wv, wk):
                    for c0 in range(0, NCH, 4):
                        n = min(4, NCH - c0)
                        ps, pk = PS()

                        def mm(e):
                            ins = None
                            for cc in range(n):
                                ch = c0 + cc
                                for kc in range(KD):
                                    ins = e.matmul(ps[:, cc * 128:(cc + 1) * 128], lhsT=hbuf[:, kc, ch * 128:(ch + 1) * 128], rhs=wv[:, kc, :],
                                                   start=(kc == 0), stop=(kc == KD - 1))
                            return ins
                        tts = sorted(set(tile_of_chunk(c0 + cc) for cc in range(n)))
                        S.op("pe", mm, reads=[wk] + [k for tt in tts for k in HK(tt)], writes=[pk])
                        S.op("act", lambda e: e.activation(out=v_tok[:, c0:c0 + n, :], in_=ps[:, 0:n * 128].rearrange("p (c n) -> p c n", c=n), func=AF.Copy),
                             reads=[pk], writes=[("v_tok", c0)])

                def vkey(ch):
                    return ("v_tok", (ch // 4) * 4)

                for hh in range(RH):
                    si = (hh % 2) * 2
                    wsl = slots[si][:, 0:KD * 512].rearrange("p (kc f n) -> p kc f n", kc=KD, f=4)
                    load_slot(si, [(wsl[:, :, f, :], winv[:, :, f * RH * 128 + hh * 128: f * RH * 128 + (hh + 1) * 128]) for f in range(4)])
                    wk = ("slot", si)
                    S.op("dve", lambda e: e.reduce_sum(out=wmean[:, :], in_=wsl[:, :, 2, :], axis=AX), reads=[wk], writes=["wmean"])
                    S.op("dve", lambda e: e.tensor_scalar(out=wmean[:, :], in0=wmean[:, :], scalar1=-1.0 / 128, scalar2=None, op0=ALU.mult),
                         reads=["wmean"], writes=["wmean"])
                    for kc in range(KD):
                        S.op("dve", lambda e: e.tensor_scalar(out=wsl[:, kc, 2, :], in0=wsl[:, kc, 2, :], scalar1=wmean[:, kc:kc + 1], scalar2=None, op0=ALU.add),
                             reads=[wk, "wmean"], writes=[("wvc", kc)])
                    wvk = [("wvc", kc) for kc in range(KD)]
                    cf = (i * 2 + 0) * RH + hh
                    cb = (i * 2 + 1) * RH + hh
                    lgf, lgb = lg[:, cf:cf + 1], lg[:, cb:cb + 1]
                    S.op("act", lambda e: e.activation(out=mtmp[:, 0, :], in_=relf, func=AF.Exp, scale=lgf), reads=["dcs", "lg"], writes=["mtmp0"])
                    S.op("act", lambda e: e.activation(out=mtmp[:, 1, :], in_=relb, func=AF.Exp, scale=lgb), reads=["dcs", "lg"], writes=["mtmp1"])
                    S.op("dve", lambda e: e.tensor_tensor(out=mtmp[:, 0, :], in0=mtmp[:, 0, :], in1=maskf, op=ALU.mult), reads=["mtmp0", "dcs"], writes=["mtmp0"])
                    S.op("dve", lambda e: e.tensor_tensor(out=mtmp[:, 1, :], in0=mtmp[:, 1, :], in1=maskb, op=ALU.mult), reads=["mtmp1", "dcs"], writes=["mtmp1"])
                    S.op("dve", lambda e: e.tensor_tensor(out=mtmp[:, 0, :], in0=mtmp[:, 0, :], in1=mtmp[:, 1, :], op=ALU.add),
                         reads=["mtmp0", "mtmp1"], writes=["mtmp0"])
                    S.op("act", lambda e: e.mul(Mh[:, :], mtmp[:, 0, :], inv_sqrt_dk), reads=["mtmp0"], writes=["Mh"])
                    for k4 in range(4):
                        S.op("act", lambda e: e.activation(out=qdf[:, k4 * 128:(k4 + 1) * 128], in_=posq1, func=AF.Exp, scale=lgf), reads=["dcs", "lg"], writes=[("qdf", k4)])
                        S.op("act", lambda e: e.activation(out=qdb[:, k4 * 128:(k4 + 1) * 128], in_=posq2, func=AF.Exp, scale=lgb), reads=["dcs", "lg"], writes=[("qdb", k4)])
                    S.op("act", lambda e: e.activation(out=dsc[:, 0:1], in_=posr, func=AF.Exp, scale=lgf), reads=["dcs", "lg"], writes=[("dsc", 0)])
                    S.op("act", lambda e: e.activation(out=dsc[:, 1:2], in_=posj, func=AF.Exp, scale=lgb), reads=["dcs", "lg"], writes=[("dsc", 1)])
                    S.op("act", lambda e: e.activation(out=dsc[:, 2:3], in_=lgf, func=AF.Exp, scale=128.0), reads=["lg"], writes=[("dsc", 2)])
                    S.op("act", lambda e: e.activation(out=dsc[:, 3:4], in_=lgb, func=AF.Exp, scale=128.0), reads=["lg"], writes=[("dsc", 3)])
                    S.op("dve", lambda e: e.tensor_scalar(out=dsc[:, 4:6], in0=dsc[:, 0:2], scalar1=inv_sqrt_dk, scalar2=None, op0=ALU.mult),
                         reads=[("dsc", 0), ("dsc", 1)], writes=[("dsc", 4)])
                    qdkeys = [("qdf", k4) for k4 in range(4)] + [("qdb", k4) for k4 in range(4)]
                    for tt in range(NT):
                        t0, t1 = tiles[tt]
                        W = t1 - t0
                        ps, pk = PS()
                        proj_fm(wsl[:, :, 1, :], wk, tt, ps, pk)
                        if tt == 0:
                            S.op("act", lambda e: e.activation(out=kT[:, t0:t1], in_=ps[:, :W], func=AF.Copy), reads=[pk], writes=[("kT", tt)])
                        else:
                            rope(ps[:, :W], [pk], kT[:, t0:t1], [("kT", tt)], t0 - CTX, W, ra, rb, 0)
                    for c0 in range(0, NCH, 4):
                        n = min(4, NCH - c0)
                        ps, pk = PS()

                        def mmv(e):
                            ins = None
                            for cc in range(n):
                                ch = c0 + cc
                                for kc in range(KD):
                                    ins = e.matmul(ps[:, cc * 128:(cc + 1) * 128], lhsT=hbuf[:, kc, ch * 128:(ch + 1) * 128], rhs=wsl[:, kc, 2, :],
                                                   start=(kc == 0), stop=(kc == KD - 1))
                            return ins
                        tts = sorted(set(tile_of_chunk(c0 + cc) for cc in range(n)))
                        S.op("pe", mmv, reads=[wk] + wvk + [k for tt in tts for k in HK(tt)], writes=[pk])
                        S.op("act", lambda e: e.activation(out=v_tok[:, c0:c0 + n, :], in_=ps[:, 0:n * 128].rearrange("p (c n) -> p c n", c=n), func=AF.Copy),
                             reads=[pk], writes=[("v_tok", c0)])
                    nctx = CTX // 128
                    orders = [list(range(NCH)), list(range(nctx - 1, -1, -1)) + list(range(NCH - 1, nctx - 1, -1))]
                    Sdsts = [Sf_bf, Sb_bf]
                    for d_ in range(2):
                        S.op("dve", lambda e: e.memset(Sst[0][:, d_, :], 0.0), writes=[("Sst", 0, d_)])
                    nsteps = NCH - 1
                    nbatch = (nsteps + 3) // 4

                    def emit_batch(d_, bidx):
                        chs = orders[d_][bidx * 4: min(bidx * 4 + 4, nsteps)]
                        n = len(chs)
                        bank = cnt["kt"] % 2
                        kbuf = cnt["kt"] % NKT
                        cnt["kt"] += 1
                        pb_, tk = ptrs[bank], ("ptr", bank)

                        def tr(e):
                            ins = None
                            for k_, ch in enumerate(chs):
                                ins = e.transpose(pb_[:, k_ * 128:(k_ + 1) * 128], kT[:, ch * 128:(ch + 1) * 128], ident_m)
                            return ins
                        S.op("pe", tr, reads=sorted(set(("kT", tile_of_chunk(ch)) for ch in chs)) + ["mats"], writes=[tk])
                        S.op("dve", lambda e: e.tensor_scalar(out=ktok[kbuf][:, 0:n * 128], in0=pb_[:, 0:n * 128], scalar1=dsc[:, 4 + d_:5 + d_],
                                                              scalar2=None, op0=ALU.mult),
                             reads=[tk, ("dsc", 4)], writes=[("ktok", kbuf)])
                        return kbuf
                    kb_cur = [None, None]
                    kb_nxt = [emit_batch(0, 0), emit_batch(1, 0)]
                    for oi in range(nsteps + 1):
                        for d_ in range(2):
                            order = orders[d_]
                            if oi % 4 == 0 and oi < nsteps:
                                kb_cur[d_] = kb_nxt[d_]
                                if oi // 4 + 1 < nbatch:
                                    kb_nxt[d_] = emit_batch(d_, oi // 4 + 1)
                            ch = order[oi]
                            cur, nxt = Sst[oi % 2], Sst[(oi + 1) % 2]
                            ck, nk = ("Sst", oi % 2, d_), ("Sst", (oi + 1) % 2, d_)
                            S.op("act", lambda e: e.activation(out=Sdsts[d_][:, ch, :], in_=cur[:, d_, :], func=AF.Copy), reads=[ck], writes=[("Sbf", d_, ch)])
                            if oi == nsteps:
                                continue
                            kbuf = kb_cur[d_]
                            k_ = oi % 4
                            ps, pk = PS()
                            S.op("pe", lambda e: e.matmul(ps[:, 0:128], lhsT=ktok[kbuf][:, k_ * 128:(k_ + 1) * 128], rhs=v_tok[:, ch, :], start=True, stop=True),
                                 reads=[("ktok", kbuf), vkey(ch)], writes=[pk])
                            S.op("dve", lambda e: e.scalar_tensor_tensor(out=nxt[:, d_, :], in0=cur[:, d_, :], scalar=dsc[:, 2 + d_:3 + d_], in1=ps[:, 0:128],
                                                                         op0=ALU.mult, op1=ALU.add),
                                 reads=[ck, pk, ("dsc", 2 + d_)], writes=[nk])
                    mslot = hh % 2

                    def Qprep(tt):
                        bi_ = tt % 2
                        t0, t1 = tiles[tt]
                        W = t1 - t0
                        ps, pk = PS()
                        proj_fm(wsl[:, :, 0, :], wk, tt, ps, pk)
                        if tt == 0:
                            S.op("act", lambda e: e.activation(out=qh[bi_][:, :W], in_=ps[:, :W], func=AF.Copy), reads=[pk, CK], writes=[("qh", bi_)])
                        else:
                            rope(ps[:, :W], [pk, CK], qh[bi_][:, :W], [("qh", bi_)], t0 - CTX, W, ra, rb, 0)
                        S.op("dve", lambda e: e.tensor_tensor(out=qf[bi_][:, :W], in0=qh[bi_][:, :W], in1=qdf[:, :W], op=ALU.mult),
                             reads=[("qh", bi_), CK] + qdkeys, writes=[("qf", bi_)])
                        S.op("dve", lambda e: e.tensor_tensor(out=qb[bi_][:, :W], in0=qh[bi_][:, :W], in1=qdb[:, :W], op=ALU.mult),
                             reads=[("qh", bi_), CK] + qdkeys, writes=[("qb", bi_)])

                    def Mpart(tt):
                        bi_ = tt % 2
                        t0, t1 = tiles[tt]
                        W = t1 - t0
                        ps, pk = PS()
                        proj_fm(wsl[:, :, 3, :], wk, tt, ps, pk)
                        S.op("act", lambda e: e.activation(out=sdv[0][:, :W], in_=ps[:, :W], func=AF.Exp, scale=-1.0), reads=[pk], writes=[("sdv", 0)])
                        S.op("act", lambda e: e.activation(out=sdv[0][:, :W], in_=sdv[0][:, :W], func=AF.Ln, bias=1.0, scale=1.0), reads=[("sdv", 0)], writes=[("sdv", 0)])
                        S.op("act", lambda e: e.activation(out=sdv[0][:, :W], in_=sdv[0][:, :W], func=AF.Exp, scale=-1.0), reads=[("sdv", 0)], writes=[("sdv", 0)])
                        S.op("dve", lambda e: e.tensor_tensor(out=gt[bi_][:, :W], in0=ps[:, :W], in1=sdv[0][:, :W], op=ALU.mult),
                             reads=[pk, ("sdv", 0), CK], writes=[("gt", bi_)])
                        ops_, opk = PS()
                        ps_live.add(opk)
                        nch_t = W // 128
                        pend = []

                        def score(cc):
                            ch = t0 // 128 + cc
                            cs = slice(cc * 128, (cc + 1) * 128)
                            ps2, pk2 = PS()
                            S.op("pe", lambda e: e.matmul(ps2[:, 0:128], lhsT=kT[:, ch * 128:(ch + 1) * 128], rhs=qh[bi_][:, cs], start=True, stop=True),
                                 reads=[("kT", tt), ("qh", bi_), CK], writes=[pk2])
                            sdi = cnt["sdT"] % 4
                            cnt["sdT"] += 1
                            S.op("dve", lambda e: e.tensor_tensor(out=sdT[sdi][:, :], in0=ps2[:, 0:128], in1=Mh[:, :], op=ALU.mult),
                                 reads=[pk2, "Mh"], writes=[("sdT", sdi)])
                            return (cc, ch, cs, sdi)

                        def accum(item):
                            cc, ch, cs, sdi = item

                            def mm(e):
                                e.matmul(ops_[:, cs], lhsT=v_tok[:, ch, :], rhs=sdT[sdi][:, :], start=True, stop=False)
                                e.matmul(ops_[:, cs], lhsT=Sf_bf[:, ch, :], rhs=qf[bi_][:, cs], start=False, stop=False)
                                return e.matmul(ops_[:, cs], lhsT=Sb_bf[:, ch, :], rhs=qb[bi_][:, cs], start=False, stop=True)
                            S.op("pe", mm, reads=[vkey(ch), ("sdT", sdi), ("Sbf", 0, ch), ("Sbf", 1, ch), ("qf", bi_), ("qb", bi_), CK], writes=[opk])
                        for cc in range(nch_t):
                            pend.append(score(cc))
                            if len(pend) > LA:
                                accum(pend.pop(0))
                        while pend:
                            accum(pend.pop(0))
                        S.op("act", lambda e: e.activation(out=sqn[bi_][:, :W], in_=ops_[:, :W], func=AF.Square), reads=[opk, CK], writes=[("sqn", bi_)])
                        return ops_, opk

                    def Npart(tt, ops_, opk):
                        bi_ = tt % 2
                        t0, t1 = tiles[tt]
                        W = t1 - t0
                        pv_, pvk = PS()
                        S.op("pe", lambda e: e.matmul(pv_[:, :W], lhsT=avg_m, rhs=sqn[bi_][:, :W], start=True, stop=True), reads=[("sqn", bi_), CK, "mats"], writes=[pvk])
                        S.op("act", lambda e: e.activation(out=sdv[0][:, :W], in_=pv_[:, :W], func=AF.Ln, bias=eps_t[:, 0:1], scale=1.0),
                             reads=[pvk, "eps"], writes=[("sdv", 0)])
                        S.op("act", lambda e: e.activation(out=sdv[0][:, :W], in_=sdv[0][:, :W], func=AF.Exp, scale=-0.5), reads=[("sdv", 0)], writes=[("sdv", 0)])
                        S.op("dve", lambda e: e.tensor_tensor(out=ra[0][:, :W], in0=ops_[:, :W], in1=sdv[0][:, :W], op=ALU.mult),
                             reads=[opk, ("sdv", 0)], writes=[("ra", 0)])
                        S.op("dve", lambda e: e.tensor_tensor(out=mixout[:, mslot, t0:t1], in0=ra[0][:, :W], in1=gt[bi_][:, :W], op=ALU.mult),
                             reads=[("ra", 0), ("gt", bi_), CK], writes=[("mix", mslot, tt)])

                    Qprep(0)
                    prev = None
                    for tt in range(NT):
                        if tt + 1 < NT:
                            Qprep(tt + 1)
                        ops_, opk = Mpart(tt)
                        if prev is not None:
                            Npart(prev[0], prev[1], prev[2])
                            ps_live.discard(prev[2])
                        prev = (tt, ops_, opk)
                    Npart(prev[0], prev[1], prev[2])
                    ps_live.discard(prev[2])
                    if hh % 2 == 1:
                        out_proj(hh // 2, (hh // 2 % 2) * 2 + 1)

                qoff = 4 * RH * 128
                koff = qoff + AH * 128
                voff = koff + AKV * 128

                def qk_norm_a(ps, pk, W, bi_):
                    S.op("act", lambda e: e.activation(out=sqn[bi_][:, :W], in_=ps[:, :W], func=AF.Square), reads=[pk, CK], writes=[("sqn", bi_)])

                def qk_norm_b(ps, pk, W, bi_, gcol, dst, dst_keys, tt):
                    p2, p2k = PS()
                    S.op("pe", lambda e: e.matmul(p2[:, :W], lhsT=avg_m, rhs=sqn[bi_][:, :W], start=True, stop=True), reads=[("sqn", bi_), CK, "mats"], writes=[p2k])
                    S.op("act", lambda e: e.activation(out=sdv[0][:, :W], in_=p2[:, :W], func=AF.Ln, bias=eps_t[:, 0:1], scale=1.0),
                         reads=[p2k, "eps"], writes=[("sdv", 0)])
                    S.op("act", lambda e: e.activation(out=sdv[0][:, :W], in_=sdv[0][:, :W], func=AF.Exp, scale=-0.5), reads=[("sdv", 0)], writes=[("sdv", 0)])
                    S.op("dve", lambda e: e.scalar_tensor_tensor(out=ra[0][:, :W], in0=ps[:, :W], scalar=gcol, in1=sdv[0][:, :W], op0=ALU.mult, op1=ALU.mult),
                         reads=[pk, ("sdv", 0), "qkg"], writes=[("ra", 0)])
                    t0, t1 = tiles[tt]
                    if tt == 0:
                        S.op("act", lambda e: e.activation(out=dst, in_=ra[0][:, :W], func=AF.Copy), reads=[("ra", 0), CK], writes=dst_keys)
                    else:
                        rope(ra[0][:, :W], [("ra", 0), CK], dst, dst_keys, t0 - CTX, W, ra, rb, 0, inplace=True)

                for gi in range(AKV):
                    si = (gi % 2) * 2
                    nf = GQ + 2
                    wsl = slots[si][:, 0:KD * nf * 128].rearrange("p (kc f n) -> p kc f n", kc=KD, f=nf)
                    pieces = [(wsl[:, :, j, :], winv[:, :, qoff + (gi * GQ + j) * 128: qoff + (gi * GQ + j + 1) * 128]) for j in range(GQ)]
                    pieces.append((wsl[:, :, GQ, :], winv[:, :, koff + gi * 128: koff + (gi + 1) * 128]))
                    pieces.append((wsl[:, :, GQ + 1, :], winv[:, :, voff + gi * 128: voff + (gi + 1) * 128]))
                    load_slot(si, pieces)
                    wk = ("slot", si)
                    for tt in range(NT):
                        t0, t1 = tiles[tt]
                        W = t1 - t0
                        ps, pk = PS()
                        proj_fm(wsl[:, :, GQ, :], wk, tt, ps, pk)
                        qk_norm_a(ps, pk, W, 0)
                        qk_norm_b(ps, pk, W, 0, qkg_s[:, i, 1:2], kT[:, t0:t1], [("kT", tt)], tt)
                    v_proj(wsl[:, :, GQ + 1, :], wk)
                    its = [(j, tt) for j in range(GQ) for tt in range(NT)]

                    def prepA(n):
                        j, tt = its[n]
                        W = tiles[tt][1] - tiles[tt][0]
                        ps, pk = PS()
                        proj_fm(wsl[:, :, j, :], wk, tt, ps, pk)
                        qk_norm_a(ps, pk, W, n % 2)
                        ps_live.add(pk)
                        return ps, pk

                    def prepB(n, ps, pk):
                        j, tt = its[n]
                        W = tiles[tt][1] - tiles[tt][0]
                        qk_norm_b(ps, pk, W, n % 2, qkg_s[:, i, 0:1], qh[n % 2][:, :W], [("qh", n % 2)], tt)
                        ps_live.discard(pk)
                    pq = prepA(0)
                    prepB(0, *pq)
                    for n, (j, tt) in enumerate(its):
                        bi_ = n % 2
                        mslot = j % 2
                        t0, t1 = tiles[tt]
                        W = t1 - t0
                        kchunks = list(range(CTX // 128)) if tt == 0 else list(range(NCH))
                        ops_, opk = PS()
                        ps_live.add(opk)
                        dps_, dpk = PS()
                        ps_live.add(dpk)
                        nxt_pq = prepA(n + 1) if n + 1 < len(its) else None
                        pend = []

                        def qk(n_, ch):
                            ps, pk = PS()
                            S.op("pe", lambda e: e.matmul(ps[:, :W], lhsT=kT[:, ch * 128:(ch + 1) * 128], rhs=qh[bi_][:, :W], start=True, stop=True),
                                 reads=[("kT", tile_of_chunk(ch)), ("qh", bi_), CK], writes=[pk])
                            pi = cnt["pT"] % NPT
                            cnt["pT"] += 1
                            S.op("act", lambda e: e.activation(out=pT[pi][:, :W], in_=ps[:, :W], func=AF.Exp, scale=inv_sqrt_dk, bias=nshift_t[:, 0:1]),
                                 reads=[pk, "nshift", CK], writes=[("pT", pi)])
                            return (n_, ch, pi)

                        def pv(item):
                            n_, ch, pi = item
                            first, last_ = (n_ == 0), (n_ == len(kchunks) - 1)

                            def mm(e):
                                e.matmul(ops_[:, :W], lhsT=v_tok[:, ch, :], rhs=pT[pi][:, :W], start=first, stop=last_)
                                return e.matmul(dps_[:, :W], lhsT=ones_m, rhs=pT[pi][:, :W], start=first, stop=last_)
                            S.op("pe", mm, reads=[vkey(ch), ("pT", pi), CK, "mats"], writes=[opk, dpk])
                        for n_, ch in enumerate(kchunks):
                            pend.append(qk(n_, ch))
                            if len(pend) > LA + 1:
                                pv(pend.pop(0))
                            if n_ == min(3, len(kchunks) - 1) and nxt_pq is not None:
                                prepB(n + 1, *nxt_pq)
                                nxt_pq = None
                        while pend:
                            pv(pend.pop(0))
                        S.op("dve", lambda e: e.reciprocal(out=sdv[0][:, :W], in_=dps_[:, :W]), reads=[dpk], writes=[("sdv", 0)])
                        S.op("dve", lambda e: e.tensor_tensor(out=mixout[:, mslot, t0:t1], in0=ops_[:, :W], in1=sdv[0][:, :W], op=ALU.mult),
                             reads=[opk, ("sdv", 0)], writes=[("mix", mslot, tt)])
                        ps_live.discard(opk)
                        ps_live.discard(dpk)
                        if tt == NT - 1 and j % 2 == 1:
                            pair = (RH + gi * GQ + j) // 2
                            out_proj(pair, (pair % 2) * 2 + 1)
                S.barrier()
                S.op("dve", lambda e: e.memset(dsc[:, 7:8], 0.0), writes=[("slot", 1), ("slot", 3), ("dsc", 7)])

        ytoks = []

        def load_x_tile(b, tt):
            t0, t1 = tiles[tt]
            S.dma("sp", xsem, lambda e: [e.dma_start(out=X[:, kc, t0:t1], in_=xin[b, kc * 128:(kc + 1) * 128, t0:t1]) for kc in range(KD)], writes=XK(tt))

        allX = [k for tt in range(NT) for k in XK(tt)]
        S.dma("sp", xsem, lambda e: [e.dma_start(out=X[:, kc, :], in_=xin[0, kc * 128:(kc + 1) * 128, :]) for kc in range(KD)], writes=allX)

        def tl_of(l):
            last = (l == DEPTH - 1)
            even = (l % 2 == 0)
            tl_mix = list(range(1, NT)) if (last and not even) else list(range(NT))
            tl_ffn = list(range(1, NT)) if last else list(range(NT))
            return tl_mix, tl_ffn

        for b in range(NB):
            normed = False
            for li, l in enumerate(layers):
                even = (l % 2 == 0)
                tl_mix, tl_ffn = tl_of(l)
                wins = None
                if not even:
                    wins = cm_prefetch(l)
                if not normed:
                    norm_phase(l, 0, b, tl_mix)
                if even:
                    ab_phase(l, b)
                else:
                    cm_phase(l, b, tl_mix, wins)
                nxt = None
                normed = False
                if li + 1 < len(layers):
                    l2 = layers[li + 1]
                    tl_mix2, _ = tl_of(l2)
                    if all(t in tl_ffn for t in tl_mix2):
                        nxt = (l2, tl_mix2)
                        normed = True
                ffn_phase(l, b, tl_ffn, nxt)
            for tt in range(1, NT):
                t0, t1 = tiles[tt]
                ytoks.append(S.dma("sp", ysem, lambda e: [e.dma_start(out=y[b, kc * 128:(kc + 1) * 128, t0 - CTX:t1 - CTX], in_=X[:, kc, t0:t1]) for kc in range(KD)],
                                   reads=XK(tt)))
            if b + 1 < NB:
                for tt in range(NT):
                    load_x_tile(b + 1, tt)
        ytok = ytoks[-1]
        S._wait("sp", [ytok])
        S.barrier(("pe", "act", "dve", "pool", "sp"))
        nc._sched_stats = (S.n_ops, S.n_wait)
    return nc


def rope_tables(L):
    rows = L // GRID_W
    row = np.repeat(np.arange(rows, dtype=np.float32), GRID_W)
    col = np.tile(np.arange(GRID_W, dtype=np.float32), rows)
    n_freq = 32
    inv = (ROPE_BASE ** (-np.arange(n_freq, dtype=np.float32) / n_freq)).astype(np.float32)
    ang = np.concatenate([row[:, None] * inv[None, :], col[:, None] * inv[None, :]], axis=-1)
    cos, sin = np.cos(ang).astype(np.float32), np.sin(ang).astype(np.float32)
    out = np.zeros((128, 2, L), np.float32)
    out[0:64, 0] = cos.T
    out[64:128, 0] = cos.T
    out[0:64, 1] = sin.T
    out[64:128, 1] = -sin.T
    return out


def const_tables():
    j = np.arange(128, dtype=np.float32)[:, None]
    i = np.arange(128, dtype=np.float32)[None, :]
    d = np.zeros((128, 6 * 128 + 2), np.float32)
    d[:, 0:128] = np.maximum(i - j, 0)
    d[:, 128:256] = (i >= j)
    d[:, 256:384] = np.maximum(j - i, 0)
    d[:, 384:512] = (j >= i)
    d[:, 512:640] = np.broadcast_to(i + 1, (128, 128))
    d[:, 640:768] = np.broadcast_to(128 - i, (128, 128))
    d[:, 768] = 127 - j[:, 0]
    d[:, 769] = j[:, 0]
    m = np.zeros((128, 4, 128), np.float32)
    m[:, 0] = 1.0
    m[:, 1] = np.eye(128)
    m[:, 2] = np.eye(128) - 1.0 / 128
    m[:, 3] = 1.0 / 128
    return d, m


def prep_inputs(cfg, inp):
    c = cfg
    f = lambda a: np.ascontiguousarray(np.asarray(a, dtype=np.float32))
    KD, D, DEPTH = c.KD, c.D, c.DEPTH
    common = {}
    common["mod_w"] = f(inp["mod_w"])
    common["mod_b"] = f(np.asarray(inp["mod_b"]).reshape(DEPTH, 6 * KD, 128).transpose(2, 0, 1))
    g1 = np.asarray(inp["norm1_g"]).reshape(DEPTH, KD, 128)
    g2 = np.asarray(inp["norm2_g"]).reshape(DEPTH, KD, 128)
    common["gains"] = f(np.stack([g1, g2], axis=1).transpose(3, 0, 1, 2))
    common["ab_w_in"] = f(inp["ab_w_in"])
    common["ab_w_out"] = f(inp["ab_w_out"])
    rd = np.asarray(inp["ret_decay"]).reshape(1, -1)
    common["ret_dec"] = f(np.broadcast_to(rd, (128, rd.shape[1])))
    common["qk_g"] = f(np.stack([np.asarray(inp["att_q_norm_g"]), np.asarray(inp["att_k_norm_g"])], axis=-1).transpose(1, 0, 2))
    common["cm_w_in"] = f(inp["cm_w_in"])
    vg = np.asarray(inp["cm_v_norm_g"])
    common["cm_vg"] = f(np.broadcast_to(vg[None], (128,) + vg.shape))
    common["cm_wsT"] = f(np.asarray(inp["cm_w_s"]).transpose(0, 3, 1, 2))
    common["cm_bs"] = f(np.asarray(inp["cm_b_s"]).reshape(1, -1))
    common["cm_w_out"] = f(inp["cm_w_out"])
    common["ff_w1"] = f(inp["ff_w1"])
    common["ff_w2"] = f(inp["ff_w2"])
    common["rope_cs"] = rope_tables(c.L)
    d, m = const_tables()
    common["dconst"] = d
    common["mats"] = m
    x = np.asarray(inp["x"], dtype=np.float32)
    ctx = np.asarray(inp["ctx"], dtype=np.float32)
    cc = np.asarray(inp["c"], dtype=np.float32)
    c_ctx = np.asarray(inp["c_ctx"], dtype=np.float32)
    maps = []
    for core in range(c.NCORES):
        bs = slice(core * c.NB, (core + 1) * c.NB)
        xin = np.concatenate([ctx[bs].transpose(0, 2, 1), x[bs].transpose(0, 2, 1)], axis=2)
        cols = np.concatenate([cc[bs], c_ctx[None]], axis=0)
        cT = cols.T.reshape(KD, 128, c.NB + 1).transpose(1, 0, 2)
        m_ = dict(common)
        m_["xin"] = f(xin)
        m_["cT"] = f(cT)
        maps.append(m_)
    return maps


_CACHE = {}


def kernel(**inputs):
    cfg = Cfg()
    if "nc" not in _CACHE:
        _CACHE["nc"] = build_program(cfg)
    nc = _CACHE["nc"]
    maps = prep_inputs(cfg, inputs)
    res = run_bass_kernel_spmd(nc, maps, core_ids=list(range(cfg.NCORES)))
    outs = [np.asarray(r["y"]).transpose(0, 2, 1) for r in res.results]
    return np.ascontiguousarray(np.concatenate(outs, axis=0).astype(np.float32))
```

```python
import contextlib
import math
import numpy as np
import concourse.bass as bass
import concourse.mybir as mybir
from concourse.bass_utils import run_bass_kernel_spmd

F32 = mybir.dt.float32
BF16 = mybir.dt.bfloat16
AF = mybir.ActivationFunctionType
ALU = mybir.AluOpType
EPS = 1e-6
ROPE_BASE = 10000.0
GRID_W = 64


class Cfg:
    def __init__(self, D=1024, L=2048, CTX=256, DEPTH=4, RH=4, AH=4, AKV=2, NB=4, NCORES=8, FP=512):
        self.D, self.L, self.CTX, self.DEPTH = D, L, CTX, DEPTH
        self.RH, self.AH, self.AKV = RH, AH, AKV
        self.NB, self.NCORES = NB, NCORES
        self.KD = D // 128
        self.T = L + CTX
        self.NCH = self.T // 128
        self.FF = 4 * D
        self.FP = FP
        self.WD = D
        self.G = self.WD // 128
        self.NE = (DEPTH + 1) // 2
        self.NO = DEPTH // 2
        self.ABW = 4 * RH * 128 + AH * 128 + 2 * AKV * 128
        self.NMIX = RH + AH
        self.GQ = AH // AKV
        self.tiles = [(0, CTX)] + [(CTX + i * 512, CTX + (i + 1) * 512) for i in range(L // 512)]
        self.SLOT = 4096


class Sched:
    def __init__(self, nc, stack):
        self.nc = nc
        self.stack = stack
        self.engs = {"pe": nc.tensor, "act": nc.scalar, "dve": nc.vector, "pool": nc.gpsimd, "sp": nc.sync}
        self.psem, self.pcnt = {}, {}
        for e in self.engs:
            self.psem[e] = stack.enter_context(nc.semaphore("prog_" + e))
            self.pcnt[e] = 0
        self.seen = {e: {} for e in self.engs}
        self.last_w = {}
        self.readers = {}
        self.self_wait = {"pe": False, "act": True, "dve": True, "pool": True, "sp": True}
        self.n_wait = 0
        self.n_ops = 0

    def new_sem(self, name):
        return self.stack.enter_context(self.nc.semaphore(name))

    def _deps(self, reads, writes):
        deps = []
        for k in reads:
            t = self.last_w.get(k)
            if t is not None:
                deps.append(t)
        for k in writes:
            t = self.last_w.get(k)
            if t is not None:
                deps.append(t)
            deps.extend(self.readers.get(k, ()))
        return deps

    def _wait(self, e, deps):
        eng = self.engs[e]
        best = {}
        for (sem, val) in deps:
            sid = id(sem)
            if sid not in best or best[sid][1] < val:
                best[sid] = (sem, val)
        own = id(self.psem[e])
        for sid, (sem, val) in best.items():
            if sid == own and not self.self_wait[e]:
                continue
            if self.seen[e].get(sid, 0) >= val:
                continue
            eng.wait_ge(sem, val)
            self.n_wait += 1
            self.seen[e][sid] = val

    def _record(self, tok, reads, writes):
        for k in reads:
            lst = self.readers.setdefault(k, [])
            lst[:] = [t for t in lst if t[0] is not tok[0]]
            lst.append(tok)
        for k in writes:
            self.last_w[k] = tok
            self.readers[k] = []

    def op(self, e, fn, reads=(), writes=()):
        self._wait(e, self._deps(reads, writes))
        ins = fn(self.engs[e])
        self.pcnt[e] += 1
        self.n_ops += 1
        ins.then_inc(self.psem[e], 1)
        tok = (self.psem[e], self.pcnt[e])
        self._record(tok, reads, writes)
        return tok

    def dma(self, q, semst, fn, reads=(), writes=()):
        deps = self._deps(reads, writes)
        if semst[1] > 0:
            deps.append((semst[0], semst[1]))
        self._wait(q, deps)
        inss = fn(self.engs[q])
        if not isinstance(inss, (list, tuple)):
            inss = [inss]
        for ins in inss:
            ins.then_inc(semst[0], 16)
            semst[1] += 16
        tok = (semst[0], semst[1])
        self._record(tok, reads, writes)
        return tok

    def barrier(self, engines=("pe", "act", "dve")):
        for e in engines:
            deps = [(self.psem[o], self.pcnt[o]) for o in engines if o != e and self.pcnt[o] > 0]
            self._wait(e, deps)


def build_program(cfg, layers=None, debug_out=False):
    c = cfg
    D, KD, T, L, CTX, NB, DEPTH = c.D, c.KD, c.T, c.L, c.CTX, c.NB, c.DEPTH
    RH, AH, AKV, GQ, G, WD, FF, FP = c.RH, c.AH, c.AKV, c.GQ, c.G, c.WD, c.FF, c.FP
    NE, NO, NCH = c.NE, c.NO, c.NCH
    tiles = c.tiles
    NT = len(tiles)
    if layers is None:
        layers = list(range(DEPTH))
    NBC = NB + 1
    nc = bass.Bass("TRN2", target_bir_lowering=False)
    dt_in = lambda name, shape: nc.dram_tensor(name, list(shape), F32, kind="ExternalInput").ap()
    xin = dt_in("xin", [NB, D, T])
    cT = dt_in("cT", [128, KD, NBC])
    mod_w = dt_in("mod_w", [DEPTH, D, 6 * D])
    mod_b = dt_in("mod_b", [128, DEPTH, 6 * KD])
    gains = dt_in("gains", [128, DEPTH, 2, KD])
    ab_w_in = dt_in("ab_w_in", [NE, D, c.ABW])
    ab_w_out = dt_in("ab_w_out", [NE, c.NMIX * 128, D])
    ret_dec = dt_in("ret_dec", [128, NE * 2 * RH])
    qk_g = dt_in("qk_g", [128, NE, 2])
    cm_w_in = dt_in("cm_w_in", [max(NO, 1), D, 2 * WD])
    cm_vg = dt_in("cm_vg", [128, max(NO, 1), WD])
    cm_wsT = dt_in("cm_wsT", [max(NO, 1), 128, G, 128])
    cm_bs = dt_in("cm_bs", [1, max(NO, 1) * G * 128])
    cm_w_out = dt_in("cm_w_out", [max(NO, 1), WD, D])
    ff_w1 = dt_in("ff_w1", [DEPTH, D, FF])
    ff_w2 = dt_in("ff_w2", [DEPTH, FF, D])
    rope_cs = dt_in("rope_cs", [128, 2, L])
    dconst = dt_in("dconst", [128, 6 * 128 + 2])
    mats = dt_in("mats", [128, 4, 128])
    y = nc.dram_tensor("y", [NB, D, L], F32, kind="ExternalOutput").ap()

    with contextlib.ExitStack() as st:
        S = Sched(nc, st)
        sb = lambda name, shape, dt: st.enter_context(nc.sbuf_tensor(name, list(shape), dt))
        X = sb("X", [128, KD, T], F32)
        hbuf = sb("hbuf", [128, KD, T], BF16)
        NSLOT = 4
        slots = [sb(f"wslot{i}", [128, c.SLOT], BF16) for i in range(NSLOT)]
        slot_sem = [[S.new_sem(f"slot{i}"), 0] for i in range(NSLOT)]
        cos_t = sb("cos_t", [128, L], BF16)
        sin_t = sb("sin_t", [128, L], BF16)
        mats_b = sb("mats_b", [128, 4, 128], BF16)
        ones_m, ident_m, c_m, avg_m = (mats_b[:, i, :] for i in range(4))
        dcs = sb("dcs", [128, 6 * 128 + 2], BF16)
        relf, maskf, relb, maskb, posq1, posq2 = (dcs[:, i * 128:(i + 1) * 128] for i in range(6))
        posr = dcs[:, 768:769]
        posj = dcs[:, 769:770]
        modT = sb("modT", [128, DEPTH, 6 * KD, NBC], F32)
        gsc = sb("gsc", [128, DEPTH, 2, NBC, KD], F32)
        gains_s = sb("gains_s", [128, DEPTH, 2, KD], F32)
        modb_s = sb("modb_s", [128, DEPTH, 6 * KD], F32)
        lg = sb("lg", [128, NE * 2 * RH], F32)
        qkg_s = sb("qkg_s", [128, NE, 2], F32)
        silc = sb("silc", [128, KD, NBC], BF16)
        eps_t = sb("eps_t", [128, 1], F32)
        nshift_t = sb("nshift_t", [128, 1], F32)
        mhalf_t = sb("mhalf_t", [128, 1], F32)
        NPB = 6
        pbanks = [st.enter_context(nc.psum_tensor(f"psb{i}", [128, 512], F32)) for i in range(NPB)]
        ptrs = [st.enter_context(nc.psum_tensor(f"ptr{i}", [128, 1024], BF16)) for i in range(2)]
        pstate = {"i": 0}

        ps_live = set()

        def PS():
            while True:
                i = pstate["i"]
                pstate["i"] = (i + 1) % NPB
                if ("ps", i) not in ps_live:
                    return pbanks[i], ("ps", i)

        _uid = [0]

        def uid():
            _uid[0] += 1
            return f"_u{_uid[0]}"

        csem = {"pool": [S.new_sem("csem_pool"), 0], "sp": [S.new_sem("csem_sp"), 0]}
        xsem = [S.new_sem("xsem"), 0]
        ysem = [S.new_sem("ysem"), 0]
        xs_sems = [[S.new_sem(f"xs{k}"), 0] for k in range(4)]

        XK = lambda tt: [("X", tt, kc) for kc in range(KD)]
        HK = lambda tt: [("h", tt, kc) for kc in range(KD)]

        def tile_of_chunk(ch):
            t0 = ch * 128
            for tt, (a, b_) in enumerate(tiles):
                if a <= t0 < b_:
                    return tt
            raise ValueError

        def load_slot(si, pieces, extra_key=None):
            def fn(e):
                return [e.dma_start(out=d, in_=s) for (d, s) in pieces]
            return S.dma("pool", slot_sem[si], fn, writes=[("slot", si)])

        def cdma(q, out, in_, key):
            S.dma(q, csem[q], lambda e: e.dma_start(out=out, in_=in_), writes=[key])

        cdma("pool", cos_t[:, :], rope_cs[:, 0, :], "cos")
        cdma("pool", sin_t[:, :], rope_cs[:, 1, :], "sin")
        cdma("pool", mats_b[:, :, :], mats[:, :, :], "mats")
        cdma("pool", dcs[:, :], dconst[:, :], "dcs")
        cdma("sp", gains_s[:, :, :, :], gains[:, :, :, :], "gains")
        cdma("sp", modb_s[:, :, :], mod_b[:, :, :], "modb")
        cdma("sp", lg[:, :], ret_dec[:, :], "lg")
        cdma("sp", qkg_s[:, :, :], qk_g[:, :, :], "qkg")
        with contextlib.ExitStack() as pst:
            psb = lambda name, shape, dt: pst.enter_context(nc.sbuf_tensor(name, list(shape), dt))
            cT_s = psb("cT_s", [128, KD, NBC], F32)
            lgt = psb("lgt", [128, NE * 2 * RH], F32)
            cdma("sp", cT_s[:, :, :], cT[:, :, :], "cT")
            S.op("dve", lambda e: e.memset(eps_t[:, :], EPS), writes=["eps"])
            S.op("dve", lambda e: e.memset(nshift_t[:, :], -math.sqrt(128.0)), writes=["nshift"])
            S.op("dve", lambda e: e.memset(mhalf_t[:, :], -0.5), writes=["mhalf"])
            S.op("act", lambda e: e.activation(out=silc[:, :, :], in_=cT_s[:, :, :], func=AF.Silu), reads=["cT"], writes=["silc"])
            S.op("act", lambda e: e.activation(out=lgt[:, :], in_=lg[:, :], func=AF.Exp, scale=-1.0), reads=["lg"], writes=["lgt"])
            S.op("act", lambda e: e.activation(out=lgt[:, :], in_=lgt[:, :], func=AF.Ln, bias=1.0, scale=1.0), reads=["lgt"], writes=["lgt"])
            S.op("act", lambda e: e.mul(lg[:, :], lgt[:, :], -1.0), reads=["lgt"], writes=["lg"])
            NPC = min(c.SLOT // KD, 512)
            si = 0
            for l in layers:
                mwv = mod_w[l].rearrange("(kc p) n -> p kc n", p=128)
                for pc in range(6 * D // NPC):
                    sl = slots[si]
                    slv = sl[:, 0:KD * NPC].rearrange("p (kc n) -> p kc n", kc=KD)
                    load_slot(si, [(slv[:, :, :], mwv[:, :, pc * NPC:(pc + 1) * NPC])])
                    ps, pk = PS()
                    nchk = NPC // 128

                    def mm(e):
                        ins = None
                        for j in range(nchk):
                            for kc in range(KD):
                                ins = e.matmul(ps[:, j * NBC:(j + 1) * NBC], lhsT=slv[:, kc, j * 128:(j + 1) * 128],
                                               rhs=silc[:, kc, :], start=(kc == 0), stop=(kc == KD - 1))
                        return ins
                    S.op("pe", mm, reads=[("slot", si), "silc"], writes=[pk])
                    for j in range(nchk):
                        ch = pc * nchk + j
                        S.op("act", lambda e: e.activation(out=modT[:, l, ch, :], in_=ps[:, j * NBC:(j + 1) * NBC],
                                                           func=AF.Identity, bias=modb_s[:, l, ch:ch + 1], scale=1.0),
                             reads=[pk, "modb"], writes=[("modT", l, ch)])
                    si = (si + 1) % NSLOT
                for ni in range(2):
                    for bi in range(NBC):
                        base = (3 * ni + 1) * KD
                        S.op("dve", lambda e: e.scalar_tensor_tensor(out=gsc[:, l, ni, bi, :], in0=modT[:, l, base:base + KD, bi],
                                                                     scalar=1.0, in1=gains_s[:, l, ni, :], op0=ALU.add, op1=ALU.mult),
                             reads=[("modT", l, base + k) for k in range(KD)] + ["gains"], writes=[("gsc", l, ni, bi)])
            S.barrier()

        def rope(src, src_keys, dst, dst_keys, lt0, W, ra, rb, ri, inplace=False):
            a, bb = ra[0], rb[0]
            ak, bk = ("ra", 0), ("rb", 0)
            S.op("dve", lambda e: e.tensor_tensor(out=bb[0:64, :W], in0=src[64:128, :], in1=sin_t[64:128, lt0:lt0 + W], op=ALU.mult),
                 reads=src_keys + ["sin"], writes=[bk])
            S.op("dve", lambda e: e.tensor_tensor(out=bb[64:128, :W], in0=src[0:64, :], in1=sin_t[0:64, lt0:lt0 + W], op=ALU.mult),
                 reads=src_keys + ["sin"], writes=[bk])
            S.op("dve", lambda e: e.tensor_tensor(out=a[:, :W], in0=src, in1=cos_t[:, lt0:lt0 + W], op=ALU.mult),
                 reads=src_keys + ["cos"], writes=[ak])
            S.op("dve", lambda e: e.tensor_tensor(out=dst, in0=a[:, :W], in1=bb[:, :W], op=ALU.add),
                 reads=[ak, bk], writes=dst_keys)

        def proj_fm(wv, wkey, tt, ps, pk):
            t0, t1 = tiles[tt]
            W = t1 - t0

            def mm(e):
                ins = None
                for kc in range(KD):
                    ins = e.matmul(ps[:, :W], lhsT=wv[:, kc, :], rhs=hbuf[:, kc, t0:t1], start=(kc == 0), stop=(kc == KD - 1))
                return ins
            S.op("pe", mm, reads=[wkey] + HK(tt), writes=[pk])

        def norm_bufs(pb):
            return dict(sq=[pb(f"sq{i}", [128, KD, 512], BF16) for i in range(2)],
                        tmp=[pb(f"ntmp{i}", [128, 512], F32) for i in range(3)],
                        srt=[pb(f"srt{i}", [128, 512], F32) for i in range(2)],
                        rstd=[pb(f"rstd{i}", [128, 512], F32) for i in range(2)])

        def norm_body(l, ni, b, tlist, bufs):
            sq, tmp, srt, rstd = bufs["sq"], bufs["tmp"], bufs["srt"], bufs["rstd"]
            cnt = 0
            for tt in tlist:
                t0, t1 = tiles[tt]
                W = t1 - t0
                bi = NB if tt == 0 else b
                i2 = tt % 2
                sqb = sq[i2]
                S.op("act", lambda e: e.activation(out=sqb[:, :, :W], in_=X[:, :, t0:t1], func=AF.Square),
                     reads=XK(tt), writes=[("sq", i2)])
                ps, pk = PS()

                def mm(e):
                    ins = None
                    for kc in range(KD):
                        ins = e.matmul(ps[:, :W], lhsT=ones_m, rhs=sqb[:, kc, :W], start=(kc == 0), stop=(kc == KD - 1))
                    return ins
                S.op("pe", mm, reads=[("sq", i2), "mats"], writes=[pk])
                S.op("act", lambda e: e.activation(out=srt[i2][:, :W], in_=ps[:, :W], func=AF.Ln, scale=1.0 / D, bias=eps_t[:, 0:1]),
                     reads=[pk, "eps"], writes=[("srt", i2)])
                S.op("act", lambda e: e.activation(out=rstd[i2][:, :W], in_=srt[i2][:, :W], func=AF.Exp, scale=-0.5), reads=[("srt", i2)], writes=[("rstd", i2)])
                for kc in range(KD):
                    tb = tmp[cnt % 3]
                    tk = ("ntmp", cnt % 3)
                    cnt += 1
                    S.op("dve", lambda e: e.scalar_tensor_tensor(out=tb[:, :W], in0=X[:, kc, t0:t1], scalar=gsc[:, l, ni, bi, kc:kc + 1],
                                                                 in1=rstd[i2][:, :W], op0=ALU.mult, op1=ALU.mult),
                         reads=[("X", tt, kc), ("rstd", i2), ("gsc", l, ni, bi)], writes=[tk])
                    ch = 3 * ni * KD + kc
                    S.op("act", lambda e: e.activation(out=hbuf[:, kc, t0:t1], in_=tb[:, :W], func=AF.Identity,
                                                       bias=modT[:, l, ch, bi:bi + 1], scale=1.0),
                         reads=[tk, ("modT", l, ch)], writes=[("h", tt, kc)])

        def norm_phase(l, ni, b, tlist):
            with contextlib.ExitStack() as ph:
                pb = lambda name, shape, dt: ph.enter_context(nc.sbuf_tensor(name + uid(), list(shape), dt))
                bufs = norm_bufs(pb)
                S.barrier()
                norm_body(l, ni, b, tlist, bufs)

        def resid(ps, pk, l, gi, bi, m, tt):
            t0, t1 = tiles[tt]
            W = t1 - t0
            ch = gi * KD + m
            S.op("dve", lambda e: e.scalar_tensor_tensor(out=X[:, m, t0:t1], in0=ps[:, :W], scalar=modT[:, l, ch, bi:bi + 1],
                                                         in1=X[:, m, t0:t1], op0=ALU.mult, op1=ALU.add),
                 reads=[pk, ("modT", l, ch), ("X", tt, m)], writes=[("X", tt, m)])

        def ffn_phase(l, b, tlist, next_norm=None):
            nparts = FF // FP
            nj = FP // 128
            with contextlib.ExitStack() as ph:
                pb = lambda name, shape, dt: ph.enter_context(nc.sbuf_tensor(name + uid(), list(shape), dt))
                bufs = norm_bufs(pb)
                hid = [pb(f"hid{i}", [128, nj, 512], BF16) for i in range(2)]
                rl = [pb(f"rl{i}", [128, 512], BF16) for i in range(3)]
                S.barrier()
                norm_body(l, 1, b, tlist, bufs)
                w1v = ff_w1[l].rearrange("(kc p) n -> p kc n", p=128)
                w2v = ff_w2[l].rearrange("(c p) n -> p c n", p=128)
                items = [(part, tt) for part in range(nparts) for tt in tlist]
                wviews = {}
                rc = [0]

                def wv_of(part):
                    if part not in wviews:
                        sa, sb_ = (part % 2) * 2, (part % 2) * 2 + 1
                        w1s = slots[sa][:, 0:KD * FP].rearrange("p (kc n) -> p kc n", kc=KD)
                        w2s = slots[sb_][:, 0:nj * D].rearrange("p (c n) -> p c n", c=nj)
                        load_slot(sa, [(w1s[:, :, :], w1v[:, :, part * FP:(part + 1) * FP])])
                        load_slot(sb_, [(w2s[:, :, :], w2v[:, part * nj:(part + 1) * nj, :])])
                        wviews[part] = (sa, sb_, w1s, w2s)
                    return wviews[part]

                def F1(k):
                    part, tt = items[k]
                    sa, sb_, w1s, w2s = wv_of(part)
                    W = tiles[tt][1] - tiles[tt][0]
                    hb, hk = hid[k % 2], ("hid", k % 2)
                    for j in range(nj):
                        ps, pk = PS()
                        proj_fm(w1s[:, :, j * 128:(j + 1) * 128], ("slot", sa), tt, ps, pk)
                        rb_ = rl[rc[0] % 3]
                        rk = ("rl", rc[0] % 3)
                        rc[0] += 1
                        S.op("act", lambda e: e.activation(out=rb_[:, :W], in_=ps[:, :W], func=AF.Relu), reads=[pk], writes=[rk])
                        S.op("act", lambda e: e.activation(out=hb[:, j, :W], in_=rb_[:, :W], func=AF.Square), reads=[rk], writes=[hk + (j,)])

                def F2(k):
                    part, tt = items[k]
                    sa, sb_, w1s, w2s = wv_of(part)
                    W = tiles[tt][1] - tiles[tt][0]
                    bi = NB if tt == 0 else b
                    hb, hk = hid[k % 2], ("hid", k % 2)
                    for m in range(KD):
                        ps, pk = PS()

                        def mm(e):
                            ins = None
                            for j in range(nj):
                                ins = e.matmul(ps[:, :W], lhsT=w2s[:, j, m * 128:(m + 1) * 128], rhs=hb[:, j, :W],
                                               start=(j == 0), stop=(j == nj - 1))
                            return ins
                        S.op("pe", mm, reads=[("slot", sb_)] + [hk + (j,) for j in range(nj)], writes=[pk])
                        resid(ps, pk, l, 5, bi, m, tt)
                F1(0)
                for k in range(len(items)):
                    if k + 1 < len(items):
                        F1(k + 1)
                    F2(k)
                    if next_norm is not None and items[k][0] == nparts - 1 and items[k][1] in next_norm[1]:
                        norm_body(next_norm[0], 0, b, [items[k][1]], bufs)

        def cm_prefetch(l):
            i = l // 2
            NPC = min(c.SLOT // KD, 2 * WD)
            n_in_slots = (2 * WD) // NPC
            wiv = cm_w_in[i].rearrange("(kc p) n -> p kc n", p=128)
            wins = []
            for s_ in range(n_in_slots):
                v_ = slots[s_][:, 0:KD * NPC].rearrange("p (kc n) -> p kc n", kc=KD)
                load_slot(s_, [(v_[:, :, :], wiv[:, :, s_ * NPC:(s_ + 1) * NPC])])
                wins.append(v_)
            return wins

        def cm_phase(l, b, tlist, wins):
            i = l // 2
            NPC = min(c.SLOT // KD, 2 * WD)
            n_in_slots = (2 * WD) // NPC
            assert n_in_slots <= NSLOT
            cps = c.SLOT // D
            n_out_slots = (G + cps - 1) // cps
            with contextlib.ExitStack() as ph:
                pb = lambda name, shape, dt: ph.enter_context(nc.sbuf_tensor(name + uid(), list(shape), dt))
                xslots = [pb(f"xslot{k}", [128, c.SLOT], BF16) for k in range(n_out_slots)]
                xsem_ = xs_sems
                u_t = [pb(f"u_t{k}", [128, G, 512], BF16) for k in range(1)]
                vg = [pb(f"vg{k}", [128, WD], F32) for k in range(2)]
                vn = [pb(f"vn{k}", [128, WD], BF16) for k in range(2)]
                ss = [pb(f"cmss{k}", [128, 4], F32) for k in range(2)]
                vgain = pb("vgain", [128, WD], F32)
                wsT = pb("wsT", [128, G, 128], BF16)
                bsr = pb("bsr", [1, G * 128], BF16)
                S.barrier(("pe", "act", "dve", "pool"))

                def win_cols(c0, n):
                    s_ = c0 // NPC
                    o = c0 % NPC
                    assert o + n <= NPC
                    return wins[s_][:, :, o:o + n], ("slot", s_)
                wov = cm_w_out[i].rearrange("(c p) n -> p c n", p=128)
                wouts = []
                for k in range(n_out_slots):
                    nch_ = min(cps, G - k * cps)
                    v_ = xslots[k][:, 0:nch_ * D].rearrange("p (c n) -> p c n", c=nch_)
                    S.dma("pool", xsem_[k], lambda e: e.dma_start(out=v_[:, :, :], in_=wov[:, k * cps:k * cps + nch_, :]), writes=[("xslot", k)])
                    wouts.append(v_)
                S.dma("pool", csem["pool"], lambda e: e.dma_start(out=wsT[:, :, :], in_=cm_wsT[i]), writes=["wsT"])
                S.dma("pool", csem["pool"], lambda e: e.dma_start(out=bsr[:, :], in_=cm_bs[:, i * G * 128:(i + 1) * G * 128]), writes=["bsr"])
                S.dma("pool", csem["pool"], lambda e: e.dma_start(out=vgain[:, :], in_=cm_vg[:, i, :]), writes=["vgain"])
                vcs = [0]
                for ti, tt in enumerate(tlist):
                    t0, t1 = tiles[tt]
                    W = t1 - t0
                    bi = NB if tt == 0 else b
                    ub, uk = u_t[0], ("u_t", 0)
                    uvb = ub
                    for g in range(G):
                        ps, pk = PS()
                        wv, wk = win_cols(g * 128, 128)
                        proj_fm(wv, wk, tt, ps, pk)
                        S.op("act", lambda e: e.activation(out=ub[:, g, :W], in_=ps[:, :W], func=AF.Gelu_apprx_tanh), reads=[pk], writes=[uk + (g,)])
                    npc_v = min(512, WD)
                    npv = WD // npc_v

                    def Vproj(cc):
                        c0 = t0 + cc * 128
                        vi = vcs[0] % 2
                        vcs[0] += 1
                        vgb, vgk = vg[vi], ("vg", vi)
                        for pcv in range(npv):
                            ps, pk = PS()
                            wv, wk = win_cols(WD + pcv * npc_v, npc_v)

                            def mm(e):
                                ins = None
                                for kc in range(KD):
                                    ins = e.matmul(ps[:, :npc_v], lhsT=hbuf[:, kc, c0:c0 + 128], rhs=wv[:, kc, :], start=(kc == 0), stop=(kc == KD - 1))
                                return ins
                            S.op("pe", mm, reads=[wk] + HK(tt), writes=[pk])
                            S.op("act", lambda e: e.activation(out=vgb[:, pcv * npc_v:(pcv + 1) * npc_v], in_=ps[:, :npc_v], func=AF.Gelu_apprx_tanh),
                                 reads=[pk], writes=[vgk + (pcv,)])
                        return vi

                    def Vrest(cc, vi):
                        vgb, vgk = vg[vi], ("vg", vi)
                        vnb, vnk = vn[vi], ("vn", vi)
                        ssb, ssk = ss[vi], ("cmss", vi)
                        allv = [vgk + (p_,) for p_ in range(npv)]
                        S.op("act", lambda e: e.activation(out=vnb[:, :], in_=vgb[:, :], func=AF.Square, accum_out=ssb[:, 0:1]),
                             reads=allv, writes=[ssk, vnk])
                        S.op("dve", lambda e: e.tensor_scalar(out=ssb[:, 1:2], in0=ssb[:, 0:1], scalar1=1.0 / WD, scalar2=EPS, op0=ALU.mult, op1=ALU.add),
                             reads=[ssk], writes=[ssk + (1,)])
                        S.op("pool", lambda e: e.tensor_tensor(out=ssb[:, 2:3], in0=ssb[:, 1:2], in1=mhalf_t[:, 0:1], op=ALU.pow),
                             reads=[ssk + (1,), "mhalf"], writes=[ssk + (2,)])
                        S.op("dve", lambda e: e.scalar_tensor_tensor(out=vnb[:, :], in0=vgb[:, :], scalar=ssb[:, 2:3], in1=vgain[:, :],
                                                                     op0=ALU.mult, op1=ALU.mult),
                             reads=allv + [ssk + (2,), "vgain"], writes=[vnk])
                        for g0 in range(0, G, 4):
                            ng = min(4, G - g0)
                            ps, pk = PS()

                            def mm(e):
                                ins = None
                                for gg in range(ng):
                                    g = g0 + gg
                                    e.matmul(ps[:, gg * 128:(gg + 1) * 128], lhsT=vnb[:, g * 128:(g + 1) * 128], rhs=wsT[:, g, :], start=True, stop=False)
                                    ins = e.matmul(ps[:, gg * 128:(gg + 1) * 128], lhsT=ones_m[0:1, :], rhs=bsr[0:1, g * 128:(g + 1) * 128], start=False, stop=True)
                                return ins
                            S.op("pe", mm, reads=[vnk, "wsT", "bsr", "mats"], writes=[pk])
                            S.op("dve", lambda e: e.tensor_tensor(out=uvb[:, g0:g0 + ng, cc * 128:(cc + 1) * 128],
                                                                  in0=ps[:, 0:ng * 128].rearrange("p (g n) -> p g n", g=ng),
                                                                  in1=ub[:, g0:g0 + ng, cc * 128:(cc + 1) * 128], op=ALU.mult),
                                 reads=[pk] + [uk + (g0 + gg,) for gg in range(ng)], writes=[uk + (g0 + gg,) for gg in range(ng)])
                    nchk_t = W // 128
                    vi_next = Vproj(0)
                    for cc in range(nchk_t):
                        vi_cur = vi_next
                        if cc + 1 < nchk_t:
                            vi_next = Vproj(cc + 1)
                        Vrest(cc, vi_cur)
                    uv_keys = [uk + (g,) for g in range(G)]
                    for m in range(KD):
                        ps, pk = PS()

                        def mm(e):
                            ins = None
                            for g in range(G):
                                wv_ = wouts[g // cps]
                                ins = e.matmul(ps[:, :W], lhsT=wv_[:, g % cps, m * 128:(m + 1) * 128], rhs=uvb[:, g, :W], start=(g == 0), stop=(g == G - 1))
                            return ins
                        S.op("pe", mm, reads=[("xslot", k) for k in range(n_out_slots)] + uv_keys, writes=[pk])
                        resid(ps, pk, l, 2, bi, m, tt)

        def ab_phase(l, b):
            i = l // 2
            winv = ab_w_in[i].rearrange("(kc p) n -> p kc n", p=128)
            woutv = ab_w_out[i].rearrange("(c p) n -> p c n", p=128)
            AX = mybir.AxisListType.X
            with contextlib.ExitStack() as ph:
                pb = lambda name, shape, dt: ph.enter_context(nc.sbuf_tensor(name + uid(), list(shape), dt))
                carve1 = lambda k: slots[1][:, 2048 + k * 512: 2048 + (k + 1) * 512]
                carve3 = lambda k: slots[3][:, 2048 + k * 512: 2048 + (k + 1) * 512]
                mixout = pb("mixout", [128, 2, T], BF16)
                kT = pb("kT", [128, T], BF16)
                v_tok = pb("v_tok", [128, NCH, 128], BF16)
                Sf_bf = pb("Sf_bf", [128, NCH, 128], BF16)
                Sb_bf = pb("Sb_bf", [128, NCH, 128], BF16)
                qh = [pb("qh0", [128, 512], BF16), carve1(0)]
                qf = [pb("qf0", [128, 512], BF16), carve1(1)]
                qb = [pb("qb0", [128, 512], BF16), carve1(2)]
                gt = [pb("gt0", [128, 512], BF16), carve1(3)]
                sqn = [pb("sqn0", [128, 512], BF16), carve3(0)]
                pT = [pb(f"pT{k}", [128, 512], BF16) for k in range(2)] + [carve3(1), carve3(2), carve3(3)]
                NPT = len(pT)
                ra = [pb("ra0", [128, 512], F32)]
                rb = [pb("rb0", [128, 512], F32)]
                sdv = [pb("sdv0", [128, 512], F32)]
                sdT = [pb(f"sdT{k}", [128, 128], BF16) for k in range(4)]
                NKT = 4
                ktok = [pb(f"ktok{k}", [128, 512], BF16) for k in range(NKT)]
                Sst = [pb(f"Sst{k}", [128, 2, 128], F32) for k in range(2)]
                Mh = pb("Mh", [128, 128], BF16)
                mtmp = pb("mtmp", [128, 2, 128], F32)
                qdf = pb("qdf", [128, 512], BF16)
                qdb = pb("qdb", [128, 512], BF16)
                dsc = pb("dsc", [128, 8], F32)
                wmean = pb("wmean", [128, KD], F32)
                S.barrier()
                CK = ("carve",)
                cnt = {"pT": 0, "sdT": 0, "kt": 0}
                inv_sqrt_dk = 1.0 / math.sqrt(128.0)
                LA = 2

                def out_proj(pair, slot_i):
                    wo = slots[slot_i][:, 0:2 * D].rearrange("p (c n) -> p c n", c=2)
                    load_slot(slot_i, [(wo[:, :, :], woutv[:, 2 * pair:2 * pair + 2, :])])
                    for tt in range(NT):
                        t0, t1 = tiles[tt]
                        W = t1 - t0
                        bi = NB if tt == 0 else b
                        for m in range(KD):
                            ps, pk = PS()

                            def mm(e):
                                e.matmul(ps[:, :W], lhsT=wo[:, 0, m * 128:(m + 1) * 128], rhs=mixout[:, 0, t0:t1], start=True, stop=False)
                                return e.matmul(ps[:, :W], lhsT=wo[:, 1, m * 128:(m + 1) * 128], rhs=mixout[:, 1, t0:t1], start=False, stop=True)
                            S.op("pe", mm, reads=[("slot", slot_i), ("mix", 0, tt), ("mix", 1, tt)], writes=[pk])
                            resid(ps, pk, l, 2, bi, m, tt)

                def v_proj(wv, wk):
                    for c0 in range(0, NCH, 4):
                        n = min(4, NCH - c0)
                        ps, pk = PS()

                        def mm(e):
                            ins = None
                            for cc in range(n):
                                ch = c0 + cc
                                for kc in range(KD):
                                    ins = e.matmul(ps[:, cc * 128:(cc + 1) * 128], lhsT=hbuf[:, kc, ch * 128:(ch + 1) * 128], rhs=wv[:, kc, :],
                                                   start=(kc == 0), stop=(kc == KD - 1))
                            return ins
                        tts = sorted(set(tile_of_chunk(c0 + cc) for cc in range(n)))
                        S.op("pe", mm, reads=[wk] + [k for tt in tts for k in HK(tt)], writes=[pk])
                        S.op("act", lambda e: e.activation(out=v_tok[:, c0:c0 + n, :], in_=ps[:, 0:n * 128].rearrange("p (c n) -> p c n", c=n), func=AF.Copy),
                             reads=[pk], writes=[("v_tok", c0)])

                def vkey(ch):
                    return ("v_tok", (ch // 4) * 4)

                for hh in range(RH):
                    si = (hh % 2) * 2
                    wsl = slots[si][:, 0:KD * 512].rearrange("p (kc f n) -> p kc f n", kc=KD, f=4)
                    load_slot(si, [(wsl[:, :, f, :], winv[:, :, f * RH * 128 + hh * 128: f * RH * 128 + (hh + 1) * 128]) for f in range(4)])
                    wk = ("slot", si)
                    S.op("dve", lambda e: e.reduce_sum(out=wmean[:, :], in_=wsl[:, :, 2, :], axis=AX), reads=[wk], writes=["wmean"])
                    S.op("dve", lambda e: e.tensor_scalar(out=wmean[:, :], in0=wmean[:, :], scalar1=-1.0 / 128, scalar2=None, op0=ALU.mult),
                         reads=["wmean"], writes=["wmean"])
                    for kc in range(KD):
                        S.op("dve", lambda e: e.tensor_scalar(out=wsl[:, kc, 2, :], in0=wsl[:, kc, 2, :], scalar1=wmean[:, kc:kc + 1], scalar2=None, op0=ALU.add),
                             reads=[wk, "wmean"], writes=[("wvc", kc)])
                    wvk = [("wvc", kc) for kc in range(KD)]
                    cf = (i * 2 + 0) * RH + hh
                    cb = (i * 2 + 1) * RH + hh
                    lgf, lgb = lg[:, cf:cf + 1], lg[:, cb:cb + 1]
                    S.op("act", lambda e: e.activation(out=mtmp[:, 0, :], in_=relf, func=AF.Exp, scale=lgf), reads=["dcs", "lg"], writes=["mtmp0"])
                    S.op("act", lambda e: e.activation(out=mtmp[:, 1, :], in_=relb, func=AF.Exp, scale=lgb), reads=["dcs", "lg"], writes=["mtmp1"])
                    S.op("dve", lambda e: e.tensor_tensor(out=mtmp[:, 0, :], in0=mtmp[:, 0, :], in1=maskf, op=ALU.mult), reads=["mtmp0", "dcs"], writes=["mtmp0"])
                    S.op("dve", lambda e: e.tensor_tensor(out=mtmp[:, 1, :], in0=mtmp[:, 1, :], in1=maskb, op=ALU.mult), reads=["mtmp1", "dcs"], writes=["mtmp1"])
                    S.op("dve", lambda e: e.tensor_tensor(out=mtmp[:, 0, :], in0=mtmp[:, 0, :], in1=mtmp[:, 1, :], op=ALU.add),
                         reads=["mtmp0", "mtmp1"], writes=["mtmp0"])
                    S.op("act", lambda e: e.mul(Mh[:, :], mtmp[:, 0, :], inv_sqrt_dk), reads=["mtmp0"], writes=["Mh"])
                    for k4 in range(4):
                        S.op("act", lambda e: e.activation(out=qdf[:, k4 * 128:(k4 + 1) * 128], in_=posq1, func=AF.Exp, scale=lgf), reads=["dcs", "lg"], writes=[("qdf", k4)])
                        S.op("act", lambda e: e.activation(out=qdb[:, k4 * 128:(k4 + 1) * 128], in_=posq2, func=AF.Exp, scale=lgb), reads=["dcs", "lg"], writes=[("qdb", k4)])
                    S.op("act", lambda e: e.activation(out=dsc[:, 0:1], in_=posr, func=AF.Exp, scale=lgf), reads=["dcs", "lg"], writes=[("dsc", 0)])
                    S.op("act", lambda e: e.activation(out=dsc[:, 1:2], in_=posj, func=AF.Exp, scale=lgb), reads=["dcs", "lg"], writes=[("dsc", 1)])
                    S.op("act", lambda e: e.activation(out=dsc[:, 2:3], in_=lgf, func=AF.Exp, scale=128.0), reads=["lg"], writes=[("dsc", 2)])
                    S.op("act", lambda e: e.activation(out=dsc[:, 3:4], in_=lgb, func=AF.Exp, scale=128.0), reads=["lg"], writes=[("dsc", 3)])
                    S.op("dve", lambda e: e.tensor_scalar(out=dsc[:, 4:6], in0=dsc[:, 0:2], scalar1=inv_sqrt_dk, scalar2=None, op0=ALU.mult),
                         reads=[("dsc", 0), ("dsc", 1)], writes=[("dsc", 4)])
                    qdkeys = [("qdf", k4) for k4 in range(4)] + [("qdb", k4) for k4 in range(4)]
                    for tt in range(NT):
                        t0, t1 = tiles[tt]
                        W = t1 - t0
                        ps, pk = PS()
                        proj_fm(wsl[:, :, 1, :], wk, tt, ps, pk)
                        if tt == 0:
                            S.op("act", lambda e: e.activation(out=kT[:, t0:t1], in_=ps[:, :W], func=AF.Copy), reads=[pk], writes=[("kT", tt)])
                        else:
                            rope(ps[:, :W], [pk], kT[:, t0:t1], [("kT", tt)], t0 - CTX, W, ra, rb, 0)
                    for c0 in range(0, NCH, 4):
                        n = min(4, NCH - c0)
                        ps, pk = PS()

                        def mmv(e):
                            ins = None
                            for cc in range(n):
                                ch = c0 + cc
                                for kc in range(KD):
                                    ins = e.matmul(ps[:, cc * 128:(cc + 1) * 128], lhsT=hbuf[:, kc, ch * 128:(ch + 1) * 128], rhs=wsl[:, kc, 2, :],
                                                   start=(kc == 0), stop=(kc == KD - 1))
                            return ins
                        tts = sorted(set(tile_of_chunk(c0 + cc) for cc in range(n)))
                        S.op("pe", mmv, reads=[wk] + wvk + [k for tt in tts for k in HK(tt)], writes=[pk])
                        S.op("act", lambda e: e.activation(out=v_tok[:, c0:c0 + n, :], in_=ps[:, 0:n * 128].rearrange("p (c n) -> p c n", c=n), func=AF.Copy),
                             reads=[pk], writes=[("v_tok", c0)])
                    nctx = CTX // 128
                    orders = [list(range(NCH)), list(range(nctx - 1, -1, -1)) + list(range(NCH - 1, nctx - 1, -1))]
                    Sdsts = [Sf_bf, Sb_bf]
                    for d_ in range(2):
                        S.op("dve", lambda e: e.memset(Sst[0][:, d_, :], 0.0), writes=[("Sst", 0, d_)])
                    nsteps = NCH - 1
                    nbatch = (nsteps + 3) // 4

                    def emit_batch(d_, bidx):
                        chs = orders[d_][bidx * 4: min(bidx * 4 + 4, nsteps)]
                        n = len(chs)
                        bank = cnt["kt"] % 2
                        kbuf = cnt["kt"] % NKT
                        cnt["kt"] += 1
                        pb_, tk = ptrs[bank], ("ptr", bank)

                        def tr(e):
                            ins = None
                            for k_, ch in enumerate(chs):
                                ins = e.transpose(pb_[:, k_ * 128:(k_ + 1) * 128], kT[:, ch * 128:(ch + 1) * 128], ident_m)
                            return ins
                        S.op("pe", tr, reads=sorted(set(("kT", tile_of_chunk(ch)) for ch in chs)) + ["mats"], writes=[tk])
                        S.op("dve", lambda e: e.tensor_scalar(out=ktok[kbuf][:, 0:n * 128], in0=pb_[:, 0:n * 128], scalar1=dsc[:, 4 + d_:5 + d_],
                                                              scalar2=None, op0=ALU.mult),
                             reads=[tk, ("dsc", 4)], writes=[("ktok", kbuf)])
                        return kbuf
                    kb_cur = [None, None]
                    kb_nxt = [emit_batch(0, 0), emit_batch(1, 0)]
                    for oi in range(nsteps + 1):
                        for d_ in range(2):
                            order = orders[d_]
                            if oi % 4 == 0 and oi < nsteps:
                                kb_cur[d_] = kb_nxt[d_]
                                if oi // 4 + 1 < nbatch:
                                    kb_nxt[d_] = emit_batch(d_, oi // 4 + 1)
                            ch = order[oi]
                            cur, nxt = Sst[oi % 2], Sst[(oi + 1) % 2]
                            ck, nk = ("Sst", oi % 2, d_), ("Sst", (oi + 1) % 2, d_)
                            S.op("act", lambda e: e.activation(out=Sdsts[d_][:, ch, :], in_=cur[:, d_, :], func=AF.Copy), reads=[ck], writes=[("Sbf", d_, ch)])
                            if oi == nsteps:
                                continue
                            kbuf = kb_cur[d_]
                            k_ = oi % 4
                            ps, pk = PS()
                            S.op("pe", lambda e: e.matmul(ps[:, 0:128], lhsT=ktok[kbuf][:, k_ * 128:(k_ + 1) * 128], rhs=v_tok[:, ch, :], start=True, stop=True),
                                 reads=[("ktok", kbuf), vkey(ch)], writes=[pk])
                            S.op("dve", lambda e: e.scalar_tensor_tensor(out=nxt[:, d_, :], in0=cur[:, d_, :], scalar=dsc[:, 2 + d_:3 + d_], in1=ps[:, 0:128],
                                                                         op0=ALU.mult, op1=ALU.add),
                                 reads=[ck, pk, ("dsc", 2 + d_)], writes=[nk])
                    mslot = hh % 2

                    def Qprep(tt):
                        bi_ = tt % 2
                        t0, t1 = tiles[tt]
                        W = t1 - t0
                        ps, pk = PS()
                        proj_fm(wsl[:, :, 0, :], wk, tt, ps, pk)
                        if tt == 0:
                            S.op("act", lambda e: e.activation(out=qh[bi_][:, :W], in_=ps[:, :W], func=AF.Copy), reads=[pk, CK], writes=[("qh", bi_)])
                        else:
                            rope(ps[:, :W], [pk, CK], qh[bi_][:, :W], [("qh", bi_)], t0 - CTX, W, ra, rb, 0)
                        S.op("dve", lambda e: e.tensor_tensor(out=qf[bi_][:, :W], in0=qh[bi_][:, :W], in1=qdf[:, :W], op=ALU.mult),
                             reads=[("qh", bi_), CK] + qdkeys, writes=[("qf", bi_)])
                        S.op("dve", lambda e: e.tensor_tensor(out=qb[bi_][:, :W], in0=qh[bi_][:, :W], in1=qdb[:, :W], op=ALU.mult),
                             reads=[("qh", bi_), CK] + qdkeys, writes=[("qb", bi_)])

                    def Mpart(tt):
                        bi_ = tt % 2
                        t0, t1 = tiles[tt]
                        W = t1 - t0
                        ps, pk = PS()
                        proj_fm(wsl[:, :, 3, :], wk, tt, ps, pk)
                        S.op("act", lambda e: e.activation(out=sdv[0][:, :W], in_=ps[:, :W], func=AF.Exp, scale=-1.0), reads=[pk], writes=[("sdv", 0)])
                        S.op("act", lambda e: e.activation(out=sdv[0][:, :W], in_=sdv[0][:, :W], func=AF.Ln, bias=1.0, scale=1.0), reads=[("sdv", 0)], writes=[("sdv", 0)])
                        S.op("act", lambda e: e.activation(out=sdv[0][:, :W], in_=sdv[0][:, :W], func=AF.Exp, scale=-1.0), reads=[("sdv", 0)], writes=[("sdv", 0)])
                        S.op("dve", lambda e: e.tensor_tensor(out=gt[bi_][:, :W], in0=ps[:, :W], in1=sdv[0][:, :W], op=ALU.mult),
                             reads=[pk, ("sdv", 0), CK], writes=[("gt", bi_)])
                        ops_, opk = PS()
                        ps_live.add(opk)
                        nch_t = W // 128
                        pend = []

                        def score(cc):
                            ch = t0 // 128 + cc
                            cs = slice(cc * 128, (cc + 1) * 128)
                            ps2, pk2 = PS()
                            S.op("pe", lambda e: e.matmul(ps2[:, 0:128], lhsT=kT[:, ch * 128:(ch + 1) * 128], rhs=qh[bi_][:, cs], start=True, stop=True),
                                 reads=[("kT", tt), ("qh", bi_), CK], writes=[pk2])
                            sdi = cnt["sdT"] % 4
                            cnt["sdT"] += 1
                            S.op("dve", lambda e: e.tensor_tensor(out=sdT[sdi][:, :], in0=ps2[:, 0:128], in1=Mh[:, :], op=ALU.mult),
                                 reads=[pk2, "Mh"], writes=[("sdT", sdi)])
                            return (cc, ch, cs, sdi)

                        def accum(item):
                            cc, ch, cs, sdi = item

                            def mm(e):
                                e.matmul(ops_[:, cs], lhsT=v_tok[:, ch, :], rhs=sdT[sdi][:, :], start=True, stop=False)
                                e.matmul(ops_[:, cs], lhsT=Sf_bf[:, ch, :], rhs=qf[bi_][:, cs], start=False, stop=False)
                                return e.matmul(ops_[:, cs], lhsT=Sb_bf[:, ch, :], rhs=qb[bi_][:, cs], start=False, stop=True)
                            S.op("pe", mm, reads=[vkey(ch), ("sdT", sdi), ("Sbf", 0, ch), ("Sbf", 1, ch), ("qf", bi_), ("qb", bi_), CK], writes=[opk])
                        for cc in range(nch_t):
                            pend.append(score(cc))
                            if len(pend) > LA:
                                accum(pend.pop(0))
                        while pend:
                            accum(pend.pop(0))
                        S.op("act", lambda e: e.activation(out=sqn[bi_][:, :W], in_=ops_[:, :W], func=AF.Square), reads=[opk, CK], writes=[("sqn", bi_)])
                        return ops_, opk

                    def Npart(tt, ops_, opk):
                        bi_ = tt % 2
                        t0, t1 = tiles[tt]
                        W = t1 - t0
                        pv_, pvk = PS()
                        S.op("pe", lambda e: e.matmul(pv_[:, :W], lhsT=avg_m, rhs=sqn[bi_][:, :W], start=True, stop=True), reads=[("sqn", bi_), CK, "mats"], writes=[pvk])
                        S.op("act", lambda e: e.activation(out=sdv[0][:, :W], in_=pv_[:, :W], func=AF.Ln, bias=eps_t[:, 0:1], scale=1.0),
                             reads=[pvk, "eps"], writes=[("sdv", 0)])
                        S.op("act", lambda e: e.activation(out=sdv[0][:, :W], in_=sdv[0][:, :W], func=AF.Exp, scale=-0.5), reads=[("sdv", 0)], writes=[("sdv", 0)])
                        S.op("dve", lambda e: e.tensor_tensor(out=ra[0][:, :W], in0=ops_[:, :W], in1=sdv[0][:, :W], op=ALU.mult),
                             reads=[opk, ("sdv", 0)], writes=[("ra", 0)])
                        S.op("dve", lambda e: e.tensor_tensor(out=mixout[:, mslot, t0:t1], in0=ra[0][:, :W], in1=gt[bi_][:, :W], op=ALU.mult),
                             reads=[("ra", 0), ("gt", bi_), CK], writes=[("mix", mslot, tt)])

                    Qprep(0)
                    prev = None
                    for tt in range(NT):
                        if tt + 1 < NT:
                            Qprep(tt + 1)
                        ops_, opk = Mpart(tt)
                        if prev is not None:
                            Npart(prev[0], prev[1], prev[2])
                            ps_live.discard(prev[2])
                        prev = (tt, ops_, opk)
                    Npart(prev[0], prev[1], prev[2])
                    ps_live.discard(prev[2])
                    if hh % 2 == 1:
                        out_proj(hh // 2, (hh // 2 % 2) * 2 + 1)

                qoff = 4 * RH * 128
                koff = qoff + AH * 128
                voff = koff + AKV * 128

                def qk_norm_a(ps, pk, W, bi_):
                    S.op("act", lambda e: e.activation(out=sqn[bi_][:, :W], in_=ps[:, :W], func=AF.Square), reads=[pk, CK], writes=[("sqn", bi_)])

                def qk_norm_b(ps, pk, W, bi_, gcol, dst, dst_keys, tt):
                    p2, p2k = PS()
                    S.op("pe", lambda e: e.matmul(p2[:, :W], lhsT=avg_m, rhs=sqn[bi_][:, :W], start=True, stop=True), reads=[("sqn", bi_), CK, "mats"], writes=[p2k])
                    S.op("act", lambda e: e.activation(out=sdv[0][:, :W], in_=p2[:, :W], func=AF.Ln, bias=eps_t[:, 0:1], scale=1.0),
                         reads=[p2k, "eps"], writes=[("sdv", 0)])
                    S.op("act", lambda e: e.activation(out=sdv[0][:, :W], in_=sdv[0][:, :W], func=AF.Exp, scale=-0.5), reads=[("sdv", 0)], writes=[("sdv", 0)])
                    S.op("dve", lambda e: e.scalar_tensor_tensor(out=ra[0][:, :W], in0=ps[:, :W], scalar=gcol, in1=sdv[0][:, :W], op0=ALU.mult, op1=ALU.mult),
                         reads=[pk, ("sdv", 0), "qkg"], writes=[("ra", 0)])
                    t0, t1 = tiles[tt]
                    if tt == 0:
                        S.op("act", lambda e: e.activation(out=dst, in_=ra[0][:, :W], func=AF.Copy), reads=[("ra", 0), CK], writes=dst_keys)
                    else:
                        rope(ra[0][:, :W], [("ra", 0), CK], dst, dst_keys, t0 - CTX, W, ra, rb, 0, inplace=True)

                for gi in range(AKV):
                    si = (gi % 2) * 2
                    nf = GQ + 2
                    wsl = slots[si][:, 0:KD * nf * 128].rearrange("p (kc f n) -> p kc f n", kc=KD, f=nf)
                    pieces = [(wsl[:, :, j, :], winv[:, :, qoff + (gi * GQ + j) * 128: qoff + (gi * GQ + j + 1) * 128]) for j in range(GQ)]
                    pieces.append((wsl[:, :, GQ, :], winv[:, :, koff + gi * 128: koff + (gi + 1) * 128]))
                    pieces.append((wsl[:, :, GQ + 1, :], winv[:, :, voff + gi * 128: voff + (gi + 1) * 128]))
                    load_slot(si, pieces)
                    wk = ("slot", si)
                    for tt in range(NT):
                        t0, t1 = tiles[tt]
                        W = t1 - t0
                        ps, pk = PS()
                        proj_fm(wsl[:, :, GQ, :], wk, tt, ps, pk)
                        qk_norm_a(ps, pk, W, 0)
                        qk_norm_b(ps, pk, W, 0, qkg_s[:, i, 1:2], kT[:, t0:t1], [("kT", tt)], tt)
                    v_proj(wsl[:, :, GQ + 1, :], wk)
                    its = [(j, tt) for j in range(GQ) for tt in range(NT)]

                    def prepA(n):
                        j, tt = its[n]
                        W = tiles[tt][1] - tiles[tt][0]
                        ps, pk = PS()
                        proj_fm(wsl[:, :, j, :], wk, tt, ps, pk)
                        qk_norm_a(ps, pk, W, n % 2)
                        ps_live.add(pk)
                        return ps, pk

                    def prepB(n, ps, pk):
                        j, tt = its[n]
                        W = tiles[tt][1] - tiles[tt][0]
                        qk_norm_b(ps, pk, W, n % 2, qkg_s[:, i, 0:1], qh[n % 2][:, :W], [("qh", n % 2)], tt)
                        ps_live.discard(pk)
                    pq = prepA(0)
                    prepB(0, *pq)
                    for n, (j, tt) in enumerate(its):
                        bi_ = n % 2
                        mslot = j % 2
                        t0, t1 = tiles[tt]
                        W = t1 - t0
                        kchunks = list(range(CTX // 128)) if tt == 0 else list(range(NCH))
                        ops_, opk = PS()
                        ps_live.add(opk)
                        dps_, dpk = PS()
                        ps_live.add(dpk)
                        nxt_pq = prepA(n + 1) if n + 1 < len(its) else None
                        pend = []

                        def qk(n_, ch):
                            ps, pk = PS()
                            S.op("pe", lambda e: e.matmul(ps[:, :W], lhsT=kT[:, ch * 128:(ch + 1) * 128], rhs=qh[bi_][:, :W], start=True, stop=True),
                                 reads=[("kT", tile_of_chunk(ch)), ("qh", bi_), CK], writes=[pk])
                            pi = cnt["pT"] % NPT
                            cnt["pT"] += 1
                            S.op("act", lambda e: e.activation(out=pT[pi][:, :W], in_=ps[:, :W], func=AF.Exp, scale=inv_sqrt_dk, bias=nshift_t[:, 0:1]),
                                 reads=[pk, "nshift", CK], writes=[("pT", pi)])
                            return (n_, ch, pi)

                        def pv(item):
                            n_, ch, pi = item
                            first, last_ = (n_ == 0), (n_ == len(kchunks) - 1)

                            def mm(e):
                                e.matmul(ops_[:, :W], lhsT=v_tok[:, ch, :], rhs=pT[pi][:, :W], start=first, stop=last_)
                                return e.matmul(dps_[:, :W], lhsT=ones_m, rhs=pT[pi][:, :W], start=first, stop=last_)
                            S.op("pe", mm, reads=[vkey(ch), ("pT", pi), CK, "mats"], writes=[opk, dpk])
                        for n_, ch in enumerate(kchunks):
                            pend.append(qk(n_, ch))
                            if len(pend) > LA + 1:
                                pv(pend.pop(0))
                            if n_ == min(3, len(kchunks) - 1) and nxt_pq is not None:
                                prepB(n + 1, *nxt_pq)
                                nxt_pq = None
                        while pend:
                            pv(pend.pop(0))
                        S.op("dve", lambda e: e.reciprocal(out=sdv[0][:, :W], in_=dps_[:, :W]), reads=[dpk], writes=[("sdv", 0)])
                        S.op("dve", lambda e: e.tensor_tensor(out=mixout[:, mslot, t0:t1], in0=ops_[:, :W], in1=sdv[0][:, :W], op=ALU.mult),
                             reads=[opk, ("sdv", 0)], writes=[("mix", mslot, tt)])
                        ps_live.discard(opk)
                        ps_live.discard(dpk)
                        if tt == NT - 1 and j % 2 == 1:
                            pair = (RH + gi * GQ + j) // 2
                            out_proj(pair, (pair % 2) * 2 + 1)
                S.barrier()
                S.op("dve", lambda e: e.memset(dsc[:, 7:8], 0.0), writes=[("slot", 1), ("slot", 3), ("dsc", 7)])

        ytoks = []

        def load_x_tile(b, tt):
            t0, t1 = tiles[tt]
            S.dma("sp", xsem, lambda e: [e.dma_start(out=X[:, kc, t0:t1], in_=xin[b, kc * 128:(kc + 1) * 128, t0:t1]) for kc in range(KD)], writes=XK(tt))

        allX = [k for tt in range(NT) for k in XK(tt)]
        S.dma("sp", xsem, lambda e: [e.dma_start(out=X[:, kc, :], in_=xin[0, kc * 128:(kc + 1) * 128, :]) for kc in range(KD)], writes=allX)

        def tl_of(l):
            last = (l == DEPTH - 1)
            even = (l % 2 == 0)
            tl_mix = list(range(1, NT)) if (last and not even) else list(range(NT))
            tl_ffn = list(range(1, NT)) if last else list(range(NT))
            return tl_mix, tl_ffn

        for b in range(NB):
            normed = False
            for li, l in enumerate(layers):
                even = (l % 2 == 0)
                tl_mix, tl_ffn = tl_of(l)
                wins = None
                if not even:
                    wins = cm_prefetch(l)
                if not normed:
                    norm_phase(l, 0, b, tl_mix)
                if even:
                    ab_phase(l, b)
                else:
                    cm_phase(l, b, tl_mix, wins)
                nxt = None
                normed = False
                if li + 1 < len(layers):
                    l2 = layers[li + 1]
                    tl_mix2, _ = tl_of(l2)
                    if all(t in tl_ffn for t in tl_mix2):
                        nxt = (l2, tl_mix2)
                        normed = True
                ffn_phase(l, b, tl_ffn, nxt)
            for tt in range(1, NT):
                t0, t1 = tiles[tt]
                ytoks.append(S.dma("sp", ysem, lambda e: [e.dma_start(out=y[b, kc * 128:(kc + 1) * 128, t0 - CTX:t1 - CTX], in_=X[:, kc, t0:t1]) for kc in range(KD)],
                                   reads=XK(tt)))
            if b + 1 < NB:
                for tt in range(NT):
                    load_x_tile(b + 1, tt)
        ytok = ytoks[-1]
        S._wait("sp", [ytok])
        S.barrier(("pe", "act", "dve", "pool", "sp"))
        nc._sched_stats = (S.n_ops, S.n_wait)
    return nc


def rope_tables(L):
    rows = L // GRID_W
    row = np.repeat(np.arange(rows, dtype=np.float32), GRID_W)
    col = np.tile(np.arange(GRID_W, dtype=np.float32), rows)
    n_freq = 32
    inv = (ROPE_BASE ** (-np.arange(n_freq, dtype=np.float32) / n_freq)).astype(np.float32)
    ang = np.concatenate([row[:, None] * inv[None, :], col[:, None] * inv[None, :]], axis=-1)
    cos, sin = np.cos(ang).astype(np.float32), np.sin(ang).astype(np.float32)
    out = np.zeros((128, 2, L), np.float32)
    out[0:64, 0] = cos.T
    out[64:128, 0] = cos.T
    out[0:64, 1] = sin.T
    out[64:128, 1] = -sin.T
    return out


def const_tables():
    j = np.arange(128, dtype=np.float32)[:, None]
    i = np.arange(128, dtype=np.float32)[None, :]
    d = np.zeros((128, 6 * 128 + 2), np.float32)
    d[:, 0:128] = np.maximum(i - j, 0)
    d[:, 128:256] = (i >= j)
    d[:, 256:384] = np.maximum(j - i, 0)
    d[:, 384:512] = (j >= i)
    d[:, 512:640] = np.broadcast_to(i + 1, (128, 128))
    d[:, 640:768] = np.broadcast_to(128 - i, (128, 128))
    d[:, 768] = 127 - j[:, 0]
    d[:, 769] = j[:, 0]
    m = np.zeros((128, 4, 128), np.float32)
    m[:, 0] = 1.0
    m[:, 1] = np.eye(128)
    m[:, 2] = np.eye(128) - 1.0 / 128
    m[:, 3] = 1.0 / 128
    return d, m


def prep_inputs(cfg, inp):
    c = cfg
    f = lambda a: np.ascontiguousarray(np.asarray(a, dtype=np.float32))
    KD, D, DEPTH = c.KD, c.D, c.DEPTH
    common = {}
    common["mod_w"] = f(inp["mod_w"])
    common["mod_b"] = f(np.asarray(inp["mod_b"]).reshape(DEPTH, 6 * KD, 128).transpose(2, 0, 1))
    g1 = np.asarray(inp["norm1_g"]).reshape(DEPTH, KD, 128)
    g2 = np.asarray(inp["norm2_g"]).reshape(DEPTH, KD, 128)
    common["gains"] = f(np.stack([g1, g2], axis=1).transpose(3, 0, 1, 2))
    common["ab_w_in"] = f(inp["ab_w_in"])
    common["ab_w_out"] = f(inp["ab_w_out"])
    rd = np.asarray(inp["ret_decay"]).reshape(1, -1)
    common["ret_dec"] = f(np.broadcast_to(rd, (128, rd.shape[1])))
    common["qk_g"] = f(np.stack([np.asarray(inp["att_q_norm_g"]), np.asarray(inp["att_k_norm_g"])], axis=-1).transpose(1, 0, 2))
    common["cm_w_in"] = f(inp["cm_w_in"])
    vg = np.asarray(inp["cm_v_norm_g"])
    common["cm_vg"] = f(np.broadcast_to(vg[None], (128,) + vg.shape))
    common["cm_wsT"] = f(np.asarray(inp["cm_w_s"]).transpose(0, 3, 1, 2))
    common["cm_bs"] = f(np.asarray(inp["cm_b_s"]).reshape(1, -1))
    common["cm_w_out"] = f(inp["cm_w_out"])
    common["ff_w1"] = f(inp["ff_w1"])
    common["ff_w2"] = f(inp["ff_w2"])
    common["rope_cs"] = rope_tables(c.L)
    d, m = const_tables()
    common["dconst"] = d
    common["mats"] = m
    x = np.asarray(inp["x"], dtype=np.float32)
    ctx = np.asarray(inp["ctx"], dtype=np.float32)
    cc = np.asarray(inp["c"], dtype=np.float32)
    c_ctx = np.asarray(inp["c_ctx"], dtype=np.float32)
    maps = []
    for core in range(c.NCORES):
        bs = slice(core * c.NB, (core + 1) * c.NB)
        xin = np.concatenate([ctx[bs].transpose(0, 2, 1), x[bs].transpose(0, 2, 1)], axis=2)
        cols = np.concatenate([cc[bs], c_ctx[None]], axis=0)
        cT = cols.T.reshape(KD, 128, c.NB + 1).transpose(1, 0, 2)
        m_ = dict(common)
        m_["xin"] = f(xin)
        m_["cT"] = f(cT)
        maps.append(m_)
    return maps


_CACHE = {}


def kernel(**inputs):
    cfg = Cfg()
    if "nc" not in _CACHE:
        _CACHE["nc"] = build_program(cfg)
    nc = _CACHE["nc"]
    maps = prep_inputs(cfg, inputs)
    res = run_bass_kernel_spmd(nc, maps, core_ids=list(range(cfg.NCORES)))
    outs = [np.asarray(r["y"]).transpose(0, 2, 1) for r in res.results]
    return np.ascontiguousarray(np.concatenate(outs, axis=0).astype(np.float32))
```

```python
import contextlib
import math
import numpy as np
import concourse.bass as bass
import concourse.mybir as mybir
from concourse.bass_utils import run_bass_kernel_spmd

F32 = mybir.dt.float32
BF16 = mybir.dt.bfloat16
AF = mybir.ActivationFunctionType
ALU = mybir.AluOpType
EPS = 1e-6
ROPE_BASE = 10000.0
GRID_W = 64


class Cfg:
    def __init__(self, D=1024, L=2048, CTX=256, DEPTH=4, RH=4, AH=4, AKV=2, NB=4, NCORES=8, FP=512):
        self.D, self.L, self.CTX, self.DEPTH = D, L, CTX, DEPTH
        self.RH, self.AH, self.AKV = RH, AH, AKV
        self.NB, self.NCORES = NB, NCORES
        self.KD = D // 128
        self.T = L + CTX
        self.NCH = self.T // 128
        self.FF = 4 * D
        self.FP = FP
        self.WD = D
        self.G = self.WD // 128
        self.NE = (DEPTH + 1) // 2
        self.NO = DEPTH // 2
        self.ABW = 4 * RH * 128 + AH * 128 + 2 * AKV * 128
        self.NMIX = RH + AH
        self.GQ = AH // AKV
        self.tiles = [(0, CTX)] + [(CTX + i * 512, CTX + (i + 1) * 512) for i in range(L // 512)]
        self.SLOT = 4096


class Sched:
    def __init__(self, nc, stack):
        self.nc = nc
        self.stack = stack
        self.engs = {"pe": nc.tensor, "act": nc.scalar, "dve": nc.vector, "pool": nc.gpsimd, "sp": nc.sync}
        self.psem, self.pcnt = {}, {}
        for e in self.engs:
            self.psem[e] = stack.enter_context(nc.semaphore("prog_" + e))
            self.pcnt[e] = 0
        self.seen = {e: {} for e in self.engs}
        self.last_w = {}
        self.readers = {}
        self.self_wait = {"pe": False, "act": True, "dve": True, "pool": True, "sp": True}
        self.n_wait = 0
        self.n_ops = 0

    def new_sem(self, name):
        return self.stack.enter_context(self.nc.semaphore(name))

    def _deps(self, reads, writes):
        deps = []
        for k in reads:
            t = self.last_w.get(k)
            if t is not None:
                deps.append(t)
        for k in writes:
            t = self.last_w.get(k)
            if t is not None:
                deps.append(t)
            deps.extend(self.readers.get(k, ()))
        return deps

    def _wait(self, e, deps):
        eng = self.engs[e]
        best = {}
        for (sem, val) in deps:
            sid = id(sem)
            if sid not in best or best[sid][1] < val:
                best[sid] = (sem, val)
        own = id(self.psem[e])
        for sid, (sem, val) in best.items():
            if sid == own and not self.self_wait[e]:
                continue
            if self.seen[e].get(sid, 0) >= val:
                continue
            eng.wait_ge(sem, val)
            self.n_wait += 1
            self.seen[e][sid] = val

    def _record(self, tok, reads, writes):
        for k in reads:
            lst = self.readers.setdefault(k, [])
            lst[:] = [t for t in lst if t[0] is not tok[0]]
            lst.append(tok)
        for k in writes:
            self.last_w[k] = tok
            self.readers[k] = []

    def op(self, e, fn, reads=(), writes=()):
        self._wait(e, self._deps(reads, writes))
        ins = fn(self.engs[e])
        self.pcnt[e] += 1
        self.n_ops += 1
        ins.then_inc(self.psem[e], 1)
        tok = (self.psem[e], self.pcnt[e])
        self._record(tok, reads, writes)
        return tok

    def dma(self, q, semst, fn, reads=(), writes=()):
        deps = self._deps(reads, writes)
        if semst[1] > 0:
            deps.append((semst[0], semst[1]))
        self._wait(q, deps)
        inss = fn(self.engs[q])
        if not isinstance(inss, (list, tuple)):
            inss = [inss]
        for ins in inss:
            ins.then_inc(semst[0], 16)
            semst[1] += 16
        tok = (semst[0], semst[1])
        self._record(tok, reads, writes)
        return tok

    def barrier(self, engines=("pe", "act", "dve")):
        for e in engines:
            deps = [(self.psem[o], self.pcnt[o]) for o in engines if o != e and self.pcnt[o] > 0]
            self._wait(e, deps)


def build_program(cfg, layers=None, debug_out=False):
    c = cfg
    D, KD, T, L, CTX, NB, DEPTH = c.D, c.KD, c.T, c.L, c.CTX, c.NB, c.DEPTH
    RH, AH, AKV, GQ, G, WD, FF, FP = c.RH, c.AH, c.AKV, c.GQ, c.G, c.WD, c.FF, c.FP
    NE, NO, NCH = c.NE, c.NO, c.NCH
    tiles = c.tiles
    NT = len(tiles)
    if layers is None:
        layers = list(range(DEPTH))
    NBC = NB + 1
    nc = bass.Bass("TRN2", target_bir_lowering=False)
    dt_in = lambda name, shape: nc.dram_tensor(name, list(shape), F32, kind="ExternalInput").ap()
    xin = dt_in("xin", [NB, D, T])
    cT = dt_in("cT", [128, KD, NBC])
    mod_w = dt_in("mod_w", [DEPTH, D, 6 * D])
    mod_b = dt_in("mod_b", [128, DEPTH, 6 * KD])
    gains = dt_in("gains", [128, DEPTH, 2, KD])
    ab_w_in = dt_in("ab_w_in", [NE, D, c.ABW])
    ab_w_out = dt_in("ab_w_out", [NE, c.NMIX * 128, D])
    ret_dec = dt_in("ret_dec", [128, NE * 2 * RH])
    qk_g = dt_in("qk_g", [128, NE, 2])
    cm_w_in = dt_in("cm_w_in", [max(NO, 1), D, 2 * WD])
    cm_vg = dt_in("cm_vg", [128, max(NO, 1), WD])
    cm_wsT = dt_in("cm_wsT", [max(NO, 1), 128, G, 128])
    cm_bs = dt_in("cm_bs", [1, max(NO, 1) * G * 128])
    cm_w_out = dt_in("cm_w_out", [max(NO, 1), WD, D])
    ff_w1 = dt_in("ff_w1", [DEPTH, D, FF])
    ff_w2 = dt_in("ff_w2", [DEPTH, FF, D])
    rope_cs = dt_in("rope_cs", [128, 2, L])
    dconst = dt_in("dconst", [128, 6 * 128 + 2])
    mats = dt_in("mats", [128, 4, 128])
    y = nc.dram_tensor("y", [NB, D, L], F32, kind="ExternalOutput").ap()

    with contextlib.ExitStack() as st:
        S = Sched(nc, st)
        sb = lambda name, shape, dt: st.enter_context(nc.sbuf_tensor(name, list(shape), dt))
        X = sb("X", [128, KD, T], F32)
        hbuf = sb("hbuf", [128, KD, T], BF16)
        NSLOT = 4
        slots = [sb(f"wslot{i}", [128, c.SLOT], BF16) for i in range(NSLOT)]
        slot_sem = [[S.new_sem(f"slot{i}"), 0] for i in range(NSLOT)]
        cos_t = sb("cos_t", [128, L], BF16)
        sin_t = sb("sin_t", [128, L], BF16)
        mats_b = sb("mats_b", [128, 4, 128], BF16)
        ones_m, ident_m, c_m, avg_m = (mats_b[:, i, :] for i in range(4))
        dcs = sb("dcs", [128, 6 * 128 + 2], BF16)
        relf, maskf, relb, maskb, posq1, posq2 = (dcs[:, i * 128:(i + 1) * 128] for i in range(6))
        posr = dcs[:, 768:769]
        posj = dcs[:, 769:770]
        modT = sb("modT", [128, DEPTH, 6 * KD, NBC], F32)
        gsc = sb("gsc", [128, DEPTH, 2, NBC, KD], F32)
        gains_s = sb("gains_s", [128, DEPTH, 2, KD], F32)
        modb_s = sb("modb_s", [128, DEPTH, 6 * KD], F32)
        lg = sb("lg", [128, NE * 2 * RH], F32)
        qkg_s = sb("qkg_s", [128, NE, 2], F32)
        silc = sb("silc", [128, KD, NBC], BF16)
        eps_t = sb("eps_t", [128, 1], F32)
        nshift_t = sb("nshift_t", [128, 1], F32)
        NPB = 6
        pbanks = [st.enter_context(nc.psum_tensor(f"psb{i}", [128, 512], F32)) for i in range(NPB)]
        ptrs = [st.enter_context(nc.psum_tensor(f"ptr{i}", [128, 1024], BF16)) for i in range(2)]
        pstate = {"i": 0}

        ps_live = set()

        def PS():
            while True:
                i = pstate["i"]
                pstate["i"] = (i + 1) % NPB
                if ("ps", i) not in ps_live:
                    return pbanks[i], ("ps", i)

        _uid = [0]

        def uid():
            _uid[0] += 1
            return f"_u{_uid[0]}"

        csem = {"pool": [S.new_sem("csem_pool"), 0], "sp": [S.new_sem("csem_sp"), 0]}
        xsem = [S.new_sem("xsem"), 0]
        ysem = [S.new_sem("ysem"), 0]
        xs_sems = [[S.new_sem(f"xs{k}"), 0] for k in range(4)]

        XK = lambda tt: [("X", tt, kc) for kc in range(KD)]
        HK = lambda tt: [("h", tt, kc) for kc in range(KD)]

        def tile_of_chunk(ch):
            t0 = ch * 128
            for tt, (a, b_) in enumerate(tiles):
                if a <= t0 < b_:
                    return tt
            raise ValueError

        def load_slot(si, pieces, extra_key=None):
            def fn(e):
                return [e.dma_start(out=d, in_=s) for (d, s) in pieces]
            return S.dma("pool", slot_sem[si], fn, writes=[("slot", si)])

        def cdma(q, out, in_, key):
            S.dma(q, csem[q], lambda e: e.dma_start(out=out, in_=in_), writes=[key])

        cdma("pool", cos_t[:, :], rope_cs[:, 0, :], "cos")
        cdma("pool", sin_t[:, :], rope_cs[:, 1, :], "sin")
        cdma("pool", mats_b[:, :, :], mats[:, :, :], "mats")
        cdma("pool", dcs[:, :], dconst[:, :], "dcs")
        cdma("sp", gains_s[:, :, :, :], gains[:, :, :, :], "gains")
        cdma("sp", modb_s[:, :, :], mod_b[:, :, :], "modb")
        cdma("sp", lg[:, :], ret_dec[:, :], "lg")
        cdma("sp", qkg_s[:, :, :], qk_g[:, :, :], "qkg")
        with contextlib.ExitStack() as pst:
            psb = lambda name, shape, dt: pst.enter_context(nc.sbuf_tensor(name, list(shape), dt))
            cT_s = psb("cT_s", [128, KD, NBC], F32)
            lgt = psb("lgt", [128, NE * 2 * RH], F32)
            cdma("sp", cT_s[:, :, :], cT[:, :, :], "cT")
            S.op("dve", lambda e: e.memset(eps_t[:, :], EPS), writes=["eps"])
            S.op("dve", lambda e: e.memset(nshift_t[:, :], -math.sqrt(128.0)), writes=["nshift"])
            S.op("act", lambda e: e.activation(out=silc[:, :, :], in_=cT_s[:, :, :], func=AF.Silu), reads=["cT"], writes=["silc"])
            S.op("act", lambda e: e.activation(out=lgt[:, :], in_=lg[:, :], func=AF.Exp, scale=-1.0), reads=["lg"], writes=["lgt"])
            S.op("act", lambda e: e.activation(out=lgt[:, :], in_=lgt[:, :], func=AF.Ln, bias=1.0, scale=1.0), reads=["lgt"], writes=["lgt"])
            S.op("act", lambda e: e.mul(lg[:, :], lgt[:, :], -1.0), reads=["lgt"], writes=["lg"])
            NPC = min(c.SLOT // KD, 512)
            si = 0
            for l in layers:
                mwv = mod_w[l].rearrange("(kc p) n -> p kc n", p=128)
                for pc in range(6 * D // NPC):
                    sl = slots[si]
                    slv = sl[:, 0:KD * NPC].rearrange("p (kc n) -> p kc n", kc=KD)
                    load_slot(si, [(slv[:, :, :], mwv[:, :, pc * NPC:(pc + 1) * NPC])])
                    ps, pk = PS()
                    nchk = NPC // 128

                    def mm(e):
                        ins = None
                        for j in range(nchk):
                            for kc in range(KD):
                                ins = e.matmul(ps[:, j * NBC:(j + 1) * NBC], lhsT=slv[:, kc, j * 128:(j + 1) * 128],
                                               rhs=silc[:, kc, :], start=(kc == 0), stop=(kc == KD - 1))
                        return ins
                    S.op("pe", mm, reads=[("slot", si), "silc"], writes=[pk])
                    for j in range(nchk):
                        ch = pc * nchk + j
                        S.op("act", lambda e: e.activation(out=modT[:, l, ch, :], in_=ps[:, j * NBC:(j + 1) * NBC],
                                                           func=AF.Identity, bias=modb_s[:, l, ch:ch + 1], scale=1.0),
                             reads=[pk, "modb"], writes=[("modT", l, ch)])
                    si = (si + 1) % NSLOT
                for ni in range(2):
                    for bi in range(NBC):
                        base = (3 * ni + 1) * KD
                        S.op("dve", lambda e: e.scalar_tensor_tensor(out=gsc[:, l, ni, bi, :], in0=modT[:, l, base:base + KD, bi],
                                                                     scalar=1.0, in1=gains_s[:, l, ni, :], op0=ALU.add, op1=ALU.mult),
                             reads=[("modT", l, base + k) for k in range(KD)] + ["gains"], writes=[("gsc", l, ni, bi)])
            S.barrier()

        def rope(src, src_keys, dst, dst_keys, lt0, W, ra, rb, ri, inplace=False):
            a, bb = ra[0], rb[0]
            ak, bk = ("ra", 0), ("rb", 0)
            S.op("dve", lambda e: e.tensor_tensor(out=bb[0:64, :W], in0=src[64:128, :], in1=sin_t[64:128, lt0:lt0 + W], op=ALU.mult),
                 reads=src_keys + ["sin"], writes=[bk])
            S.op("dve", lambda e: e.tensor_tensor(out=bb[64:128, :W], in0=src[0:64, :], in1=sin_t[0:64, lt0:lt0 + W], op=ALU.mult),
                 reads=src_keys + ["sin"], writes=[bk])
            S.op("dve", lambda e: e.tensor_tensor(out=a[:, :W], in0=src, in1=cos_t[:, lt0:lt0 + W], op=ALU.mult),
                 reads=src_keys + ["cos"], writes=[ak])
            S.op("dve", lambda e: e.tensor_tensor(out=dst, in0=a[:, :W], in1=bb[:, :W], op=ALU.add),
                 reads=[ak, bk], writes=dst_keys)

        def proj_fm(wv, wkey, tt, ps, pk):
            t0, t1 = tiles[tt]
            W = t1 - t0

            def mm(e):
                ins = None
                for kc in range(KD):
                    ins = e.matmul(ps[:, :W], lhsT=wv[:, kc, :], rhs=hbuf[:, kc, t0:t1], start=(kc == 0), stop=(kc == KD - 1))
                return ins
            S.op("pe", mm, reads=[wkey] + HK(tt), writes=[pk])

        def norm_bufs(pb):
            return dict(sq=[pb(f"sq{i}", [128, KD, 512], BF16) for i in range(2)],
                        tmp=[pb(f"ntmp{i}", [128, 512], F32) for i in range(3)],
                        srt=[pb(f"srt{i}", [128, 512], F32) for i in range(2)],
                        rstd=[pb(f"rstd{i}", [128, 512], F32) for i in range(2)])

        def norm_body(l, ni, b, tlist, bufs):
            sq, tmp, srt, rstd = bufs["sq"], bufs["tmp"], bufs["srt"], bufs["rstd"]
            cnt = 0
            for tt in tlist:
                t0, t1 = tiles[tt]
                W = t1 - t0
                bi = NB if tt == 0 else b
                i2 = tt % 2
                sqb = sq[i2]
                S.op("act", lambda e: e.activation(out=sqb[:, :, :W], in_=X[:, :, t0:t1], func=AF.Square),
                     reads=XK(tt), writes=[("sq", i2)])
                ps, pk = PS()

                def mm(e):
                    ins = None
                    for kc in range(KD):
                        ins = e.matmul(ps[:, :W], lhsT=ones_m, rhs=sqb[:, kc, :W], start=(kc == 0), stop=(kc == KD - 1))
                    return ins
                S.op("pe", mm, reads=[("sq", i2), "mats"], writes=[pk])
                S.op("act", lambda e: e.activation(out=srt[i2][:, :W], in_=ps[:, :W], func=AF.Ln, scale=1.0 / D, bias=eps_t[:, 0:1]),
                     reads=[pk, "eps"], writes=[("srt", i2)])
                S.op("act", lambda e: e.activation(out=rstd[i2][:, :W], in_=srt[i2][:, :W], func=AF.Exp, scale=-0.5), reads=[("srt", i2)], writes=[("rstd", i2)])
                for kc in range(KD):
                    tb = tmp[cnt % 3]
                    tk = ("ntmp", cnt % 3)
                    cnt += 1
                    S.op("dve", lambda e: e.scalar_tensor_tensor(out=tb[:, :W], in0=X[:, kc, t0:t1], scalar=gsc[:, l, ni, bi, kc:kc + 1],
                                                                 in1=rstd[i2][:, :W], op0=ALU.mult, op1=ALU.mult),
                         reads=[("X", tt, kc), ("rstd", i2), ("gsc", l, ni, bi)], writes=[tk])
                    ch = 3 * ni * KD + kc
                    S.op("act", lambda e: e.activation(out=hbuf[:, kc, t0:t1], in_=tb[:, :W], func=AF.Identity,
                                                       bias=modT[:, l, ch, bi:bi + 1], scale=1.0),
                         reads=[tk, ("modT", l, ch)], writes=[("h", tt, kc)])

        def norm_phase(l, ni, b, tlist):
            with contextlib.ExitStack() as ph:
                pb = lambda name, shape, dt: ph.enter_context(nc.sbuf_tensor(name + uid(), list(shape), dt))
                bufs = norm_bufs(pb)
                S.barrier()
                norm_body(l, ni, b, tlist, bufs)

        def resid(ps, pk, l, gi, bi, m, tt):
            t0, t1 = tiles[tt]
            W = t1 - t0
            ch = gi * KD + m
            S.op("dve", lambda e: e.scalar_tensor_tensor(out=X[:, m, t0:t1], in0=ps[:, :W], scalar=modT[:, l, ch, bi:bi + 1],
                                                         in1=X[:, m, t0:t1], op0=ALU.mult, op1=ALU.add),
                 reads=[pk, ("modT", l, ch), ("X", tt, m)], writes=[("X", tt, m)])

        def ffn_phase(l, b, tlist, next_norm=None):
            nparts = FF // FP
            nj = FP // 128
            with contextlib.ExitStack() as ph:
                pb = lambda name, shape, dt: ph.enter_context(nc.sbuf_tensor(name + uid(), list(shape), dt))
                bufs = norm_bufs(pb)
                hid = [pb(f"hid{i}", [128, nj, 512], BF16) for i in range(2)]
                rl = [pb(f"rl{i}", [128, 512], BF16) for i in range(3)]
                S.barrier()
                norm_body(l, 1, b, tlist, bufs)
                w1v = ff_w1[l].rearrange("(kc p) n -> p kc n", p=128)
                w2v = ff_w2[l].rearrange("(c p) n -> p c n", p=128)
                items = [(part, tt) for part in range(nparts) for tt in tlist]
                wviews = {}
                rc = [0]

                def wv_of(part):
                    if part not in wviews:
                        sa, sb_ = (part % 2) * 2, (part % 2) * 2 + 1
                        w1s = slots[sa][:, 0:KD * FP].rearrange("p (kc n) -> p kc n", kc=KD)
                        w2s = slots[sb_][:, 0:nj * D].rearrange("p (c n) -> p c n", c=nj)
                        load_slot(sa, [(w1s[:, :, :], w1v[:, :, part * FP:(part + 1) * FP])])
                        load_slot(sb_, [(w2s[:, :, :], w2v[:, part * nj:(part + 1) * nj, :])])
                        wviews[part] = (sa, sb_, w1s, w2s)
                    return wviews[part]

                def F1(k):
                    part, tt = items[k]
                    sa, sb_, w1s, w2s = wv_of(part)
                    W = tiles[tt][1] - tiles[tt][0]
                    hb, hk = hid[k % 2], ("hid", k % 2)
                    for j in range(nj):
                        ps, pk = PS()
                        proj_fm(w1s[:, :, j * 128:(j + 1) * 128], ("slot", sa), tt, ps, pk)
                        rb_ = rl[rc[0] % 3]
                        rk = ("rl", rc[0] % 3)
                        rc[0] += 1
                        S.op("act", lambda e: e.activation(out=rb_[:, :W], in_=ps[:, :W], func=AF.Relu), reads=[pk], writes=[rk])
                        S.op("act", lambda e: e.activation(out=hb[:, j, :W], in_=rb_[:, :W], func=AF.Square), reads=[rk], writes=[hk + (j,)])

                def F2(k):
                    part, tt = items[k]
                    sa, sb_, w1s, w2s = wv_of(part)
                    W = tiles[tt][1] - tiles[tt][0]
                    bi = NB if tt == 0 else b
                    hb, hk = hid[k % 2], ("hid", k % 2)
                    for m in range(KD):
                        ps, pk = PS()

                        def mm(e):
                            ins = None
                            for j in range(nj):
                                ins = e.matmul(ps[:, :W], lhsT=w2s[:, j, m * 128:(m + 1) * 128], rhs=hb[:, j, :W],
                                               start=(j == 0), stop=(j == nj - 1))
                            return ins
                        S.op("pe", mm, reads=[("slot", sb_)] + [hk + (j,) for j in range(nj)], writes=[pk])
                        resid(ps, pk, l, 5, bi, m, tt)
                F1(0)
                for k in range(len(items)):
                    if k + 1 < len(items):
                        F1(k + 1)
                    F2(k)
                    if next_norm is not None and items[k][0] == nparts - 1 and items[k][1] in next_norm[1]:
                        norm_body(next_norm[0], 0, b, [items[k][1]], bufs)

        def cm_prefetch(l):
            i = l // 2
            NPC = min(c.SLOT // KD, 2 * WD)
            n_in_slots = (2 * WD) // NPC
            wiv = cm_w_in[i].rearrange("(kc p) n -> p kc n", p=128)
            wins = []
            for s_ in range(n_in_slots):
                v_ = slots[s_][:, 0:KD * NPC].rearrange("p (kc n) -> p kc n", kc=KD)
                load_slot(s_, [(v_[:, :, :], wiv[:, :, s_ * NPC:(s_ + 1) * NPC])])
                wins.append(v_)
            return wins

        def cm_phase(l, b, tlist, wins):
            i = l // 2
            NPC = min(c.SLOT // KD, 2 * WD)
            n_in_slots = (2 * WD) // NPC
            assert n_in_slots <= NSLOT
            cps = c.SLOT // D
            n_out_slots = (G + cps - 1) // cps
            with contextlib.ExitStack() as ph:
                pb = lambda name, shape, dt: ph.enter_context(nc.sbuf_tensor(name + uid(), list(shape), dt))
                xslots = [pb(f"xslot{k}", [128, c.SLOT], BF16) for k in range(n_out_slots)]
                xsem_ = xs_sems
                u_t = [pb(f"u_t{k}", [128, G, 512], BF16) for k in range(1)]
                vg = [pb(f"vg{k}", [128, WD], BF16) for k in range(4)]
                vn = [pb(f"vn{k}", [128, WD], BF16) for k in range(2)]
                ss = [pb(f"cmss{k}", [128, 12], F32) for k in range(1)]
                vgain = pb("vgain", [128, WD], F32)
                wsT = pb("wsT", [128, G, 128], BF16)
                bsr = pb("bsr", [1, G * 128], BF16)
                S.barrier(("pe", "act", "dve", "pool"))

                def win_cols(c0, n):
                    s_ = c0 // NPC
                    o = c0 % NPC
                    assert o + n <= NPC
                    return wins[s_][:, :, o:o + n], ("slot", s_)
                wov = cm_w_out[i].rearrange("(c p) n -> p c n", p=128)
                wouts = []
                for k in range(n_out_slots):
                    nch_ = min(cps, G - k * cps)
                    v_ = xslots[k][:, 0:nch_ * D].rearrange("p (c n) -> p c n", c=nch_)
                    S.dma("pool", xsem_[k], lambda e: e.dma_start(out=v_[:, :, :], in_=wov[:, k * cps:k * cps + nch_, :]), writes=[("xslot", k)])
                    wouts.append(v_)
                S.dma("pool", csem["pool"], lambda e: e.dma_start(out=wsT[:, :, :], in_=cm_wsT[i]), writes=["wsT"])
                S.dma("pool", csem["pool"], lambda e: e.dma_start(out=bsr[:, :], in_=cm_bs[:, i * G * 128:(i + 1) * G * 128]), writes=["bsr"])
                S.dma("pool", csem["pool"], lambda e: e.dma_start(out=vgain[:, :], in_=cm_vg[:, i, :]), writes=["vgain"])
                npc_v = min(512, WD)
                npv = WD // npc_v
                ub, uk = u_t[0], ("u_t", 0)
                uv_keys = [uk + (g,) for g in range(G)]

                def Upath(tt):
                    W = tiles[tt][1] - tiles[tt][0]
                    for g in range(G):
                        ps, pk = PS()
                        wv, wk = win_cols(g * 128, 128)
                        proj_fm(wv, wk, tt, ps, pk)
                        S.op("act", lambda e: e.activation(out=ub[:, g, :W], in_=ps[:, :W], func=AF.Gelu_apprx_tanh), reads=[pk], writes=[uk + (g,)])

                def Vproj(tt):
                    t0, t1 = tiles[tt]
                    ssb, ssk = ss[0], ("cmss", 0)
                    for cc in range((t1 - t0) // 128):
                        c0 = t0 + cc * 128
                        vgb, vgk = vg[cc], ("vg", cc)
                        for pcv in range(npv):
                            ps, pk = PS()
                            wv, wk = win_cols(WD + pcv * npc_v, npc_v)

                            def mm(e):
                                ins = None
                                for kc in range(KD):
                                    ins = e.matmul(ps[:, :npc_v], lhsT=hbuf[:, kc, c0:c0 + 128], rhs=wv[:, kc, :], start=(kc == 0), stop=(kc == KD - 1))
                                return ins
                            S.op("pe", mm, reads=[wk] + HK(tt), writes=[pk])
                            S.op("act", lambda e: e.activation(out=vgb[:, pcv * npc_v:(pcv + 1) * npc_v], in_=ps[:, :npc_v], func=AF.Gelu_apprx_tanh),
                                 reads=[pk], writes=[vgk + (pcv,)])
                        allv = [vgk + (p_,) for p_ in range(npv)]
                        S.op("act", lambda e: e.activation(out=vn[cc % 2][:, :], in_=vgb[:, :], func=AF.Square, accum_out=ssb[:, cc:cc + 1]),
                             reads=allv, writes=[ssk + (cc,), ("vn", cc % 2)])

                def Vstats(tt):
                    n = (tiles[tt][1] - tiles[tt][0]) // 128
                    ssb, ssk = ss[0], ("cmss", 0)
                    S.op("act", lambda e: e.activation(out=ssb[:, 4:4 + n], in_=ssb[:, 0:n], func=AF.Sqrt, scale=1.0 / WD, bias=eps_t[:, 0:1]),
                         reads=[ssk + (cc,) for cc in range(n)] + ["eps"], writes=[ssk + ("s",)])
                    S.op("dve", lambda e: e.reciprocal(out=ssb[:, 8:8 + n], in_=ssb[:, 4:4 + n]), reads=[ssk + ("s",)], writes=[ssk + ("r",)])

                def Vrest(tt):
                    t0, t1 = tiles[tt]
                    ssb, ssk = ss[0], ("cmss", 0)
                    for cc in range((t1 - t0) // 128):
                        vgb, vgk = vg[cc], ("vg", cc)
                        vnb, vnk = vn[cc % 2], ("vn", cc % 2)
                        allv = [vgk + (p_,) for p_ in range(npv)]
                        S.op("dve", lambda e: e.scalar_tensor_tensor(out=vnb[:, :], in0=vgb[:, :], scalar=ssb[:, 8 + cc:9 + cc], in1=vgain[:, :],
                                                                     op0=ALU.mult, op1=ALU.mult),
                             reads=allv + [ssk + ("r",), "vgain"], writes=[vnk])
                        for g0 in range(0, G, 4):
                            ng = min(4, G - g0)
                            ps, pk = PS()

                            def mm(e):
                                ins = None
                                for gg in range(ng):
                                    g = g0 + gg
                                    e.matmul(ps[:, gg * 128:(gg + 1) * 128], lhsT=vnb[:, g * 128:(g + 1) * 128], rhs=wsT[:, g, :], start=True, stop=False)
                                    ins = e.matmul(ps[:, gg * 128:(gg + 1) * 128], lhsT=ones_m[0:1, :], rhs=bsr[0:1, g * 128:(g + 1) * 128], start=False, stop=True)
                                return ins
                            S.op("pe", mm, reads=[vnk, "wsT", "bsr", "mats"], writes=[pk])
                            S.op("dve", lambda e: e.tensor_tensor(out=ub[:, g0:g0 + ng, cc * 128:(cc + 1) * 128],
                                                                  in0=ps[:, 0:ng * 128].rearrange("p (g n) -> p g n", g=ng),
                                                                  in1=ub[:, g0:g0 + ng, cc * 128:(cc + 1) * 128], op=ALU.mult),
                                 reads=[pk] + [uk + (g0 + gg,) for gg in range(ng)], writes=[uk + (g0 + gg,) for gg in range(ng)])

                def Out(tt):
                    t0, t1 = tiles[tt]
                    W = t1 - t0
                    bi = NB if tt == 0 else b
                    for m in range(KD):
                        ps, pk = PS()

                        def mm(e):
                            ins = None
                            for g in range(G):
                                wv_ = wouts[g // cps]
                                ins = e.matmul(ps[:, :W], lhsT=wv_[:, g % cps, m * 128:(m + 1) * 128], rhs=ub[:, g, :W], start=(g == 0), stop=(g == G - 1))
                            return ins
                        S.op("pe", mm, reads=[("xslot", k) for k in range(n_out_slots)] + uv_keys, writes=[pk])
                        resid(ps, pk, l, 2, bi, m, tt)
                prev_tt = None
                for tt in tlist:
                    Vproj(tt)
                    if prev_tt is not None:
                        Out(prev_tt)
                    Upath(tt)
                    Vstats(tt)
                    Vrest(tt)
                    prev_tt = tt
                Out(prev_tt)

        def ab_phase(l, b):
            i = l // 2
            winv = ab_w_in[i].rearrange("(kc p) n -> p kc n", p=128)
            woutv = ab_w_out[i].rearrange("(c p) n -> p c n", p=128)
            AX = mybir.AxisListType.X
            with contextlib.ExitStack() as ph:
                pb = lambda name, shape, dt: ph.enter_context(nc.sbuf_tensor(name + uid(), list(shape), dt))
                carve1 = lambda k: slots[1][:, 2048 + k * 512: 2048 + (k + 1) * 512]
                carve3 = lambda k: slots[3][:, 2048 + k * 512: 2048 + (k + 1) * 512]
                mixout = pb("mixout", [128, 2, T], BF16)
                kT = pb("kT", [128, T], BF16)
                v_tok = pb("v_tok", [128, NCH, 128], BF16)
                Sf_bf = pb("Sf_bf", [128, NCH, 128], BF16)
                Sb_bf = pb("Sb_bf", [128, NCH, 128], BF16)
                qh = [pb("qh0", [128, 512], BF16), carve1(0)]
                qf = [pb("qf0", [128, 512], BF16), carve1(1)]
                qb = [pb("qb0", [128, 512], BF16), carve1(2)]
                gt = [pb("gt0", [128, 512], BF16), carve1(3)]
                sqn = [pb("sqn0", [128, 512], BF16), carve3(0)]
                pT = [pb(f"pT{k}", [128, 512], BF16) for k in range(2)] + [carve3(1), carve3(2), carve3(3)]
                NPT = len(pT)
                ra = [pb("ra0", [128, 512], F32)]
                rb = [pb("rb0", [128, 512], F32)]
                sdv = [pb("sdv0", [128, 512], F32)]
                sdT = [pb(f"sdT{k}", [128, 128], BF16) for k in range(4)]
                NKT = 4
                ktok = [pb(f"ktok{k}", [128, 512], BF16) for k in range(NKT)]
                Sst = [pb(f"Sst{k}", [128, 2, 128], F32) for k in range(2)]
                Mh = pb("Mh", [128, 128], BF16)
                mtmp = pb("mtmp", [128, 2, 128], F32)
                qdf = pb("qdf", [128, 512], BF16)
                qdb = pb("qdb", [128, 512], BF16)
                dsc = pb("dsc", [128, 8], F32)
                wmean = pb("wmean", [128, KD], F32)
                S.barrier()
                CK = ("carve",)
                cnt = {"pT": 0, "sdT": 0, "kt": 0}
                inv_sqrt_dk = 1.0 / math.sqrt(128.0)
                LA = 2

                def out_proj(pair, slot_i):
                    wo = slots[slot_i][:, 0:2 * D].rearrange("p (c n) -> p c n", c=2)
                    load_slot(slot_i, [(wo[:, :, :], woutv[:, 2 * pair:2 * pair + 2, :])])
                    for tt in range(NT):
                        t0, t1 = tiles[tt]
                        W = t1 - t0
                        bi = NB if tt == 0 else b
                        for m in range(KD):
                            ps, pk = PS()

                            def mm(e):
                                e.matmul(ps[:, :W], lhsT=wo[:, 0, m * 128:(m + 1) * 128], rhs=mixout[:, 0, t0:t1], start=True, stop=False)
                                return e.matmul(ps[:, :W], lhsT=wo[:, 1, m * 128:(m + 1) * 128], rhs=mixout[:, 1, t0:t1], start=False, stop=True)
                            S.op("pe", mm, reads=[("slot", slot_i), ("mix", 0, tt), ("mix", 1, tt)], writes=[pk])
                            resid(ps, pk, l, 2, bi, m, tt)

                def v_proj(wv, wk):
                    for c0 in range(0, NCH, 4):
                        n = min(4, NCH - c0)
                        ps, pk = PS()

                        def mm(e):
                            ins = None
                            for cc in range(n):
                                ch = c0 + cc
                                for kc in range(KD):
                                    ins = e.matmul(ps[:, cc * 128:(cc + 1) * 128], lhsT=hbuf[:, kc, ch * 128:(ch + 1) * 128], rhs=wv[:, kc, :],
                                                   start=(kc == 0), stop=(kc == KD - 1))
                            return ins
                        tts = sorted(set(tile_of_chunk(c0 + cc) for cc in range(n)))
                        S.op("pe", mm, reads=[wk] + [k for tt in tts for k in HK(tt)], writes=[pk])
                        S.op("act", lambda e: e.activation(out=v_tok[:, c0:c0 + n, :], in_=ps[:, 0:n * 128].rearrange("p (c n) -> p c n", c=n), func=AF.Copy),
                             reads=[pk], writes=[("v_tok", c0)])

                def vkey(ch):
                    return ("v_tok", (ch // 4) * 4)

                for hh in range(RH):
                    si = (hh % 2) * 2
                    wsl = slots[si][:, 0:KD * 512].rearrange("p (kc f n) -> p kc f n", kc=KD, f=4)
                    load_slot(si, [(wsl[:, :, f, :], winv[:, :, f * RH * 128 + hh * 128: f * RH * 128 + (hh + 1) * 128]) for f in range(4)])
                    wk = ("slot", si)
                    S.op("dve", lambda e: e.reduce_sum(out=wmean[:, :], in_=wsl[:, :, 2, :], axis=AX), reads=[wk], writes=["wmean"])
                    S.op("dve", lambda e: e.tensor_scalar(out=wmean[:, :], in0=wmean[:, :], scalar1=-1.0 / 128, scalar2=None, op0=ALU.mult),
                         reads=["wmean"], writes=["wmean"])
                    for kc in range(KD):
                        S.op("act", lambda e: e.activation(out=wsl[:, kc, 2, :], in_=wsl[:, kc, 2, :], func=AF.Identity, bias=wmean[:, kc:kc + 1], scale=1.0),
                             reads=[wk, "wmean"], writes=[("wvc", kc)])
                    wvk = [("wvc", kc) for kc in range(KD)]
                    cf = (i * 2 + 0) * RH + hh
                    cb = (i * 2 + 1) * RH + hh
                    lgf, lgb = lg[:, cf:cf + 1], lg[:, cb:cb + 1]
                    S.op("act", lambda e: e.activation(out=mtmp[:, 0, :], in_=relf, func=AF.Exp, scale=lgf), reads=["dcs", "lg"], writes=["mtmp0"])
                    S.op("act", lambda e: e.activation(out=mtmp[:, 1, :], in_=relb, func=AF.Exp, scale=lgb), reads=["dcs", "lg"], writes=["mtmp1"])
                    S.op("dve", lambda e: e.tensor_tensor(out=mtmp[:, 0, :], in0=mtmp[:, 0, :], in1=maskf, op=ALU.mult), reads=["mtmp0", "dcs"], writes=["mtmp0"])
                    S.op("dve", lambda e: e.tensor_tensor(out=mtmp[:, 1, :], in0=mtmp[:, 1, :], in1=maskb, op=ALU.mult), reads=["mtmp1", "dcs"], writes=["mtmp1"])
                    S.op("dve", lambda e: e.tensor_tensor(out=mtmp[:, 0, :], in0=mtmp[:, 0, :], in1=mtmp[:, 1, :], op=ALU.add),
                         reads=["mtmp0", "mtmp1"], writes=["mtmp0"])
                    S.op("act", lambda e: e.mul(Mh[:, :], mtmp[:, 0, :], inv_sqrt_dk), reads=["mtmp0"], writes=["Mh"])
                    for k4 in range(4):
                        S.op("act", lambda e: e.activation(out=qdf[:, k4 * 128:(k4 + 1) * 128], in_=posq1, func=AF.Exp, scale=lgf), reads=["dcs", "lg"], writes=[("qdf", k4)])
                        S.op("act", lambda e: e.activation(out=qdb[:, k4 * 128:(k4 + 1) * 128], in_=posq2, func=AF.Exp, scale=lgb), reads=["dcs", "lg"], writes=[("qdb", k4)])
                    S.op("act", lambda e: e.activation(out=dsc[:, 0:1], in_=posr, func=AF.Exp, scale=lgf), reads=["dcs", "lg"], writes=[("dsc", 0)])
                    S.op("act", lambda e: e.activation(out=dsc[:, 1:2], in_=posj, func=AF.Exp, scale=lgb), reads=["dcs", "lg"], writes=[("dsc", 1)])
                    S.op("act", lambda e: e.activation(out=dsc[:, 2:3], in_=lgf, func=AF.Exp, scale=128.0), reads=["lg"], writes=[("dsc", 2)])
                    S.op("act", lambda e: e.activation(out=dsc[:, 3:4], in_=lgb, func=AF.Exp, scale=128.0), reads=["lg"], writes=[("dsc", 3)])
                    S.op("dve", lambda e: e.tensor_scalar(out=dsc[:, 4:6], in0=dsc[:, 0:2], scalar1=inv_sqrt_dk, scalar2=None, op0=ALU.mult),
                         reads=[("dsc", 0), ("dsc", 1)], writes=[("dsc", 4)])
                    qdkeys = [("qdf", k4) for k4 in range(4)] + [("qdb", k4) for k4 in range(4)]
                    for tt in range(NT):
                        t0, t1 = tiles[tt]
                        W = t1 - t0
                        ps, pk = PS()
                        proj_fm(wsl[:, :, 1, :], wk, tt, ps, pk)
                        if tt == 0:
                            S.op("act", lambda e: e.activation(out=kT[:, t0:t1], in_=ps[:, :W], func=AF.Copy), reads=[pk], writes=[("kT", tt)])
                        else:
                            rope(ps[:, :W], [pk], kT[:, t0:t1], [("kT", tt)], t0 - CTX, W, ra, rb, 0)
                    for c0 in range(0, NCH, 4):
                        n = min(4, NCH - c0)
                        ps, pk = PS()

                        def mmv(e):
                            ins = None
                            for cc in range(n):
                                ch = c0 + cc
                                for kc in range(KD):
                                    ins = e.matmul(ps[:, cc * 128:(cc + 1) * 128], lhsT=hbuf[:, kc, ch * 128:(ch + 1) * 128], rhs=wsl[:, kc, 2, :],
                                                   start=(kc == 0), stop=(kc == KD - 1))
                            return ins
                        tts = sorted(set(tile_of_chunk(c0 + cc) for cc in range(n)))
                        S.op("pe", mmv, reads=[wk] + wvk + [k for tt in tts for k in HK(tt)], writes=[pk])
                        S.op("act", lambda e: e.activation(out=v_tok[:, c0:c0 + n, :], in_=ps[:, 0:n * 128].rearrange("p (c n) -> p c n", c=n), func=AF.Copy),
                             reads=[pk], writes=[("v_tok", c0)])
                    nctx = CTX // 128
                    orders = [list(range(NCH)), list(range(nctx - 1, -1, -1)) + list(range(NCH - 1, nctx - 1, -1))]
                    Sdsts = [Sf_bf, Sb_bf]
                    for d_ in range(2):
                        S.op("dve", lambda e: e.memset(Sst[0][:, d_, :], 0.0), writes=[("Sst", 0, d_)])
                    nsteps = NCH - 1
                    nbatch = (nsteps + 3) // 4

                    def emit_batch(d_, bidx):
                        chs = orders[d_][bidx * 4: min(bidx * 4 + 4, nsteps)]
                        n = len(chs)
                        bank = cnt["kt"] % 2
                        kbuf = cnt["kt"] % NKT
                        cnt["kt"] += 1
                        pb_, tk = ptrs[bank], ("ptr", bank)

                        def tr(e):
                            ins = None
                            for k_, ch in enumerate(chs):
                                ins = e.transpose(pb_[:, k_ * 128:(k_ + 1) * 128], kT[:, ch * 128:(ch + 1) * 128], ident_m)
                            return ins
                        S.op("pe", tr, reads=sorted(set(("kT", tile_of_chunk(ch)) for ch in chs)) + ["mats"], writes=[tk])
                        S.op("act", lambda e: e.activation(out=ktok[kbuf][:, 0:n * 128], in_=pb_[:, 0:n * 128], func=AF.Copy, scale=dsc[:, 4 + d_:5 + d_]),
                             reads=[tk, ("dsc", 4)], writes=[("ktok", kbuf)])
                        return kbuf
                    kb_cur = [None, None]
                    kb_nxt = [emit_batch(0, 0), emit_batch(1, 0)]
                    for oi in range(nsteps + 1):
                        for d_ in range(2):
                            order = orders[d_]
                            if oi % 4 == 0 and oi < nsteps:
                                kb_cur[d_] = kb_nxt[d_]
                                if oi // 4 + 1 < nbatch:
                                    kb_nxt[d_] = emit_batch(d_, oi // 4 + 1)
                            ch = order[oi]
                            cur, nxt = Sst[oi % 2], Sst[(oi + 1) % 2]
                            ck, nk = ("Sst", oi % 2, d_), ("Sst", (oi + 1) % 2, d_)
                            S.op("act", lambda e: e.activation(out=Sdsts[d_][:, ch, :], in_=cur[:, d_, :], func=AF.Copy), reads=[ck], writes=[("Sbf", d_, ch)])
                            if oi == nsteps:
                                continue
                            kbuf = kb_cur[d_]
                            k_ = oi % 4
                            ps, pk = PS()
                            S.op("pe", lambda e: e.matmul(ps[:, 0:128], lhsT=ktok[kbuf][:, k_ * 128:(k_ + 1) * 128], rhs=v_tok[:, ch, :], start=True, stop=True),
                                 reads=[("ktok", kbuf), vkey(ch)], writes=[pk])
                            S.op("dve", lambda e: e.scalar_tensor_tensor(out=nxt[:, d_, :], in0=cur[:, d_, :], scalar=dsc[:, 2 + d_:3 + d_], in1=ps[:, 0:128],
                                                                         op0=ALU.mult, op1=ALU.add),
                                 reads=[ck, pk, ("dsc", 2 + d_)], writes=[nk])
                    mslot = hh % 2

                    def Qprep(tt):
                        bi_ = tt % 2
                        t0, t1 = tiles[tt]
                        W = t1 - t0
                        ps, pk = PS()
                        proj_fm(wsl[:, :, 0, :], wk, tt, ps, pk)
                        if tt == 0:
                            S.op("act", lambda e: e.activation(out=qh[bi_][:, :W], in_=ps[:, :W], func=AF.Copy), reads=[pk, CK], writes=[("qh", bi_)])
                        else:
                            rope(ps[:, :W], [pk, CK], qh[bi_][:, :W], [("qh", bi_)], t0 - CTX, W, ra, rb, 0)
                        S.op("dve", lambda e: e.tensor_tensor(out=qf[bi_][:, :W], in0=qh[bi_][:, :W], in1=qdf[:, :W], op=ALU.mult),
                             reads=[("qh", bi_), CK] + qdkeys, writes=[("qf", bi_)])
                        S.op("dve", lambda e: e.tensor_tensor(out=qb[bi_][:, :W], in0=qh[bi_][:, :W], in1=qdb[:, :W], op=ALU.mult),
                             reads=[("qh", bi_), CK] + qdkeys, writes=[("qb", bi_)])

                    def Mpart(tt):
                        bi_ = tt % 2
                        t0, t1 = tiles[tt]
                        W = t1 - t0
                        ps, pk = PS()
                        proj_fm(wsl[:, :, 3, :], wk, tt, ps, pk)
                        S.op("act", lambda e: e.activation(out=sdv[0][:, :W], in_=ps[:, :W], func=AF.Exp, scale=-1.0), reads=[pk], writes=[("sdv", 0)])
                        S.op("act", lambda e: e.activation(out=sdv[0][:, :W], in_=sdv[0][:, :W], func=AF.Ln, bias=1.0, scale=1.0), reads=[("sdv", 0)], writes=[("sdv", 0)])
                        S.op("act", lambda e: e.activation(out=sdv[0][:, :W], in_=sdv[0][:, :W], func=AF.Exp, scale=-1.0), reads=[("sdv", 0)], writes=[("sdv", 0)])
                        S.op("dve", lambda e: e.tensor_tensor(out=gt[bi_][:, :W], in0=ps[:, :W], in1=sdv[0][:, :W], op=ALU.mult),
                             reads=[pk, ("sdv", 0), CK], writes=[("gt", bi_)])
                        ops_, opk = PS()
                        ps_live.add(opk)
                        nch_t = W // 128
                        pend = []

                        def score(cc):
                            ch = t0 // 128 + cc
                            cs = slice(cc * 128, (cc + 1) * 128)
                            ps2, pk2 = PS()
                            S.op("pe", lambda e: e.matmul(ps2[:, 0:128], lhsT=kT[:, ch * 128:(ch + 1) * 128], rhs=qh[bi_][:, cs], start=True, stop=True),
                                 reads=[("kT", tt), ("qh", bi_), CK], writes=[pk2])
                            sdi = cnt["sdT"] % 4
                            cnt["sdT"] += 1
                            S.op("dve", lambda e: e.tensor_tensor(out=sdT[sdi][:, :], in0=ps2[:, 0:128], in1=Mh[:, :], op=ALU.mult),
                                 reads=[pk2, "Mh"], writes=[("sdT", sdi)])
                            return (cc, ch, cs, sdi)

                        def accum(item):
                            cc, ch, cs, sdi = item

                            def mm(e):
                                e.matmul(ops_[:, cs], lhsT=v_tok[:, ch, :], rhs=sdT[sdi][:, :], start=True, stop=False)
                                e.matmul(ops_[:, cs], lhsT=Sf_bf[:, ch, :], rhs=qf[bi_][:, cs], start=False, stop=False)
                                return e.matmul(ops_[:, cs], lhsT=Sb_bf[:, ch, :], rhs=qb[bi_][:, cs], start=False, stop=True)
                            S.op("pe", mm, reads=[vkey(ch), ("sdT", sdi), ("Sbf", 0, ch), ("Sbf", 1, ch), ("qf", bi_), ("qb", bi_), CK], writes=[opk])
                        for cc in range(nch_t):
                            pend.append(score(cc))
                            if len(pend) > LA:
                                accum(pend.pop(0))
                        while pend:
                            accum(pend.pop(0))
                        S.op("act", lambda e: e.activation(out=sqn[bi_][:, :W], in_=ops_[:, :W], func=AF.Square), reads=[opk, CK], writes=[("sqn", bi_)])
                        return ops_, opk

                    def Npart(tt, ops_, opk):
                        bi_ = tt % 2
                        t0, t1 = tiles[tt]
                        W = t1 - t0
                        pv_, pvk = PS()
                        S.op("pe", lambda e: e.matmul(pv_[:, :W], lhsT=avg_m, rhs=sqn[bi_][:, :W], start=True, stop=True), reads=[("sqn", bi_), CK, "mats"], writes=[pvk])
                        S.op("act", lambda e: e.activation(out=sdv[0][:, :W], in_=pv_[:, :W], func=AF.Ln, bias=eps_t[:, 0:1], scale=1.0),
                             reads=[pvk, "eps"], writes=[("sdv", 0)])
                        S.op("act", lambda e: e.activation(out=sdv[0][:, :W], in_=sdv[0][:, :W], func=AF.Exp, scale=-0.5), reads=[("sdv", 0)], writes=[("sdv", 0)])
                        S.op("dve", lambda e: e.tensor_tensor(out=ra[0][:, :W], in0=ops_[:, :W], in1=sdv[0][:, :W], op=ALU.mult),
                             reads=[opk, ("sdv", 0)], writes=[("ra", 0)])
                        S.op("dve", lambda e: e.tensor_tensor(out=mixout[:, mslot, t0:t1], in0=ra[0][:, :W], in1=gt[bi_][:, :W], op=ALU.mult),
                             reads=[("ra", 0), ("gt", bi_), CK], writes=[("mix", mslot, tt)])

                    Qprep(0)
                    prev = None
                    for tt in range(NT):
                        if tt + 1 < NT:
                            Qprep(tt + 1)
                        ops_, opk = Mpart(tt)
                        if prev is not None:
                            Npart(prev[0], prev[1], prev[2])
                            ps_live.discard(prev[2])
                        prev = (tt, ops_, opk)
                    Npart(prev[0], prev[1], prev[2])
                    ps_live.discard(prev[2])
                    if hh % 2 == 1:
                        out_proj(hh // 2, (hh // 2 % 2) * 2 + 1)

                qoff = 4 * RH * 128
                koff = qoff + AH * 128
                voff = koff + AKV * 128

                def qk_norm_a(ps, pk, W, bi_):
                    S.op("act", lambda e: e.activation(out=sqn[bi_][:, :W], in_=ps[:, :W], func=AF.Square), reads=[pk, CK], writes=[("sqn", bi_)])

                def qk_norm_b(ps, pk, W, bi_, gcol, dst, dst_keys, tt):
                    p2, p2k = PS()
                    S.op("pe", lambda e: e.matmul(p2[:, :W], lhsT=avg_m, rhs=sqn[bi_][:, :W], start=True, stop=True), reads=[("sqn", bi_), CK, "mats"], writes=[p2k])
                    S.op("act", lambda e: e.activation(out=sdv[0][:, :W], in_=p2[:, :W], func=AF.Ln, bias=eps_t[:, 0:1], scale=1.0),
                         reads=[p2k, "eps"], writes=[("sdv", 0)])
                    S.op("act", lambda e: e.activation(out=sdv[0][:, :W], in_=sdv[0][:, :W], func=AF.Exp, scale=-0.5), reads=[("sdv", 0)], writes=[("sdv", 0)])
                    S.op("dve", lambda e: e.scalar_tensor_tensor(out=ra[0][:, :W], in0=ps[:, :W], scalar=gcol, in1=sdv[0][:, :W], op0=ALU.mult, op1=ALU.mult),
                         reads=[pk, ("sdv", 0), "qkg"], writes=[("ra", 0)])
                    t0, t1 = tiles[tt]
                    if tt == 0:
                        S.op("act", lambda e: e.activation(out=dst, in_=ra[0][:, :W], func=AF.Copy), reads=[("ra", 0), CK], writes=dst_keys)
                    else:
                        rope(ra[0][:, :W], [("ra", 0), CK], dst, dst_keys, t0 - CTX, W, ra, rb, 0, inplace=True)

                for gi in range(AKV):
                    si = (gi % 2) * 2
                    nf = GQ + 2
                    wsl = slots[si][:, 0:KD * nf * 128].rearrange("p (kc f n) -> p kc f n", kc=KD, f=nf)
                    pieces = [(wsl[:, :, j, :], winv[:, :, qoff + (gi * GQ + j) * 128: qoff + (gi * GQ + j + 1) * 128]) for j in range(GQ)]
                    pieces.append((wsl[:, :, GQ, :], winv[:, :, koff + gi * 128: koff + (gi + 1) * 128]))
                    pieces.append((wsl[:, :, GQ + 1, :], winv[:, :, voff + gi * 128: voff + (gi + 1) * 128]))
                    load_slot(si, pieces)
                    wk = ("slot", si)
                    for tt in range(NT):
                        t0, t1 = tiles[tt]
                        W = t1 - t0
                        ps, pk = PS()
                        proj_fm(wsl[:, :, GQ, :], wk, tt, ps, pk)
                        qk_norm_a(ps, pk, W, 0)
                        qk_norm_b(ps, pk, W, 0, qkg_s[:, i, 1:2], kT[:, t0:t1], [("kT", tt)], tt)
                    v_proj(wsl[:, :, GQ + 1, :], wk)
                    its = [(j, tt) for j in range(GQ) for tt in range(NT)]

                    def prepA(n):
                        j, tt = its[n]
                        W = tiles[tt][1] - tiles[tt][0]
                        ps, pk = PS()
                        proj_fm(wsl[:, :, j, :], wk, tt, ps, pk)
                        qk_norm_a(ps, pk, W, n % 2)
                        ps_live.add(pk)
                        return ps, pk

                    def prepB(n, ps, pk):
                        j, tt = its[n]
                        W = tiles[tt][1] - tiles[tt][0]
                        qk_norm_b(ps, pk, W, n % 2, qkg_s[:, i, 0:1], qh[n % 2][:, :W], [("qh", n % 2)], tt)
                        ps_live.discard(pk)
                    pq = prepA(0)
                    prepB(0, *pq)
                    for n, (j, tt) in enumerate(its):
                        bi_ = n % 2
                        mslot = j % 2
                        t0, t1 = tiles[tt]
                        W = t1 - t0
                        kchunks = list(range(CTX // 128)) if tt == 0 else list(range(NCH))
                        ops_, opk = PS()
                        ps_live.add(opk)
                        dps_, dpk = PS()
                        ps_live.add(dpk)
                        nxt_pq = prepA(n + 1) if n + 1 < len(its) else None
                        pend = []

                        def qk(n_, ch):
                            ps, pk = PS()
                            S.op("pe", lambda e: e.matmul(ps[:, :W], lhsT=kT[:, ch * 128:(ch + 1) * 128], rhs=qh[bi_][:, :W], start=True, stop=True),
                                 reads=[("kT", tile_of_chunk(ch)), ("qh", bi_), CK], writes=[pk])
                            pi = cnt["pT"] % NPT
                            cnt["pT"] += 1
                            S.op("act", lambda e: e.activation(out=pT[pi][:, :W], in_=ps[:, :W], func=AF.Exp, scale=inv_sqrt_dk, bias=nshift_t[:, 0:1]),
                                 reads=[pk, "nshift", CK], writes=[("pT", pi)])
                            return (n_, ch, pi)

                        def pv(item):
                            n_, ch, pi = item
                            first, last_ = (n_ == 0), (n_ == len(kchunks) - 1)

                            def mm(e):
                                e.matmul(ops_[:, :W], lhsT=v_tok[:, ch, :], rhs=pT[pi][:, :W], start=first, stop=last_)
                                return e.matmul(dps_[:, :W], lhsT=ones_m, rhs=pT[pi][:, :W], start=first, stop=last_)
                            S.op("pe", mm, reads=[vkey(ch), ("pT", pi), CK, "mats"], writes=[opk, dpk])
                        for n_, ch in enumerate(kchunks):
                            pend.append(qk(n_, ch))
                            if len(pend) > LA + 1:
                                pv(pend.pop(0))
                            if n_ == min(3, len(kchunks) - 1) and nxt_pq is not None:
                                prepB(n + 1, *nxt_pq)
                                nxt_pq = None
                        while pend:
                            pv(pend.pop(0))
                        S.op("dve", lambda e: e.reciprocal(out=sdv[0][:, :W], in_=dps_[:, :W]), reads=[dpk], writes=[("sdv", 0)])
                        S.op("dve", lambda e: e.tensor_tensor(out=mixout[:, mslot, t0:t1], in0=ops_[:, :W], in1=sdv[0][:, :W], op=ALU.mult),
                             reads=[opk, ("sdv", 0)], writes=[("mix", mslot, tt)])
                        ps_live.discard(opk)
                        ps_live.discard(dpk)
                        if tt == NT - 1 and j % 2 == 1:
                            pair = (RH + gi * GQ + j) // 2
                            out_proj(pair, (pair % 2) * 2 + 1)
                S.barrier()
                S.op("dve", lambda e: e.memset(dsc[:, 7:8], 0.0), writes=[("slot", 1), ("slot", 3), ("dsc", 7)])

        ytoks = []

        def load_x_tile(b, tt):
            t0, t1 = tiles[tt]
            S.dma("sp", xsem, lambda e: [e.dma_start(out=X[:, kc, t0:t1], in_=xin[b, kc * 128:(kc + 1) * 128, t0:t1]) for kc in range(KD)], writes=XK(tt))

        allX = [k for tt in range(NT) for k in XK(tt)]
        S.dma("sp", xsem, lambda e: [e.dma_start(out=X[:, kc, :], in_=xin[0, kc * 128:(kc + 1) * 128, :]) for kc in range(KD)], writes=allX)

        def tl_of(l):
            last = (l == DEPTH - 1)
            even = (l % 2 == 0)
            tl_mix = list(range(1, NT)) if (last and not even) else list(range(NT))
            tl_ffn = list(range(1, NT)) if last else list(range(NT))
            return tl_mix, tl_ffn

        for b in range(NB):
            normed = False
            for li, l in enumerate(layers):
                even = (l % 2 == 0)
                tl_mix, tl_ffn = tl_of(l)
                wins = None
                if not even:
                    wins = cm_prefetch(l)
                if not normed:
                    norm_phase(l, 0, b, tl_mix)
                if even:
                    ab_phase(l, b)
                else:
                    cm_phase(l, b, tl_mix, wins)
                nxt = None
                normed = False
                if li + 1 < len(layers):
                    l2 = layers[li + 1]
                    tl_mix2, _ = tl_of(l2)
                    if all(t in tl_ffn for t in tl_mix2):
                        nxt = (l2, tl_mix2)
                        normed = True
                ffn_phase(l, b, tl_ffn, nxt)
            for tt in range(1, NT):
                t0, t1 = tiles[tt]
                ytoks.append(S.dma("sp", ysem, lambda e: [e.dma_start(out=y[b, kc * 128:(kc + 1) * 128, t0 - CTX:t1 - CTX], in_=X[:, kc, t0:t1]) for kc in range(KD)],
                                   reads=XK(tt)))
            if b + 1 < NB:
                for tt in range(NT):
                    load_x_tile(b + 1, tt)
        ytok = ytoks[-1]
        S._wait("sp", [ytok])
        S.barrier(("pe", "act", "dve", "pool", "sp"))
        nc._sched_stats = (S.n_ops, S.n_wait)
    return nc


def rope_tables(L):
    rows = L // GRID_W
    row = np.repeat(np.arange(rows, dtype=np.float32), GRID_W)
    col = np.tile(np.arange(GRID_W, dtype=np.float32), rows)
    n_freq = 32
    inv = (ROPE_BASE ** (-np.arange(n_freq, dtype=np.float32) / n_freq)).astype(np.float32)
    ang = np.concatenate([row[:, None] * inv[None, :], col[:, None] * inv[None, :]], axis=-1)
    cos, sin = np.cos(ang).astype(np.float32), np.sin(ang).astype(np.float32)
    out = np.zeros((128, 2, L), np.float32)
    out[0:64, 0] = cos.T
    out[64:128, 0] = cos.T
    out[0:64, 1] = sin.T
    out[64:128, 1] = -sin.T
    return out


def const_tables():
    j = np.arange(128, dtype=np.float32)[:, None]
    i = np.arange(128, dtype=np.float32)[None, :]
    d = np.zeros((128, 6 * 128 + 2), np.float32)
    d[:, 0:128] = np.maximum(i - j, 0)
    d[:, 128:256] = (i >= j)
    d[:, 256:384] = np.maximum(j - i, 0)
    d[:, 384:512] = (j >= i)
    d[:, 512:640] = np.broadcast_to(i + 1, (128, 128))
    d[:, 640:768] = np.broadcast_to(128 - i, (128, 128))
    d[:, 768] = 127 - j[:, 0]
    d[:, 769] = j[:, 0]
    m = np.zeros((128, 4, 128), np.float32)
    m[:, 0] = 1.0
    m[:, 1] = np.eye(128)
    m[:, 2] = np.eye(128) - 1.0 / 128
    m[:, 3] = 1.0 / 128
    return d, m


def prep_inputs(cfg, inp):
    c = cfg
    f = lambda a: np.ascontiguousarray(np.asarray(a, dtype=np.float32))
    KD, D, DEPTH = c.KD, c.D, c.DEPTH
    common = {}
    common["mod_w"] = f(inp["mod_w"])
    common["mod_b"] = f(np.asarray(inp["mod_b"]).reshape(DEPTH, 6 * KD, 128).transpose(2, 0, 1))
    g1 = np.asarray(inp["norm1_g"]).reshape(DEPTH, KD, 128)
    g2 = np.asarray(inp["norm2_g"]).reshape(DEPTH, KD, 128)
    common["gains"] = f(np.stack([g1, g2], axis=1).transpose(3, 0, 1, 2))
    common["ab_w_in"] = f(inp["ab_w_in"])
    common["ab_w_out"] = f(inp["ab_w_out"])
    rd = np.asarray(inp["ret_decay"]).reshape(1, -1)
    common["ret_dec"] = f(np.broadcast_to(rd, (128, rd.shape[1])))
    common["qk_g"] = f(np.stack([np.asarray(inp["att_q_norm_g"]), np.asarray(inp["att_k_norm_g"])], axis=-1).transpose(1, 0, 2))
    common["cm_w_in"] = f(inp["cm_w_in"])
    vg = np.asarray(inp["cm_v_norm_g"])
    common["cm_vg"] = f(np.broadcast_to(vg[None], (128,) + vg.shape))
    common["cm_wsT"] = f(np.asarray(inp["cm_w_s"]).transpose(0, 3, 1, 2))
    common["cm_bs"] = f(np.asarray(inp["cm_b_s"]).reshape(1, -1))
    common["cm_w_out"] = f(inp["cm_w_out"])
    common["ff_w1"] = f(inp["ff_w1"])
    common["ff_w2"] = f(inp["ff_w2"])
    common["rope_cs"] = rope_tables(c.L)
    d, m = const_tables()
    common["dconst"] = d
    common["mats"] = m
    x = np.asarray(inp["x"], dtype=np.float32)
    ctx = np.asarray(inp["ctx"], dtype=np.float32)
    cc = np.asarray(inp["c"], dtype=np.float32)
    c_ctx = np.asarray(inp["c_ctx"], dtype=np.float32)
    maps = []
    for core in range(c.NCORES):
        bs = slice(core * c.NB, (core + 1) * c.NB)
        xin = np.concatenate([ctx[bs].transpose(0, 2, 1), x[bs].transpose(0, 2, 1)], axis=2)
        cols = np.concatenate([cc[bs], c_ctx[None]], axis=0)
        cT = cols.T.reshape(KD, 128, c.NB + 1).transpose(1, 0, 2)
        m_ = dict(common)
        m_["xin"] = f(xin)
        m_["cT"] = f(cT)
        maps.append(m_)
    return maps


_CACHE = {}


def kernel(**inputs):
    cfg = Cfg()
    if "nc" not in _CACHE:
        _CACHE["nc"] = build_program(cfg)
    nc = _CACHE["nc"]
    maps = prep_inputs(cfg, inputs)
    res = run_bass_kernel_spmd(nc, maps, core_ids=list(range(cfg.NCORES)))
    outs = [np.asarray(r["y"]).transpose(0, 2, 1) for r in res.results]
    return np.ascontiguousarray(np.concatenate(outs, axis=0).astype(np.float32))
```
